# Optimizing a Trainium2 kernel written in Bass

```python
import jax
import jax.numpy as jnp
from jax import lax
import numpy as np

D_MODEL = 1024
BATCH = 2
SEQ = 16384
DEPTH = 2

GRID_W = 64
CTX_LEN = 256
D_FF = 2816
FFN_RES_WEIGHT = 0.5
N_MOD = 9
EPS = 1e-6
ROPE_THETA = 10000.0
Q_BLOCK = 128
F_FLOOR = 1e-20

A_WIDTH = 256
A_GROUP = 16
A_GROUPS = A_WIDTH // A_GROUP
A_STATE = 64
DT_MIN = 1e-3
DT_MAX = 1e-1

B_HEADS = 4
B_NOPE = 64
B_ROPE = 32
B_V = 64
B_Q_LORA = 192
B_KV_LORA = 128

C_HEADS = 4
C_DK = 64
C_DV = 64
C_CHUNK = 64

D_HEADS = 4
D_KV_HEADS = 2
D_HEAD = 64

N_BRANCH = 4
BRANCH_W = 256

IN_SPLITS = (A_WIDTH,
             B_Q_LORA, B_KV_LORA, B_ROPE,
             C_HEADS * C_DK, C_HEADS * C_DV, C_HEADS * C_DK, C_HEADS * C_DK, C_HEADS * C_DV,
             D_HEADS * D_HEAD, D_KV_HEADS * D_HEAD, D_KV_HEADS * D_HEAD,
             N_BRANCH * D_MODEL)
N_IN = sum(IN_SPLITS)

kernel_name = 'hybrid_s5_mla_hgrn2_gqa_block'


def rms_norm(x, g):
    xf = x.astype(jnp.float32)
    y = xf * lax.rsqrt(jnp.mean(xf * xf, axis=-1, keepdims=True) + EPS)
    return (y * g.astype(jnp.float32)).astype(x.dtype)


def modulate(x, shift, scale):
    return x * (1.0 + scale) + shift


def swiglu(h, w1, w3, w2):
    return (jax.nn.silu(h @ w1) * (h @ w3)) @ w2


def ffn_sublayer(x, mod, j, g_pre, g_post, w1, w3, w2):
    h = modulate(rms_norm(x, g_pre), mod[:, :, 3 * j], mod[:, :, 3 * j + 1])
    return x + FFN_RES_WEIGHT * mod[:, :, 3 * j + 2] * rms_norm(swiglu(h, w1, w3, w2), g_post)


def axial_rope_tables(n_rows, rot_dim):
    rows = jnp.repeat(jnp.arange(n_rows, dtype=jnp.float32), GRID_W)
    cols = jnp.tile(jnp.arange(GRID_W, dtype=jnp.float32), n_rows)
    half = rot_dim // 2
    inv = ROPE_THETA ** (-jnp.arange(0, half, 2, dtype=jnp.float32) / half)
    ang_r = rows[:, None] * inv
    ang_c = cols[:, None] * inv
    ang = jnp.concatenate([ang_r, ang_r, ang_c, ang_c], axis=-1)
    return jnp.cos(ang), jnp.sin(ang)


def apply_axial_rope(x, cos, sin):
    q = x.shape[-1] // 4
    xr = x.reshape(x.shape[:-1] + (2, 2, q))
    rot = jnp.stack([-xr[..., 1, :], xr[..., 0, :]], axis=-2).reshape(x.shape)
    return x * cos[None, :, None, :].astype(x.dtype) + rot * sin[None, :, None, :].astype(x.dtype)


def attend_block(q, k, v, scale):
    s = jnp.einsum('bqhgd,bkhd->bhgqk', q, k, preferred_element_type=jnp.float32) * scale
    p = jax.nn.softmax(s, axis=-1).astype(v.dtype)
    return jnp.einsum('bhgqk,bkhd->bqhgd', p, v)


def blocked_attention(q, k, v, scale):
    b, lq, hq, dk = q.shape
    hkv = k.shape[2]
    nblk = lq // Q_BLOCK
    qb = q.reshape(b, nblk, Q_BLOCK, hkv, hq // hkv, dk).swapaxes(0, 1)
    ob = lax.map(lambda qi: attend_block(qi, k, v, scale), qb)
    return ob.swapaxes(0, 1).reshape(b, lq, hq, v.shape[-1])


def split_projection(z):
    offsets = np.cumsum(IN_SPLITS)[:-1].tolist()
    return jnp.split(z, offsets, axis=-1)


def s5_discretize(lam_re, lam_im, log_dt, b_re, b_im):
    lam_re = jnp.minimum(lam_re.astype(jnp.float32), -1e-4)
    lam_im = lam_im.astype(jnp.float32)
    dt = jnp.exp(log_dt.astype(jnp.float32))[:, None]
    mag = jnp.exp(lam_re * dt)
    a_re = mag * jnp.cos(lam_im * dt)
    a_im = mag * jnp.sin(lam_im * dt)
    den = lam_re * lam_re + lam_im * lam_im
    num_re = a_re - 1.0
    f_re = (num_re * lam_re + a_im * lam_im) / den
    f_im = (a_im * lam_re - num_re * lam_im) / den
    b_re = b_re.astype(jnp.float32)
    b_im = b_im.astype(jnp.float32)
    bb_re = f_re[..., None] * b_re - f_im[..., None] * b_im
    bb_im = f_re[..., None] * b_im + f_im[..., None] * b_re
    return a_re, a_im, bb_re, bb_im


def complex_combine(e1, e2):
    a1r, a1i, b1r, b1i = e1
    a2r, a2i, b2r, b2i = e2
    return (a2r * a1r - a2i * a1i, a2r * a1i + a2i * a1r,
            a2r * b1r - a2i * b1i + b2r, a2r * b1i + a2i * b1r + b2i)


def complex_scan(a_re, a_im, bu_re, bu_im, h0, reverse):
    if h0 is not None:
        h_re, h_im = h0
        first = -1 if reverse else 0
        bu_re = bu_re.at[:, first].add(a_re * h_re - a_im * h_im)
        bu_im = bu_im.at[:, first].add(a_re * h_im + a_im * h_re)
    a_re = jnp.broadcast_to(a_re, bu_re.shape)
    a_im = jnp.broadcast_to(a_im, bu_im.shape)
    _, _, x_re, x_im = lax.associative_scan(complex_combine, (a_re, a_im, bu_re, bu_im),
                                            reverse=reverse, axis=1)
    return x_re, x_im


def s5_branch(u_lat, u_ctx, lam_re, lam_im, log_dt, b_re, b_im, c_re, c_im, d_skip, w_glu, with_ctx_out):
    dt_ = u_lat.dtype
    ug_lat = u_lat.reshape(u_lat.shape[0], u_lat.shape[1], A_GROUPS, A_GROUP)
    ug_ctx = u_ctx.reshape(u_ctx.shape[0], u_ctx.shape[1], A_GROUPS, A_GROUP)
    y_lat = d_skip * u_lat
    y_ctx = d_skip * u_ctx if with_ctx_out else None
    for dr, rev in enumerate((False, True)):
        a_re, a_im, bb_re, bb_im = (t.astype(dt_) for t in s5_discretize(
            lam_re[dr], lam_im[dr], log_dt[dr], b_re[dr], b_im[dr]))
        x_ctx = complex_scan(a_re, a_im,
                             jnp.einsum('gpn,blgn->blgp', bb_re, ug_ctx),
                             jnp.einsum('gpn,blgn->blgp', bb_im, ug_ctx), None, rev)
        end = 0 if rev else -1
        x_lat = complex_scan(a_re, a_im,
                             jnp.einsum('gpn,blgn->blgp', bb_re, ug_lat),
                             jnp.einsum('gpn,blgn->blgp', bb_im, ug_lat),
                             (x_ctx[0][:, end], x_ctx[1][:, end]), rev)
        y_lat = y_lat + (jnp.einsum('gnp,blgp->blgn', c_re[dr], x_lat[0])
                         - jnp.einsum('gnp,blgp->blgn', c_im[dr], x_lat[1])).reshape(u_lat.shape)
        if with_ctx_out:
            y_ctx = y_ctx + (jnp.einsum('gnp,blgp->blgn', c_re[dr], x_ctx[0])
                             - jnp.einsum('gnp,blgp->blgn', c_im[dr], x_ctx[1])).reshape(u_ctx.shape)

    def glu(y):
        g = jax.nn.gelu(y)
        return g * jax.nn.sigmoid(g @ w_glu)
    return glu(y_lat), (glu(y_ctx) if with_ctx_out else None)


def mla_keys(ckv, kr, kv_norm, w_ukv, rope):
    b, l, _ = ckv.shape
    kv = (rms_norm(ckv, kv_norm) @ w_ukv).reshape(b, l, B_HEADS, B_NOPE + B_V)
    k_nope, v = kv[..., :B_NOPE], kv[..., B_NOPE:]
    k_rope = kr[:, :, None, :]
    if rope is not None:
        k_rope = apply_axial_rope(k_rope, *rope)
    k = jnp.concatenate([k_nope, jnp.broadcast_to(k_rope, (b, l, B_HEADS, B_ROPE))], axis=-1)
    return k, v


def mla_queries(cq, q_norm, w_uq, rope):
    b, l, _ = cq.shape
    q = (rms_norm(cq, q_norm) @ w_uq).reshape(b, l, B_HEADS, B_NOPE + B_ROPE)
    if rope is None:
        return q
    return jnp.concatenate([q[..., :B_NOPE], apply_axial_rope(q[..., B_NOPE:], *rope)], axis=-1)


def mla_branch(lat, ctx, q_norm, w_uq, kv_norm, w_ukv, rope, with_ctx_out):
    scale = (B_NOPE + B_ROPE) ** -0.5
    k_ctx, v_ctx = mla_keys(ctx[1], ctx[2], kv_norm, w_ukv, None)
    k_lat, v_lat = mla_keys(lat[1], lat[2], kv_norm, w_ukv, rope)
    q_lat = mla_queries(lat[0], q_norm, w_uq, rope)
    o_lat = blocked_attention(q_lat, jnp.concatenate([k_ctx, k_lat], axis=1),
                              jnp.concatenate([v_ctx, v_lat], axis=1), scale)
    y_lat = o_lat.reshape(o_lat.shape[0], o_lat.shape[1], B_HEADS * B_V)
    if not with_ctx_out:
        return y_lat, None
    o_ctx = blocked_attention(mla_queries(ctx[0], q_norm, w_uq, None), k_ctx, v_ctx, scale)
    return y_lat, o_ctx.reshape(o_ctx.shape[0], o_ctx.shape[1], B_HEADS * B_V)


def hgrn_gates(z, lb):
    z = z.astype(jnp.float32)
    f = lb + (1.0 - lb) * jax.nn.sigmoid(z)
    log_f = jnp.log(jnp.maximum(f, F_FLOOR))
    k = (1.0 - lb) * jax.nn.sigmoid(-z)
    return log_f, k


def hgrn_chunk_scan(q, k, v, log_f, s0):
    b, l, h, _ = q.shape
    n = l // C_CHUNK
    tri = jnp.tril(jnp.ones((C_CHUNK, C_CHUNK), dtype=bool))[None, :, :, None, None]

    def chunks(t):
        return t.reshape(b, n, C_CHUNK, h, t.shape[-1]).swapaxes(0, 1)

    def step(s, inp):
        qc, kc, vc, fc = inp
        cum = jnp.cumsum(fc, axis=1)
        o_inter = jnp.einsum('bchk,bhkv->bchv', qc * jnp.exp(cum), s)
        diff = cum[:, :, None] - cum[:, None, :]
        decay = jnp.where(tri, jnp.exp(jnp.minimum(diff, 0.0)), 0.0)
        att = jnp.einsum('bthk,bshk,btshk->bths', qc, kc, decay)
        o_intra = jnp.einsum('bths,bshv->bthv', att, vc)
        last = cum[:, -1]
        s_new = jnp.exp(last)[..., None] * s + jnp.einsum(
            'bshk,bshv->bhkv', kc * jnp.exp(last[:, None] - cum), vc)
        return s_new, o_inter + o_intra

    s_fin, o = lax.scan(step, s0, (chunks(q), chunks(k), chunks(v), chunks(log_f)))
    return o.swapaxes(0, 1).reshape(b, l, h, v.shape[-1]), s_fin


def hgrn_branch(lat, ctx, lb, o_norm, with_ctx_out):
    def heads(t):
        return t.reshape(t.shape[0], t.shape[1], C_HEADS, -1).astype(jnp.float32)

    def flip(t):
        return jnp.flip(t, axis=1)

    b = ctx[0].shape[0]
    o_lat = 0.0
    o_ctx = 0.0
    for dr in range(2):
        lb_d = lb[dr].reshape(C_HEADS, C_DK)
        streams = []
        for s in (ctx, lat):
            log_f, k = hgrn_gates(heads(s[2 + dr]), lb_d)
            t = (heads(s[0]), k, heads(s[1]), log_f)
            streams.append(tuple(flip(u) for u in t) if dr == 1 else t)
        s0 = jnp.zeros((b, C_HEADS, C_DK, C_DV), jnp.float32)
        oc, s_ctx = hgrn_chunk_scan(*streams[0], s0)
        ol, _ = hgrn_chunk_scan(*streams[1], s_ctx)
        if dr == 1:
            oc, ol = flip(oc), flip(ol)
        o_lat = o_lat + ol
        o_ctx = o_ctx + oc

    def readout(o, g):
        o = rms_norm(o, o_norm).reshape(o.shape[0], o.shape[1], C_HEADS * C_DV)
        return o.astype(g.dtype) * jax.nn.silu(g)
    return readout(o_lat, lat[4]), (readout(o_ctx, ctx[4]) if with_ctx_out else None)


def gqa_keys(kd, vd, k_norm, rope):
    b, l, _ = kd.shape
    k = rms_norm(kd.reshape(b, l, D_KV_HEADS, D_HEAD), k_norm)
    v = vd.reshape(b, l, D_KV_HEADS, D_HEAD)
    return (k if rope is None else apply_axial_rope(k, *rope)), v


def gqa_queries(qd, q_norm, rope):
    b, l, _ = qd.shape
    q = rms_norm(qd.reshape(b, l, D_HEADS, D_HEAD), q_norm)
    return q if rope is None else apply_axial_rope(q, *rope)


def gqa_branch(lat, ctx, q_norm, k_norm, rope, with_ctx_out):
    scale = D_HEAD ** -0.5
    k_ctx, v_ctx = gqa_keys(ctx[1], ctx[2], k_norm, None)
    k_lat, v_lat = gqa_keys(lat[1], lat[2], k_norm, rope)
    o_lat = blocked_attention(gqa_queries(lat[0], q_norm, rope),
                              jnp.concatenate([k_ctx, k_lat], axis=1),
                              jnp.concatenate([v_ctx, v_lat], axis=1), scale)
    y_lat = o_lat.reshape(o_lat.shape[0], o_lat.shape[1], D_HEADS * D_HEAD)
    if not with_ctx_out:
        return y_lat, None
    o_ctx = blocked_attention(gqa_queries(ctx[0], q_norm, None), k_ctx, v_ctx, scale)
    return y_lat, o_ctx.reshape(o_ctx.shape[0], o_ctx.shape[1], D_HEADS * D_HEAD)


def merge_branches(branches, gate_raw, w_branch, w_out):
    b, l, _ = gate_raw.shape
    gates = jax.nn.sigmoid(gate_raw).reshape(b, l, N_BRANCH, D_MODEL)
    merged = gates[:, :, 0] * (branches[0] @ w_branch[0])
    for i in range(1, N_BRANCH):
        merged = merged + gates[:, :, i] * (branches[i] @ w_branch[i])
    return merged @ w_out


def token_mixing(h_lat, h_ctx, w_in, s5_p, mla_p, hgrn_p, gqa_p, w_branch, w_out, rope_b, rope_d, with_ctx_out):
    zl = split_projection(h_lat @ w_in)
    zc = split_projection(h_ctx @ w_in)
    ya_l, ya_c = s5_branch(zl[0], zc[0], *s5_p, with_ctx_out)
    yb_l, yb_c = mla_branch(zl[1:4], zc[1:4], *mla_p, rope_b, with_ctx_out)
    yc_l, yc_c = hgrn_branch(zl[4:9], zc[4:9], *hgrn_p, with_ctx_out)
    yd_l, yd_c = gqa_branch(zl[9:12], zc[9:12], *gqa_p, rope_d, with_ctx_out)
    y_lat = merge_branches((ya_l, yb_l, yc_l, yd_l), zl[12], w_branch, w_out)
    if not with_ctx_out:
        return y_lat, None
    y_ctx = merge_branches((ya_c, yb_c, yc_c, yd_c), zc[12], w_branch, w_out)
    return y_lat, y_ctx


def setup_inputs(seed: int = 0) -> dict:
    key = jax.random.key(seed)
    ks = iter(jax.random.split(key, 32))
    f32 = jnp.float32

    def nrm(shape, std):
        return std * jax.random.normal(next(ks), shape, f32)

    def gain(shape):
        return 1.0 + nrm(shape, 0.02)

    L = DEPTH
    return {
        'x': nrm((BATCH, SEQ, D_MODEL), 1.0),
        'c': nrm((BATCH, D_MODEL), 1.0),
        'ctx': nrm((BATCH, CTX_LEN, D_MODEL), 1.0),
        'c_ctx': nrm((D_MODEL,), 1.0),
        'w_ada': nrm((L, D_MODEL, N_MOD * D_MODEL), 0.5 * D_MODEL ** -0.5),
        'b_ada': nrm((L, N_MOD * D_MODEL), 0.02),
        'norm_pre': gain((L, 3, D_MODEL)),
        'norm_post': gain((L, 3, D_MODEL)),
        'ffn_w1': nrm((L, 2, D_MODEL, D_FF), D_MODEL ** -0.5),
        'ffn_w3': nrm((L, 2, D_MODEL, D_FF), D_MODEL ** -0.5),
        'ffn_w2': nrm((L, 2, D_FF, D_MODEL), D_FF ** -0.5),
        'w_in': nrm((L, D_MODEL, N_IN), D_MODEL ** -0.5),
        's5_lambda_re': -0.5 + nrm((L, 2, A_GROUPS, A_STATE), 0.01),
        's5_lambda_im': jnp.pi * jnp.arange(A_STATE, dtype=f32) + nrm((L, 2, A_GROUPS, A_STATE), 0.01),
        's5_log_dt': jax.random.uniform(next(ks), (L, 2, A_GROUPS), f32,
                                        minval=float(np.log(DT_MIN)), maxval=float(np.log(DT_MAX))),
        's5_b_re': nrm((L, 2, A_GROUPS, A_STATE, A_GROUP), A_GROUP ** -0.5),
        's5_b_im': nrm((L, 2, A_GROUPS, A_STATE, A_GROUP), A_GROUP ** -0.5),
        's5_c_re': nrm((L, 2, A_GROUPS, A_GROUP, A_STATE), A_STATE ** -0.5),
        's5_c_im': nrm((L, 2, A_GROUPS, A_GROUP, A_STATE), A_STATE ** -0.5),
        's5_d': nrm((L, A_WIDTH), 1.0),
        's5_w_glu': nrm((L, A_WIDTH, A_WIDTH), A_WIDTH ** -0.5),
        'mla_q_norm': gain((L, B_Q_LORA)),
        'mla_w_uq': nrm((L, B_Q_LORA, B_HEADS * (B_NOPE + B_ROPE)), B_Q_LORA ** -0.5),
        'mla_kv_norm': gain((L, B_KV_LORA)),
        'mla_w_ukv': nrm((L, B_KV_LORA, B_HEADS * (B_NOPE + B_V)), B_KV_LORA ** -0.5),
        'hgrn_lb_raw': nrm((2, L, C_HEADS * C_DK), 0.5),
        'hgrn_o_norm': gain((L, C_DV)),
        'gqa_q_norm': gain((L, D_HEAD)),
        'gqa_k_norm': gain((L, D_HEAD)),
        'w_branch': nrm((L, N_BRANCH, BRANCH_W, D_MODEL), BRANCH_W ** -0.5),
        'w_out': nrm((L, D_MODEL, D_MODEL), D_MODEL ** -0.5),
    }


def reference(x, c, ctx, c_ctx, w_ada, b_ada, norm_pre, norm_post, ffn_w1, ffn_w3, ffn_w2, w_in,
              s5_lambda_re, s5_lambda_im, s5_log_dt, s5_b_re, s5_b_im, s5_c_re, s5_c_im, s5_d, s5_w_glu,
              mla_q_norm, mla_w_uq, mla_kv_norm, mla_w_ukv, hgrn_lb_raw, hgrn_o_norm,
              gqa_q_norm, gqa_k_norm, w_branch, w_out):
    b, seq_len, _ = x.shape
    n_rows = seq_len // GRID_W
    rope_b = axial_rope_tables(n_rows, B_ROPE)
    rope_d = axial_rope_tables(n_rows, D_HEAD)
    lb_step = jax.nn.softmax(hgrn_lb_raw.astype(jnp.float32), axis=1)
    lb_all = jnp.clip(jnp.cumsum(lb_step, axis=1) - lb_step[:, :1], 0.0, 1.0)

    x_lat, x_ctx = x, ctx
    for layer in range(DEPTH):
        last = layer == DEPTH - 1
        mod_lat = (jax.nn.silu(c) @ w_ada[layer] + b_ada[layer]).reshape(b, 1, N_MOD, D_MODEL)
        mod_ctx = (jax.nn.silu(c_ctx) @ w_ada[layer] + b_ada[layer]).reshape(1, 1, N_MOD, D_MODEL)
        ffn_a = (norm_pre[layer, 0], norm_post[layer, 0], ffn_w1[layer, 0], ffn_w3[layer, 0], ffn_w2[layer, 0])
        ffn_b = (norm_pre[layer, 2], norm_post[layer, 2], ffn_w1[layer, 1], ffn_w3[layer, 1], ffn_w2[layer, 1])

        x_lat = ffn_sublayer(x_lat, mod_lat, 0, *ffn_a)
        x_ctx = ffn_sublayer(x_ctx, mod_ctx, 0, *ffn_a)

        h_lat = modulate(rms_norm(x_lat, norm_pre[layer, 1]), mod_lat[:, :, 3], mod_lat[:, :, 4])
        h_ctx = modulate(rms_norm(x_ctx, norm_pre[layer, 1]), mod_ctx[:, :, 3], mod_ctx[:, :, 4])
        s5_p = (s5_lambda_re[layer], s5_lambda_im[layer], s5_log_dt[layer], s5_b_re[layer], s5_b_im[layer],
                s5_c_re[layer], s5_c_im[layer], s5_d[layer], s5_w_glu[layer])
        mla_p = (mla_q_norm[layer], mla_w_uq[layer], mla_kv_norm[layer], mla_w_ukv[layer])
        hgrn_p = (lb_all[:, layer], hgrn_o_norm[layer])
        gqa_p = (gqa_q_norm[layer], gqa_k_norm[layer])
        y_lat, y_ctx = token_mixing(h_lat, h_ctx, w_in[layer], s5_p, mla_p, hgrn_p, gqa_p,
                                    w_branch[layer], w_out[layer], rope_b, rope_d, not last)
        x_lat = x_lat + mod_lat[:, :, 5] * rms_norm(y_lat, norm_post[layer, 1])

        x_lat = ffn_sublayer(x_lat, mod_lat, 2, *ffn_b)
        if not last:
            x_ctx = x_ctx + mod_ctx[:, :, 5] * rms_norm(y_ctx, norm_post[layer, 1])
            x_ctx = ffn_sublayer(x_ctx, mod_ctx, 2, *ffn_b)
    return x_lat
```

```python
from contextlib import ExitStack
import numpy as np
import ml_dtypes
import concourse.bass as bass
import concourse.mybir as mybir
from concourse.bass_utils import run_bass_kernel_spmd

F32 = mybir.dt.float32
BF16 = mybir.dt.bfloat16
AF = mybir.ActivationFunctionType
ALU = mybir.AluOpType
AX = mybir.AxisListType
NPBF = ml_dtypes.bfloat16

D = 1024
DFF = 2816
NIN = 6496
EPS = 1e-6
NCORE = 8
SKIP_OWN = False
SEQ = 16384
CTX = 256
LTOK = 4096
CTOK = 64
NTOK = LTOK + CTOK
LFULL = SEQ + CTX


class Buf:
    __slots__ = ("w", "r", "sem", "cnt", "name")

    def __init__(self, name):
        self.w = None
        self.r = {}
        self.sem = None
        self.cnt = 0
        self.name = name


class Tile:
    def __init__(self, t, name):
        self.t = t
        self.b = Buf(name)

    def __getitem__(self, idx):
        return self.t[idx]


class Prog:
    def __init__(self, name):
        self.name = name
        self.nc = bass.Bass("TRN2", target_bir_lowering=False)
        self.es = ExitStack()
        nc = self.nc
        self.eng = dict(pe=nc.tensor, act=nc.scalar, dve=nc.vector, pool=nc.gpsimd, sp=nc.sync)
        self.sem = {}
        self.cnt = {}
        for e in ("pe", "act", "dve", "pool"):
            self.sem[e] = self.es.enter_context(nc.semaphore("s_" + e))
            self.cnt[e] = 0
        self.seen = {e: {} for e in self.eng}
        self.semobj = {("s_" + e): self.sem[e] for e in self.sem}
        self.out_toks = []
        self.nsem = 4
        self.dram = {}

    def din(self, name, shape, dt=F32):
        t = self.nc.dram_tensor(name, list(shape), dt, kind="ExternalInput").ap()
        self.dram[name] = t
        return t

    def dout(self, name, shape, dt=F32):
        t = self.nc.dram_tensor(name, list(shape), dt, kind="ExternalOutput").ap()
        self.dram[name] = t
        return t

    def sb(self, name, shape, dt=F32):
        return Tile(self.es.enter_context(self.nc.sbuf_tensor(name, list(shape), dt)), name)

    def ps(self, name, shape, dt=F32):
        return Tile(self.es.enter_context(self.nc.psum_tensor(name, list(shape), dt)), name)

    def _wait(self, e, toks, skip_own=False):
        need = {}
        for (s, v) in toks:
            if v > need.get(s, 0):
                need[s] = v
        own = "s_" + e
        for s, v in need.items():
            if s == own and (skip_own or e == "pe" or v > self.cnt[e]):
                continue
            if self.seen[e].get(s, 0) < v:
                self.eng[e].wait_ge(self.semobj[s], v)
                self.seen[e][s] = v

    @staticmethod
    def _bufs(xs):
        return [x.b if isinstance(x, Tile) else x for x in xs]

    def _deps(self, reads, writes):
        toks = []
        for b in reads:
            if b.w:
                toks.append(b.w)
        for b in writes:
            if b.w:
                toks.append(b.w)
            toks.extend(b.r.items())
        return toks

    def _mark(self, tok, reads, writes):
        for b in reads:
            if b.r.get(tok[0], 0) < tok[1]:
                b.r[tok[0]] = tok[1]
        for b in writes:
            b.w = tok
            b.r = {}

    def op(self, e, fn, reads=(), writes=(), inc=True):
        reads = self._bufs(reads)
        writes = self._bufs(writes)
        self._wait(e, self._deps(reads, writes), skip_own=SKIP_OWN)
        ins = fn()
        idx = self.cnt[e] + 1
        if inc:
            ins.then_inc(self.sem[e], 1)
            self.cnt[e] = idx
        self._mark(("s_" + e, idx), reads, writes)
        return ins

    def dma(self, out, in_, sbuf, reads=(), writes=(), q="sp", is_out=False, **kw):
        b = sbuf.b
        if b.sem is None:
            b.sem = self.es.enter_context(self.nc.semaphore("d_" + b.name))
            self.semobj["d_" + b.name] = b.sem
            self.nsem += 1
        reads = self._bufs(reads)
        writes = self._bufs(writes)
        self._wait(q, self._deps(reads, writes), skip_own=False)
        ins = self.eng[q].dma_start(out=out, in_=in_, **kw)
        b.cnt += 16
        ins.then_inc(b.sem, 16)
        tok = ("d_" + b.name, b.cnt)
        self._mark(tok, reads, writes)
        if is_out:
            self.out_toks.append(tok)
        return ins

    def load(self, tile, dst_ap, src_ap, q="sp", **kw):
        return self.dma(dst_ap, src_ap, tile, reads=(), writes=(tile,), q=q, **kw)

    def store(self, dst_ap, tile, src_ap, q="pool", **kw):
        return self.dma(dst_ap, src_ap, tile, reads=(tile,), writes=(), q=q, is_out=True, **kw)

    def finish(self):
        self._wait("sp", self.out_toks)
        self.es.close()
        return self.nc

    def mm(self, out, lhsT, rhs, start, stop, reads, writes, inc=None, **kw):
        if inc is None:
            inc = stop
        return self.op("pe", lambda: self.nc.tensor.matmul(out, lhsT=lhsT, rhs=rhs, start=start, stop=stop, **kw),
                       reads, writes, inc=inc)

    def actf(self, out, in_, func, reads, writes, e="act", **kw):
        return self.op("act", lambda: self.nc.scalar.activation(out=out, in_=in_, func=func, **kw), reads, writes)

    def tt(self, out, in0, in1, op, reads, writes, e="dve"):
        return self.op(e, lambda: self.eng[e].tensor_tensor(out=out, in0=in0, in1=in1, op=op), reads, writes)

    def ts(self, out, in0, s1, s2, op0, op1, reads, writes, e="dve"):
        if op1 is None:
            return self.op(e, lambda: self.eng[e].tensor_scalar(out=out, in0=in0, scalar1=s1, scalar2=None, op0=op0),
                           reads, writes)
        return self.op(e, lambda: self.eng[e].tensor_scalar(out=out, in0=in0, scalar1=s1, scalar2=s2, op0=op0, op1=op1),
                       reads, writes)

    def stt(self, out, in0, scalar, in1, op0, op1, reads, writes):
        return self.op("dve", lambda: self.nc.vector.scalar_tensor_tensor(out=out, in0=in0, scalar=scalar, in1=in1,
                                                                          op0=op0, op1=op1), reads, writes)

    def copy(self, out, in_, reads, writes, e="dve"):
        if e == "act":
            return self.op("act", lambda: self.nc.scalar.copy(out=out, in_=in_), reads, writes)
        return self.op(e, lambda: self.eng[e].tensor_copy(out=out, in_=in_), reads, writes)


def run(prog_nc, in_maps):
    res = run_bass_kernel_spmd(prog_nc, in_maps, core_ids=list(range(NCORE)))
    return res.results


def load_cast_weight(P, dst, dst_view_fn, src, rows_chunks, ncols, stage, piece=1408):
    k = 0
    engs = ("dve", "pool", "act")
    for c in range(rows_chunks):
        for c0 in range(0, ncols, piece):
            w = min(piece, ncols - c0)
            st = stage[k % len(stage)]
            P.load(st, st[:, 0:w], src[c * 128:(c + 1) * 128, c0:c0 + w])
            e = engs[k % 3]
            P.copy(dst_view_fn(c, c0, w), st[:, 0:w], [st], [dst], e=e)
            k += 1


def rms_rstd(P, x_chunks_fn, nchunk, T, ones_b, sq, ss_ps, rstd, inv_n, reads):
    for c in range(nchunk):
        P.actf(sq[:, c, 0:T], x_chunks_fn(c), AF.Square, reads, [sq])
    for c in range(nchunk):
        P.mm(ss_ps[:, 0:T], ones_b[:, :], sq[:, c, 0:T], c == 0, c == nchunk - 1, [ones_b, sq], [ss_ps])
    P.actf(rstd[:, 0:T], ss_ps[:, 0:T], AF.Sqrt, [ss_ps], [rstd], scale=inv_n, bias=P.eps_col[:, 0:1])
    P.op("dve", lambda: P.nc.vector.reciprocal(out=rstd[:, 0:T], in_=rstd[:, 0:T]), [rstd], [rstd])


def token_tiles(T=256):
    tiles = []
    for s in range(0, LTOK, T):
        tiles.append((s, T, 0))
    tiles.append((LTOK, CTOK, 1))
    return tiles


def build_ffn():
    P = Prog("ffn")
    T = 256
    xT = P.din("xT", [D, NTOK])
    w1 = P.din("w1", [D, DFF])
    w3 = P.din("w3", [D, DFF])
    w2 = P.din("w2", [DFF, D])
    gp = P.din("gp", [128, 8, 2])
    modc = P.din("modc", [128, 8, 6])
    outT = P.dout("outT", [D, NTOK])

    w1b = P.sb("w1b", [128, 8, DFF], BF16)
    w3b = P.sb("w3b", [128, 8, DFF], BF16)
    w2b = P.sb("w2b", [128, 22, D], BF16)
    stage = [P.sb("stg%d" % i, [128, 1408], F32) for i in range(2)]
    ones_b = P.sb("ones_b", [128, 128], BF16)
    P.eps_col = P.sb("eps_col", [128, 1], F32)
    gps = P.sb("gps", [128, 8, 2], F32)
    mods = P.sb("mods", [128, 8, 6], F32)
    cols = P.sb("cols", [128, 8, 6], F32)
    P.op("dve", lambda: P.nc.vector.memset(ones_b[:, :], 1.0), [], [ones_b])
    P.op("dve", lambda: P.nc.vector.memset(P.eps_col[:, :], EPS), [], [P.eps_col])
    P.load(gps, gps[:, :, :], gp)
    P.load(mods, mods[:, :, :], modc)
    for tt in range(2):
        o = 3 * tt
        P.ts(cols[:, :, o + 0], mods[:, :, o + 1], 1.0, None, ALU.add, None, [mods], [cols])
        P.tt(cols[:, :, o + 0], cols[:, :, o + 0], gps[:, :, 0], ALU.mult, [cols, gps], [cols])
        P.copy(cols[:, :, o + 1], mods[:, :, o + 0], [mods], [cols])
        P.stt(cols[:, :, o + 2], mods[:, :, o + 2], 0.5, gps[:, :, 1], ALU.mult, ALU.mult, [mods, gps], [cols])

    load_cast_weight(P, w1b, lambda c, c0, w: w1b[:, c, c0:c0 + w], w1, 8, DFF, stage)
    load_cast_weight(P, w3b, lambda c, c0, w: w3b[:, c, c0:c0 + w], w3, 8, DFF, stage)
    load_cast_weight(P, w2b, lambda c, c0, w: w2b[:, c, c0:c0 + w], w2, 22, D, stage, piece=1024)

    NB = 2
    xs = [P.sb("x%d" % i, [128, 8, T], F32) for i in range(NB)]
    sq = P.sb("sq", [128, 8, T], BF16)
    rstd = P.sb("rstd", [128, T], F32)
    tmp = [P.sb("tmp%d" % i, [128, T], F32) for i in range(2)]
    hb = P.sb("hb", [128, 8, T], BF16)
    gb = P.sb("gb", [128, 22, T], BF16)
    sl = [P.sb("sl%d" % i, [128, T], F32) for i in range(2)]
    ys = P.sb("ys", [128, 8, T], F32)
    ss_ps = P.ps("ss_ps", [128, 512], F32)
    pa = [P.ps("pa%d" % i, [128, 512], F32) for i in range(3)]
    pb = [P.ps("pb%d" % i, [128, 512], F32) for i in range(3)]

    tiles = token_tiles(T)
    xTv = xT.rearrange("(c p) n -> p c n", p=128)
    oTv = outT.rearrange("(c p) n -> p c n", p=128)
    for ti, (s0, tn, ty) in enumerate(tiles):
        x = xs[ti % NB]
        o = x
        co = 3 * ty
        P.load(x, x[:, :, 0:tn], xTv[:, :, s0:s0 + tn])
        rms_rstd(P, lambda c: x[:, c, 0:tn], 8, tn, ones_b, sq, ss_ps, rstd, 1.0 / D, [x])
        for c in range(8):
            t = tmp[c % 2]
            P.stt(t[:, 0:tn], x[:, c, 0:tn], cols[:, c, co + 0:co + 1], rstd[:, 0:tn], ALU.mult, ALU.mult,
                  [x, cols, rstd], [t])
            P.actf(hb[:, c, 0:tn], t[:, 0:tn], AF.Identity, [t, cols], [hb], bias=cols[:, c, co + 1:co + 2], scale=1.0)
        for j in range(22):
            p1 = pa[j % 3]
            p3 = pb[j % 3]
            for c in range(8):
                P.mm(p1[:, 0:tn], w1b[:, c, j * 128:(j + 1) * 128], hb[:, c, 0:tn], c == 0, c == 7, [w1b, hb], [p1])
            for c in range(8):
                P.mm(p3[:, 0:tn], w3b[:, c, j * 128:(j + 1) * 128], hb[:, c, 0:tn], c == 0, c == 7, [w3b, hb], [p3])
            s = sl[j % 2]
            P.actf(s[:, 0:tn], p1[:, 0:tn], AF.Silu, [p1], [s])
            P.tt(gb[:, j, 0:tn], s[:, 0:tn], p3[:, 0:tn], ALU.mult, [s, p3], [gb])
        for k in range(8):
            p = pa[k % 3]
            for j in range(22):
                P.mm(p[:, 0:tn], w2b[:, j, k * 128:(k + 1) * 128], gb[:, j, 0:tn], j == 0, j == 21, [w2b, gb], [p])
            P.copy(ys[:, k, 0:tn], p[:, 0:tn], [p], [ys], e="act")
        rms_rstd(P, lambda c: ys[:, c, 0:tn], 8, tn, ones_b, sq, ss_ps, rstd, 1.0 / D, [ys])
        for c in range(8):
            t = tmp[c % 2]
            P.stt(t[:, 0:tn], ys[:, c, 0:tn], cols[:, c, co + 2:co + 3], rstd[:, 0:tn], ALU.mult, ALU.mult,
                  [ys, cols, rstd], [t])
            P.tt(o[:, c, 0:tn], t[:, 0:tn], x[:, c, 0:tn], ALU.add, [t, x], [x], e="pool")
        P.store(oTv[:, :, s0:s0 + tn], o, o[:, :, 0:tn])
    return P.finish()


MODC = 2 * 9 * D // NCORE


def build_mod():
    P = Prog("mod")
    cT = P.din("cT", [128, 8, 3])
    W = P.din("W", [D, MODC])
    bias = P.din("bias", [3, MODC])
    out = P.dout("out", [3, MODC])
    cs = P.sb("cs", [128, 8, 3], F32)
    sc = P.sb("sc", [128, 8, 3], F32)
    Ws = P.sb("Ws", [128, 8, MODC], F32)
    bs = P.sb("bs", [3, MODC], F32)
    os_ = P.sb("os", [3, MODC], F32)
    pp = [P.ps("pp%d" % i, [128, 512], F32) for i in range(2)]
    P.load(cs, cs[:, :, :], cT)
    P.load(bs, bs[:, :], bias)
    for c in range(8):
        P.load(Ws, Ws[:, c, :], W[c * 128:(c + 1) * 128, :])
    P.actf(sc[:, :, :], cs[:, :, :], AF.Silu, [cs], [sc])
    for i, c0 in enumerate(range(0, MODC, 512)):
        w = min(512, MODC - c0)
        p = pp[i % 2]
        for c in range(8):
            P.mm(p[0:3, 0:w], sc[:, c, :], Ws[:, c, c0:c0 + w], c == 0, c == 7, [sc, Ws], [p])
        P.tt(os_[:, c0:c0 + w], p[0:3, 0:w], bs[:, c0:c0 + w], ALU.add, [p, bs], [os_])
    P.store(out, os_, os_[:, :])
    return P.finish()


WEXT = 2400 + 32 + 256 + 128
NPC = 20


def make_h(P, x, tn, cols, co, ones_b, sq, ss_ps, rstd, tmp, hb):
    rms_rstd(P, lambda c: x[:, c, 0:tn], 8, tn, ones_b, sq, ss_ps, rstd, 1.0 / D, [x])
    for c in range(8):
        t = tmp[c % 2]
        P.stt(t[:, 0:tn], x[:, c, 0:tn], cols[:, c, co + 0:co + 1], rstd[:, 0:tn], ALU.mult, ALU.mult,
              [x, cols, rstd], [t])
        P.actf(hb[:, c, 0:tn], t[:, 0:tn], AF.Identity, [t, cols], [hb], bias=cols[:, c, co + 1:co + 2], scale=1.0)


def mod_cols(P, cols, mods, gps, with_gate, gate_scale=1.0, gcol=1):
    for tt in range(2):
        o = 3 * tt
        P.ts(cols[:, :, o + 0], mods[:, :, o + 1], 1.0, None, ALU.add, None, [mods], [cols])
        P.tt(cols[:, :, o + 0], cols[:, :, o + 0], gps[:, :, 0], ALU.mult, [cols, gps], [cols])
        P.copy(cols[:, :, o + 1], mods[:, :, o + 0], [mods], [cols])
        if with_gate:
            P.stt(cols[:, :, o + 2], mods[:, :, o + 2], gate_scale, gps[:, :, gcol], ALU.mult, ALU.mult,
                  [mods, gps], [cols])


def build_proj():
    P = Prog("proj")
    T = 256
    xT = P.din("xT", [D, NTOK])
    wext = P.din("wext", [D, WEXT])
    gp = P.din("gp", [128, 8, 2])
    modc = P.din("modc", [128, 8, 6])
    pcols = P.din("pcols", [128, NPC])
    wuq = P.din("wuq", [192, 512])
    wukv = P.din("wukv", [128, 512])
    ropeb = P.din("ropeb", [32, 2, NTOK])
    roped = P.din("roped", [128, 2, NTOK])
    blk = P.din("blk", [128, 128])
    o_u = P.dout("o_u", [256, NTOK])
    o_mq = P.dout("o_mq", [4, 96, NTOK], BF16)
    o_mk = P.dout("o_mk", [4, 96, NTOK], BF16)
    o_mv = P.dout("o_mv", [NTOK, 256], BF16)
    o_hq = P.dout("o_hq", [256, NTOK])
    o_hk = P.dout("o_hk", [2, 256, NTOK])
    o_hl = P.dout("o_hl", [2, 256, NTOK])
    o_hv = P.dout("o_hv", [NTOK, 256], BF16)
    o_hg = P.dout("o_hg", [256, NTOK])
    o_dq = P.dout("o_dq", [256, NTOK], BF16)
    o_dk = P.dout("o_dk", [128, NTOK], BF16)
    o_dv = P.dout("o_dv", [NTOK, 128], BF16)

    wb = P.sb("wb", [128, 8, WEXT], BF16)
    stage = [P.sb("stg%d" % i, [128, 1408], F32) for i in range(2)]
    ones_b = P.sb("ones_b", [128, 128], BF16)
    blk_f = P.sb("blk_f", [128, 128], F32)
    blk_b = P.sb("blk_b", [128, 128], BF16)
    P.eps_col = P.sb("eps_col", [128, 1], F32)
    gps = P.sb("gps", [128, 8, 2], F32)
    mods = P.sb("mods", [128, 8, 6], F32)
    cols = P.sb("cols", [128, 8, 6], F32)
    pc = P.sb("pc", [128, NPC], F32)
    lbc = P.sb("lbc", [128, 4, 8], F32)
    wuq_f = P.sb("wuq_f", [128, 2, 512], F32)
    wuq_b = P.sb("wuq_b", [128, 2, 512], BF16)
    wukv_f = P.sb("wukv_f", [128, 512], F32)
    wukv_b = P.sb("wukv_b", [128, 512], BF16)
    P.op("dve", lambda: P.nc.vector.memset(ones_b[:, :], 1.0), [], [ones_b])
    P.op("dve", lambda: P.nc.vector.memset(P.eps_col[:, :], EPS), [], [P.eps_col])
    P.load(gps, gps[:, :, :], gp)
    P.load(mods, mods[:, :, :], modc)
    P.load(pc, pc[:, :], pcols)
    P.load(blk_f, blk_f[:, :], blk)
    P.copy(blk_b[:, :], blk_f[:, :], [blk_f], [blk_b])
    P.load(wuq_f, wuq_f[:, 0, :], wuq[0:128, :])
    P.load(wuq_f, wuq_f[0:64, 1, :], wuq[128:192, :])
    P.copy(wuq_b[:, 0, :], wuq_f[:, 0, :], [wuq_f], [wuq_b])
    P.copy(wuq_b[0:64, 1, :], wuq_f[0:64, 1, :], [wuq_f], [wuq_b])
    P.load(wukv_f, wukv_f[:, :], wukv)
    P.copy(wukv_b[:, :], wukv_f[:, :], [wukv_f], [wukv_b])
    mod_cols(P, cols, mods, gps, False)
    for d in range(2):
        for c in range(2):
            k = d * 2 + c
            r0 = pc[:, 7 + (d * 2 + 0) * 2 + c:7 + (d * 2 + 0) * 2 + c + 1]
            r1 = pc[:, 7 + (d * 2 + 1) * 2 + c:7 + (d * 2 + 1) * 2 + c + 1]
            L = lambda j: lbc[:, k, j:j + 1]
            P.tt(L(2), r0, r1, ALU.subtract, [pc], [lbc])
            P.actf(L(3), L(2), AF.Sigmoid, [lbc], [lbc])
            P.actf(L(4), L(2), AF.Sigmoid, [lbc], [lbc], scale=-1.0)
            P.tt(L(5), L(3), L(3), ALU.subtract, [lbc], [lbc])
            P.tt(L(6), L(3), L(4), ALU.add, [lbc], [lbc])
            P.tt(L(6), L(6), L(3), ALU.subtract, [lbc], [lbc])
            P.ts(L(5), L(5), 0.0, 1.0, ALU.max, ALU.min, [lbc], [lbc])
            P.ts(L(6), L(6), 0.0, 1.0, ALU.max, ALU.min, [lbc], [lbc])
            P.tt(L(5), L(5), pc[:, 15:16], ALU.mult, [lbc, pc], [lbc])
            P.stt(L(0), L(6), pc[:, 16:17], L(5), ALU.mult, ALU.add, [lbc, pc], [lbc])
            P.ts(L(1), L(0), -1.0, 1.0, ALU.mult, ALU.add, [lbc], [lbc])
    load_cast_weight(P, wb, lambda c, c0, w: wb[:, c, c0:c0 + w], wext, 8, WEXT, stage)

    xs = [P.sb("x%d" % i, [128, 8, T], F32) for i in range(2)]
    sq = P.sb("sq", [128, 8, T], BF16)
    rstd = P.sb("rstd", [128, T], F32)
    rs2 = P.sb("rs2", [128, T], F32)
    tmp = [P.sb("tmp%d" % i, [128, T], F32) for i in range(2)]
    t3 = [P.sb("t3_%d" % i, [128, T], F32) for i in range(2)]
    hb = P.sb("hb", [128, 8, T], BF16)
    rb = P.sb("rb", [32, 2, T], F32)
    rd = P.sb("rd", [128, 2, T], F32)
    cqn = P.sb("cqn", [128, 2, T], BF16)
    ckvn = P.sb("ckvn", [128, T], BF16)
    s_u = P.sb("s_u", [128, 2, T], F32)
    s_mqn = P.sb("s_mqn", [64, 4, T], BF16)
    s_mqr = P.sb("s_mqr", [32, 4, T], BF16)
    s_mkn = P.sb("s_mkn", [64, 4, T], BF16)
    s_mkr = P.sb("s_mkr", [32, T], BF16)
    s_mv = P.sb("s_mv", [128, 2, 256], BF16)
    s_hq = P.sb("s_hq", [128, 2, T], F32)
    s_hk = P.sb("s_hk", [128, 4, T], F32)
    s_hl = P.sb("s_hl", [128, 4, T], F32)
    s_hv = P.sb("s_hv", [128, 2, 256], BF16)
    s_hg = P.sb("s_hg", [128, 2, T], F32)
    s_dq = P.sb("s_dq", [128, 2, T], BF16)
    s_dk = P.sb("s_dk", [128, T], BF16)
    s_dv = P.sb("s_dv", [128, 2, 128], BF16)
    ss_ps = P.ps("ss_ps", [128, 512], F32)
    pq = [P.ps("pq%d" % i, [128, 512], F32) for i in range(6)]
    pi = [0]

    def nextp():
        pi[0] += 1
        return pq[pi[0] % 6]

    def proj(p, m, c0, ncol, tn):
        for c in range(8):
            P.mm(p[0:ncol, 0:tn], wb[:, c, c0:c0 + ncol], hb[:, c, 0:tn], c == 0, c == 7, [wb, hb], [p])

    def proj_tok(p, sub, c0, ncol, tn):
        n = min(128, tn - sub * 128)
        for c in range(8):
            P.mm(p[0:n, 0:ncol], hb[:, c, sub * 128:sub * 128 + n], wb[:, c, c0:c0 + ncol], c == 0, c == 7, [wb, hb], [p])
        return n

    def rstd_from(ps_, rows, tn, inv_n, out):
        P.actf(out[0:rows, 0:tn], ps_[0:rows, 0:tn], AF.Sqrt, [ps_], [out], scale=inv_n, bias=P.eps_col[0:rows, 0:1])
        P.op("dve", lambda: P.nc.vector.reciprocal(out=out[0:rows, 0:tn], in_=out[0:rows, 0:tn]), [out], [out])

    xTv = xT.rearrange("(c p) n -> p c n", p=128)
    for ti, (s0, tn, ty) in enumerate(token_tiles(T)):
        x = xs[ti % 2]
        co = 3 * ty
        nsub = (tn + 127) // 128
        P.load(x, x[:, :, 0:tn], xTv[:, :, s0:s0 + tn])
        P.load(rb, rb[:, :, 0:tn], ropeb[:, :, s0:s0 + tn])
        P.load(rd, rd[:, :, 0:tn], roped[:, :, s0:s0 + tn])
        make_h(P, x, tn, cols, co, ones_b, sq, ss_ps, rstd, tmp, hb)
        for c in range(2):
            p = nextp()
            proj(p, 128, c * 128, 128, tn)
            P.copy(s_u[:, c, 0:tn], p[:, 0:tn], [p], [s_u], e="act")
        P.store(o_u.rearrange("(c p) n -> p c n", p=128)[:, :, s0:s0 + tn], s_u, s_u[:, :, 0:tn])
        pA = nextp()
        proj(pA, 128, 256, 128, tn)
        pB = nextp()
        proj(pB, 64, 384, 64, tn)
        P.actf(sq[:, 0, 0:tn], pA[:, 0:tn], AF.Square, [pA], [sq])
        P.actf(sq[0:64, 1, 0:tn], pB[0:64, 0:tn], AF.Square, [pB], [sq])
        P.mm(ss_ps[:, 0:tn], ones_b[:, :], sq[:, 0, 0:tn], True, False, [ones_b, sq], [ss_ps])
        P.mm(ss_ps[:, 0:tn], ones_b[0:64, :], sq[0:64, 1, 0:tn], False, True, [ones_b, sq], [ss_ps])
        rstd_from(ss_ps, 128, tn, 1.0 / 192, rs2)
        P.stt(cqn[:, 0, 0:tn], pA[:, 0:tn], pc[:, 0:1], rs2[:, 0:tn], ALU.mult, ALU.mult, [pA, pc, rs2], [cqn])
        P.stt(cqn[0:64, 1, 0:tn], pB[0:64, 0:tn], pc[0:64, 1:2], rs2[0:64, 0:tn], ALU.mult, ALU.mult, [pB, pc, rs2], [cqn])
        for h in range(4):
            def qmm(p, rows, c0):
                P.mm(p[0:rows, 0:tn], wuq_b[:, 0, c0:c0 + rows], cqn[:, 0, 0:tn], True, False, [wuq_b, cqn], [p])
                P.mm(p[0:rows, 0:tn], wuq_b[0:64, 1, c0:c0 + rows], cqn[0:64, 1, 0:tn], False, True, [wuq_b, cqn], [p])
            pn = nextp()
            qmm(pn, 64, h * 128)
            P.copy(s_mqn[:, h, 0:tn], pn[0:64, 0:tn], [pn], [s_mqn], e="act")
            pr = nextp()
            qmm(pr, 32, h * 128 + 64)
            pp_ = nextp()
            qmm(pp_, 32, h * 128 + 96)
            ta, tb = t3[0], t3[1]
            P.tt(ta[0:32, 0:tn], pr[0:32, 0:tn], rb[:, 0, 0:tn], ALU.mult, [pr, rb], [ta])
            P.tt(tb[0:32, 0:tn], pp_[0:32, 0:tn], rb[:, 1, 0:tn], ALU.mult, [pp_, rb], [tb])
            P.tt(s_mqr[:, h, 0:tn], ta[0:32, 0:tn], tb[0:32, 0:tn], ALU.add, [ta, tb], [s_mqr])
        P.store(o_mq[:, 0:64, s0:s0 + tn].rearrange("h p n -> p h n"), s_mqn, s_mqn[:, :, 0:tn])
        P.store(o_mq[:, 64:96, s0:s0 + tn].rearrange("h p n -> p h n"), s_mqr, s_mqr[:, :, 0:tn])
        pK = nextp()
        proj(pK, 128, 448, 128, tn)
        P.actf(sq[:, 0, 0:tn], pK[:, 0:tn], AF.Square, [pK], [sq])
        P.mm(ss_ps[:, 0:tn], ones_b[:, :], sq[:, 0, 0:tn], True, True, [ones_b, sq], [ss_ps])
        rstd_from(ss_ps, 128, tn, 1.0 / 128, rs2)
        P.stt(ckvn[:, 0:tn], pK[:, 0:tn], pc[:, 2:3], rs2[:, 0:tn], ALU.mult, ALU.mult, [pK, pc, rs2], [ckvn])
        for h in range(4):
            pn = nextp()
            P.mm(pn[0:64, 0:tn], wukv_b[:, h * 64:(h + 1) * 64], ckvn[:, 0:tn], True, True, [wukv_b, ckvn], [pn])
            P.copy(s_mkn[:, h, 0:tn], pn[0:64, 0:tn], [pn], [s_mkn], e="act")
        P.store(o_mk[:, 0:64, s0:s0 + tn].rearrange("h p n -> p h n"), s_mkn, s_mkn[:, :, 0:tn])
        pr = nextp()
        proj(pr, 32, 576, 32, tn)
        pp_ = nextp()
        proj(pp_, 32, 2400, 32, tn)
        ta, tb = t3[0], t3[1]
        P.tt(ta[0:32, 0:tn], pr[0:32, 0:tn], rb[:, 0, 0:tn], ALU.mult, [pr, rb], [ta])
        P.tt(tb[0:32, 0:tn], pp_[0:32, 0:tn], rb[:, 1, 0:tn], ALU.mult, [pp_, rb], [tb])
        P.tt(s_mkr[:, 0:tn], ta[0:32, 0:tn], tb[0:32, 0:tn], ALU.add, [ta, tb], [s_mkr])
        for h in range(4):
            P.store(o_mk[h, 64:96, s0:s0 + tn], s_mkr, s_mkr[:, 0:tn])
        for sub in range(nsub):
            n = min(128, tn - sub * 128)
            pv = nextp()
            P.mm(pv[0:n, 0:256], ckvn[:, sub * 128:sub * 128 + n], wukv_b[:, 256:512], True, True, [wukv_b, ckvn], [pv])
            P.copy(s_mv[0:n, sub, :], pv[0:n, 0:256], [pv], [s_mv], e="act")
            P.store(o_mv[s0 + sub * 128:s0 + sub * 128 + n, :], s_mv, s_mv[0:n, sub, :])
        for c in range(2):
            p = nextp()
            proj(p, 128, 608 + c * 128, 128, tn)
            P.copy(s_hq[:, c, 0:tn], p[:, 0:tn], [p], [s_hq], e="act")
            p = nextp()
            proj(p, 128, 1632 + c * 128, 128, tn)
            P.copy(s_hg[:, c, 0:tn], p[:, 0:tn], [p], [s_hg], e="act")
            for d in range(2):
                k = d * 2 + c
                p = nextp()
                proj(p, 128, 1120 + d * 256 + c * 128, 128, tn)
                ta, tb = t3[0], t3[1]
                P.actf(ta[:, 0:tn], p[:, 0:tn], AF.Sigmoid, [p], [ta])
                P.ts(ta[:, 0:tn], ta[:, 0:tn], lbc[:, k, 1:2], lbc[:, k, 0:1], ALU.mult, ALU.add, [ta, lbc], [ta])
                P.ts(ta[:, 0:tn], ta[:, 0:tn], 1e-20, None, ALU.max, None, [ta], [ta])
                P.actf(s_hl[:, k, 0:tn], ta[:, 0:tn], AF.Ln, [ta], [s_hl])
                P.actf(tb[:, 0:tn], p[:, 0:tn], AF.Sigmoid, [p], [tb], scale=-1.0)
                P.ts(s_hk[:, k, 0:tn], tb[:, 0:tn], lbc[:, k, 1:2], None, ALU.mult, None, [tb, lbc], [s_hk])
        P.store(o_hq.rearrange("(c p) n -> p c n", p=128)[:, :, s0:s0 + tn], s_hq, s_hq[:, :, 0:tn])
        P.store(o_hg.rearrange("(c p) n -> p c n", p=128)[:, :, s0:s0 + tn], s_hg, s_hg[:, :, 0:tn])
        P.store(o_hk.rearrange("d (c p) n -> p (d c) n", p=128)[:, :, s0:s0 + tn], s_hk, s_hk[:, :, 0:tn])
        P.store(o_hl.rearrange("d (c p) n -> p (d c) n", p=128)[:, :, s0:s0 + tn], s_hl, s_hl[:, :, 0:tn])
        for sub in range(nsub):
            pv = nextp()
            n = proj_tok(pv, sub, 864, 256, tn)
            P.copy(s_hv[0:n, sub, :], pv[0:n, 0:256], [pv], [s_hv], e="act")
            P.store(o_hv[s0 + sub * 128:s0 + sub * 128 + n, :], s_hv, s_hv[0:n, sub, :])
        for (dst, dcol, c0, cp0, gcol) in ((s_dq, 0, 1888, 2432, 3), (s_dq, 1, 2016, 2560, 3), (s_dk, None, 2144, 2688, 5)):
            pz = nextp()
            proj(pz, 128, c0, 128, tn)
            pzp = nextp()
            proj(pzp, 128, cp0, 128, tn)
            P.actf(sq[:, 0, 0:tn], pz[:, 0:tn], AF.Square, [pz], [sq])
            P.mm(ss_ps[:, 0:tn], blk_b[:, :], sq[:, 0, 0:tn], True, True, [blk_b, sq], [ss_ps])
            rstd_from(ss_ps, 128, tn, 1.0 / 64, rs2)
            ta, tb = t3[0], t3[1]
            P.stt(ta[:, 0:tn], pz[:, 0:tn], pc[:, gcol:gcol + 1], rs2[:, 0:tn], ALU.mult, ALU.mult, [pz, pc, rs2], [ta])
            P.stt(tb[:, 0:tn], pzp[:, 0:tn], pc[:, gcol + 1:gcol + 2], rs2[:, 0:tn], ALU.mult, ALU.mult, [pzp, pc, rs2], [tb])
            P.tt(ta[:, 0:tn], ta[:, 0:tn], rd[:, 0, 0:tn], ALU.mult, [ta, rd], [ta])
            P.tt(tb[:, 0:tn], tb[:, 0:tn], rd[:, 1, 0:tn], ALU.mult, [tb, rd], [tb])
            dv_ = dst[:, dcol, 0:tn] if dcol is not None else dst[:, 0:tn]
            P.tt(dv_, ta[:, 0:tn], tb[:, 0:tn], ALU.add, [ta, tb], [dst])
        P.store(o_dq.rearrange("(c p) n -> p c n", p=128)[:, :, s0:s0 + tn], s_dq, s_dq[:, :, 0:tn])
        P.store(o_dk[:, s0:s0 + tn], s_dk, s_dk[:, 0:tn])
        for sub in range(nsub):
            pv = nextp()
            n = proj_tok(pv, sub, 2272, 128, tn)
            P.copy(s_dv[0:n, sub, :], pv[0:n, 0:128], [pv], [s_dv], e="act")
            P.store(o_dv[s0 + sub * 128:s0 + sub * 128 + n, :], s_dv, s_dv[0:n, sub, :])
    return P.finish()


def build_attn(d, scale, nq=SEQ, nk=LFULL):
    P = Prog("attn%d_%d" % (d, nq))
    NKT = nk // 128
    QW = min(512, nq)
    NQT = nq // QW
    qT = P.din("qT", [d, nq], BF16)
    kT = P.din("kT", [d, nk], BF16)
    v = P.din("v", [nk, 64], BF16)
    sel = P.din("sel", [65, 64])
    oT = P.dout("oT", [64, nq])
    qs = P.sb("qs", [d, nq], BF16)
    ks = P.sb("ks", [d, nk], BF16)
    vs = P.sb("vs", [128, NKT, 65], BF16)
    sels = P.sb("sels", [65, 64], F32)
    ones_b = P.sb("ones_b", [128, 128], BF16)
    sqb = [P.sb("sqb%d" % i, [d, 512], BF16) for i in range(2)]
    mx = P.sb("mx", [128, 8], F32)
    pts = [P.sb("pt%d" % i, [128, 512], BF16) for i in range(4)]
    osb = [P.sb("osb%d" % i, [65, 512], F32) for i in range(2)]
    rec = P.sb("rec", [64, 512], F32)
    ob = [P.sb("ob%d" % i, [64, 512], F32) for i in range(2)]
    pss = [P.ps("pss%d" % i, [128, 512], F32) for i in range(4)]
    pos = [P.ps("pos%d" % i, [128, 512], F32) for i in range(2)]
    pden = P.ps("pden", [128, 512], F32)

    P.op("dve", lambda: P.nc.vector.memset(ones_b[:, :], 1.0), [], [ones_b])
    P.op("dve", lambda: P.nc.vector.memset(vs[:, :, :], 1.0), [], [vs])
    P.op("dve", lambda: P.nc.vector.memset(mx[:, :], 0.0), [], [mx])
    P.load(sels, sels[:, :], sel)
    for c0 in range(0, nq, 4096):
        w_ = min(4096, nq - c0)
        P.load(qs, qs[:, c0:c0 + w_], qT[:, c0:c0 + w_])
    for c0 in range(0, nk, 4160):
        w_ = min(4160, nk - c0)
        P.load(ks, ks[:, c0:c0 + w_], kT[:, c0:c0 + w_])
    vv = v.rearrange("(t p) e -> p t e", p=128)
    for t0 in range(0, NKT, 26):
        n_ = min(26, NKT - t0)
        P.load(vs, vs[:, t0:t0 + n_, 0:64], vv[:, t0:t0 + n_, :])
    i = 0
    for (src, n, col) in ((qs, nq, 0), (ks, nk, 1)):
        for c0 in range(0, n, 512):
            w = min(512, n - c0)
            sq = sqb[i % 2]
            pp = pss[i % 4]
            P.actf(sq[:, 0:w], src[:, c0:c0 + w], AF.Square, [src], [sq])
            P.mm(pp[:, 0:w], ones_b[0:d, :], sq[:, 0:w], True, True, [ones_b, sq], [pp])
            P.op("dve", lambda: P.nc.vector.tensor_reduce(out=mx[:, 2:3], in_=pp[:, 0:w], axis=AX.X, op=ALU.max),
                 [pp], [mx])
            P.tt(mx[:, col:col + 1], mx[:, col:col + 1], mx[:, 2:3], ALU.max, [mx], [mx])
            i += 1
    P.tt(mx[:, 3:4], mx[:, 0:1], mx[:, 1:2], ALU.mult, [mx], [mx])
    P.actf(mx[:, 4:5], mx[:, 3:4], AF.Sqrt, [mx], [mx], scale=scale * scale)
    P.ts(mx[:, 5:6], mx[:, 4:5], -1.0, None, ALU.mult, None, [mx], [mx])
    step = 0
    for qt in range(NQT):
        po = pos[qt % 2]
        qsl = slice(qt * QW, (qt + 1) * QW)
        for kt in range(NKT):
            ps_ = pss[step % 4]
            pt = pts[step % 4]
            P.mm(ps_[:, 0:QW], ks[:, kt * 128:(kt + 1) * 128], qs[:, qsl], True, True, [ks, qs], [ps_])
            P.actf(pt[:, 0:QW], ps_[:, 0:QW], AF.Exp, [ps_, mx], [pt], scale=scale, bias=mx[:, 5:6])
            P.mm(po[0:65, 0:QW], vs[:, kt, :], pt[:, 0:QW], kt == 0, kt == NKT - 1, [vs, pt], [po])
            step += 1
        o_s = osb[qt % 2]
        o_b = ob[qt % 2]
        P.copy(o_s[:, 0:QW], po[0:65, 0:QW], [po], [o_s])
        P.mm(pden[0:64, 0:QW], sels[:, :], o_s[:, 0:QW], True, True, [sels, o_s], [pden])
        P.op("dve", lambda: P.nc.vector.reciprocal(out=rec[:, 0:QW], in_=pden[0:64, 0:QW]), [pden], [rec])
        P.tt(o_b[:, 0:QW], o_s[0:64, 0:QW], rec[:, 0:QW], ALU.mult, [o_s, rec], [o_b])
        P.store(oT[:, qsl], o_b, o_b[:, 0:QW])
    return P.finish()


S5C = 512


def build_s5():
    P = Prog("s5")
    TC = S5C
    uT = P.din("uT", [128, LFULL])
    prm = P.din("prm", [128, 4, 3])
    bri = P.din("bri", [128, 4, 2, 16])
    cblk = P.din("cblk", [128, 4, 2, 32])
    ident = P.din("ident", [128, 128])
    yT = P.dout("yT", [128, LFULL])

    pr = P.sb("pr", [128, 4, 3], F32)
    br = P.sb("br", [128, 4, 2, 16], F32)
    cb = P.sb("cb", [128, 4, 2, 32], F32)
    cbb = P.sb("cbb", [128, 4, 2, 32], BF16)
    idf = P.sb("idf", [128, 128], F32)
    w = P.sb("w", [128, 24, 4], F32)
    wblk = P.sb("wblk", [128, 4, 2, 32], F32)
    wT = P.sb("wT", [32, 4, 2, 128], BF16)
    Ec = P.sb("Ec", [128, 4, TC], F32)
    Es = P.sb("Es", [128, 4, TC], F32)
    rf = P.sb("rf", [128, 4, TC], F32)
    tsc = [P.sb("tsc%d" % i, [128, TC], F32) for i in range(4)]
    zero = P.sb("zero", [128, 1], F32)
    uf = [P.sb("uf%d" % i, [32, 4, TC], F32) for i in range(2)]
    ub = [P.sb("ub%d" % i, [32, 4, TC], BF16) for i in range(2)]
    bp = [P.sb("bp%d" % i, [128, 4, TC], F32) for i in range(2)]
    gg = [P.sb("gg%d" % i, [128, 4, TC], F32) for i in range(2)]
    xx = [P.sb("xx%d" % i, [128, 4, TC], F32) for i in range(2)]
    xb = [P.sb("xb%d" % i, [128, 4, TC], BF16) for i in range(2)]
    ysb = [P.sb("ysb%d" % i, [32, 4, TC], F32) for i in range(2)]
    pbu = [P.ps("pbu%d" % i, [128, 512], F32) for i in range(4)]
    py = [P.ps("py%d" % i, [128, 512], F32) for i in range(2)]
    ptr = P.ps("ptr", [128, 512], F32)

    P.load(pr, pr[:, :, :], prm)
    P.load(br, br[:, :, :, :], bri)
    P.load(cb, cb[:, :, :, :], cblk)
    P.load(idf, idf[:, :], ident)
    P.op("dve", lambda: P.nc.vector.memset(zero[:, :], 0.0), [], [zero])
    P.op("dve", lambda: P.nc.vector.memset(wblk[:, :, :, :], 0.0), [], [wblk])
    P.copy(cbb[:, :, 0, :], cb[:, :, 0, :], [cb], [cbb])
    P.ts(cbb[:, :, 1, :], cb[:, :, 1, :], -1.0, None, ALU.mult, None, [cb], [cbb])
    W = lambda k: w[:, k, :]
    rw = [w]
    P.ts(W(0), pr[:, :, 0], -1e-4, None, ALU.min, None, [pr], rw)
    P.actf(W(1), pr[:, :, 2], AF.Exp, [pr], rw)
    P.tt(W(2), W(0), W(1), ALU.mult, rw, rw)
    P.actf(W(3), W(2), AF.Exp, rw, rw)
    P.tt(W(4), pr[:, :, 1], W(1), ALU.mult, [pr] + rw, rw)
    P.actf(W(6), W(4), AF.Sin, rw, rw, scale=1.0 / 32)
    P.ts(W(7), W(4), 1.0 / 32, 0.5 * np.pi, ALU.mult, ALU.add, rw, rw)
    P.actf(W(5), W(7), AF.Sin, rw, rw)
    for _ in range(5):
        P.tt(W(7), W(5), W(5), ALU.mult, rw, rw)
        P.tt(W(8), W(6), W(6), ALU.mult, rw, rw)
        P.tt(W(9), W(5), W(6), ALU.mult, rw, rw)
        P.tt(W(5), W(7), W(8), ALU.subtract, rw, rw)
        P.ts(W(6), W(9), 2.0, None, ALU.mult, None, rw, rw)
    P.tt(W(10), W(3), W(5), ALU.mult, rw, rw)
    P.tt(W(11), W(3), W(6), ALU.mult, rw, rw)
    P.tt(W(12), W(0), W(0), ALU.mult, rw, rw)
    P.tt(W(13), pr[:, :, 1], pr[:, :, 1], ALU.mult, [pr], rw)
    P.tt(W(12), W(12), W(13), ALU.add, rw, rw)
    P.op("dve", lambda: P.nc.vector.reciprocal(out=W(12), in_=W(12)), rw, rw)
    P.ts(W(13), W(10), -1.0, None, ALU.add, None, rw, rw)
    P.tt(W(14), W(13), W(0), ALU.mult, rw, rw)
    P.tt(W(15), W(11), pr[:, :, 1], ALU.mult, [pr] + rw, rw)
    P.tt(W(14), W(14), W(15), ALU.add, rw, rw)
    P.tt(W(14), W(14), W(12), ALU.mult, rw, rw)
    P.tt(W(15), W(11), W(0), ALU.mult, rw, rw)
    P.tt(W(16), W(13), pr[:, :, 1], ALU.mult, [pr] + rw, rw)
    P.tt(W(15), W(15), W(16), ALU.subtract, rw, rw)
    P.tt(W(15), W(15), W(12), ALU.mult, rw, rw)
    P.ts(W(17), W(15), -1.0, None, ALU.mult, None, rw, rw)
    for ct in range(4):
        fre = w[:, 14, ct:ct + 1]
        fim = w[:, 15, ct:ct + 1]
        nfim = w[:, 17, ct:ct + 1]
        for half in range(2):
            rows = slice(half * 64, half * 64 + 64)
            cs_ = slice(half * 16, half * 16 + 16)
            P.ts(wblk[rows, ct, 0, cs_], br[rows, ct, 1, :], nfim[rows], None, ALU.mult, None, [br] + rw, [wblk])
            P.stt(wblk[rows, ct, 0, cs_], br[rows, ct, 0, :], fre[rows], wblk[rows, ct, 0, cs_], ALU.mult, ALU.add,
                  [br, wblk] + rw, [wblk])
            P.ts(wblk[rows, ct, 1, cs_], br[rows, ct, 0, :], fim[rows], None, ALU.mult, None, [br] + rw, [wblk])
            P.stt(wblk[rows, ct, 1, cs_], br[rows, ct, 1, :], fre[rows], wblk[rows, ct, 1, cs_], ALU.mult, ALU.add,
                  [br, wblk] + rw, [wblk])
        for ri in range(2):
            P.op("pe", lambda: P.nc.tensor.transpose(out=ptr[0:32, 0:128], in_=wblk[:, ct, ri, :], identity=idf[:, :]),
                 [wblk, idf], [ptr])
            P.copy(wT[:, ct, ri, :], ptr[0:32, 0:128], [ptr], [wT])
        P.copy(Ec[:, ct, 0:1], w[:, 5, ct:ct + 1], rw, [Ec])
        P.copy(Es[:, ct, 0:1], w[:, 6, ct:ct + 1], rw, [Es])
        k = 1
        while k < TC:
            ck = Ec[:, ct, k - 1:k]
            sk = Es[:, ct, k - 1:k]
            t0_, t1_ = tsc[0], tsc[1]
            P.ts(t0_[:, 0:k], Es[:, ct, 0:k], sk, None, ALU.mult, None, [Es], [t0_])
            P.ts(t1_[:, 0:k], Es[:, ct, 0:k], ck, None, ALU.mult, None, [Es, Ec], [t1_])
            P.stt(Ec[:, ct, k:2 * k], Ec[:, ct, 0:k], ck, t0_[:, 0:k], ALU.mult, ALU.subtract, [Ec, t0_], [Ec])
            P.stt(Es[:, ct, k:2 * k], Ec[:, ct, 0:k], sk, t1_[:, 0:k], ALU.mult, ALU.add, [Ec, Es, t1_], [Es])
            k *= 2
        P.ts(rf[:, ct, :], Ec[:, ct, :], 0.0, w[:, 3, ct:ct + 1], ALU.mult, ALU.add, [Ec] + rw, [rf])
    uv = uT.rearrange("(ct p) n -> p ct n", p=32)
    yv = yT.rearrange("(ct p) n -> p ct n", p=32)
    chunks = [(c0, min(TC, LFULL - c0)) for c0 in range(0, LFULL, TC)]
    prev = None
    for ci, (c0, tn) in enumerate(chunks):
        u_f = uf[ci % 2]
        u_b = ub[ci % 2]
        y_s = ysb[ci % 2]
        P.load(u_f, u_f[:, :, 0:tn], uv[:, :, c0:c0 + tn])
        P.copy(u_b[:, :, 0:tn], u_f[:, :, 0:tn], [u_f], [u_b], e="pool")
        for ct in range(4):
            pre = pbu[(2 * ct) % 4]
            pim = pbu[(2 * ct + 1) % 4]
            P.mm(pre[:, 0:tn], wT[:, ct, 0, :], u_b[:, ct, 0:tn], True, True, [wT, u_b], [pre])
            P.mm(pim[:, 0:tn], wT[:, ct, 1, :], u_b[:, ct, 0:tn], True, True, [wT, u_b], [pim])
            c_ = Ec[:, ct, 0:tn]
            s_ = Es[:, ct, 0:tn]
            t0_, t1_, t2_, t3_ = tsc
            P.tt(t0_[:, 0:tn], pre[:, 0:tn], c_, ALU.mult, [pre, Ec], [t0_])
            P.tt(t1_[:, 0:tn], pim[:, 0:tn], s_, ALU.mult, [pim, Es], [t1_])
            P.tt(bp[0][:, ct, 0:tn], t0_[:, 0:tn], t1_[:, 0:tn], ALU.add, [t0_, t1_], [bp[0]])
            P.tt(t2_[:, 0:tn], pim[:, 0:tn], c_, ALU.mult, [pim, Ec], [t2_])
            P.tt(t3_[:, 0:tn], pre[:, 0:tn], s_, ALU.mult, [pre, Es], [t3_])
            P.tt(bp[1][:, ct, 0:tn], t2_[:, 0:tn], t3_[:, 0:tn], ALU.subtract, [t2_, t3_], [bp[1]])
        for ct in range(4):
            for ri in range(2):
                init = zero[:, 0:1] if prev is None else xx[ri][:, ct, prev - 1:prev]
                P.op("dve", lambda: P.nc.vector.tensor_tensor_scan(
                    out=gg[ri][:, ct, 0:tn], data0=rf[:, ct, 0:tn], data1=bp[ri][:, ct, 0:tn], initial=init,
                    op0=ALU.mult, op1=ALU.add), [rf, bp[ri], xx[ri], zero], [gg[ri]])
        for ct in range(4):
            c_ = Ec[:, ct, 0:tn]
            s_ = Es[:, ct, 0:tn]
            t0_, t1_, t2_, t3_ = tsc
            P.tt(t0_[:, 0:tn], gg[0][:, ct, 0:tn], c_, ALU.mult, [gg[0], Ec], [t0_])
            P.tt(t1_[:, 0:tn], gg[1][:, ct, 0:tn], s_, ALU.mult, [gg[1], Es], [t1_])
            P.tt(xx[0][:, ct, 0:tn], t0_[:, 0:tn], t1_[:, 0:tn], ALU.subtract, [t0_, t1_], [xx[0]])
            P.tt(t2_[:, 0:tn], gg[0][:, ct, 0:tn], s_, ALU.mult, [gg[0], Es], [t2_])
            P.tt(t3_[:, 0:tn], gg[1][:, ct, 0:tn], c_, ALU.mult, [gg[1], Ec], [t3_])
            P.tt(xx[1][:, ct, 0:tn], t2_[:, 0:tn], t3_[:, 0:tn], ALU.add, [t2_, t3_], [xx[1]])
            P.copy(xb[0][:, ct, 0:tn], xx[0][:, ct, 0:tn], [xx[0]], [xb[0]], e="act")
            P.copy(xb[1][:, ct, 0:tn], xx[1][:, ct, 0:tn], [xx[1]], [xb[1]], e="act")
            pp = py[ct % 2]
            P.mm(pp[0:32, 0:tn], cbb[:, ct, 0, :], xb[0][:, ct, 0:tn], True, False, [cbb, xb[0]], [pp])
            P.mm(pp[0:32, 0:tn], cbb[:, ct, 1, :], xb[1][:, ct, 0:tn], False, True, [cbb, xb[1]], [pp])
            P.copy(y_s[:, ct, 0:tn], pp[0:32, 0:tn], [pp], [y_s], e="act")
        P.store(yv[:, :, c0:c0 + tn], y_s, y_s[:, :, 0:tn])
        prev = tn
    return P.finish()


def build_hgrn():
    P = Prog("hgrn")
    SP = 512
    qT = P.din("qT", [128, LFULL])
    kT = P.din("kT", [128, LFULL])
    lT = P.din("lT", [128, LFULL])
    v = P.din("v", [LFULL, 128], BF16)
    identb = P.din("identb", [64, 64])
    mask = P.din("mask", [64, 64])
    oT = P.dout("oT", [128, LFULL])

    idf = P.sb("idf", [64, 64], F32)
    idb = P.sb("idb", [64, 64], BF16)
    mk = P.sb("mk", [64, 128], F32)
    ones = P.sb("ones", [64, 2 * SP], F32)
    zero = P.sb("zero", [64, 1], F32)
    S = P.sb("S", [64, 2, 64], F32)
    Sb = P.sb("Sb", [64, 2, 64], BF16)
    Stmp = P.sb("Stmp", [64, 2, 64], F32)
    Mm = P.sb("Mm", [64, 2, 8], F32)
    em = P.sb("em", [64, 2, 8], F32)
    el = P.sb("el", [64, 2, 8], F32)
    attc = P.sb("attc", [64, 128], F32)
    qs = [P.sb("qs%d" % i, [64, 2, SP], F32) for i in range(2)]
    ks = [P.sb("ks%d" % i, [64, 2, SP], F32) for i in range(2)]
    ls = [P.sb("ls%d" % i, [64, 2, SP], F32) for i in range(2)]
    vs = [P.sb("vs%d" % i, [64, 8, 128], BF16) for i in range(2)]
    G = P.sb("G", [64, 2, SP], F32)
    Gc = P.sb("Gc", [64, 2, SP], F32)
    e1 = P.sb("e1", [64, 2, SP], F32)
    e2 = P.sb("e2", [64, 2, SP], F32)
    qb = P.sb("qb", [64, 2, SP], BF16)
    kb = P.sb("kb", [64, 2, SP], BF16)
    ktok = [P.sb("ktok%d" % i, [64, 128], BF16) for i in range(2)]
    attb = [P.sb("attb%d" % i, [64, 128], BF16) for i in range(2)]
    osb = [P.sb("osb%d" % i, [64, 2, SP], F32) for i in range(2)]
    pkt = [P.ps("pkt%d" % i, [64, 1024], BF16) for i in range(1)]
    patt = [P.ps("patt%d" % i, [64, 512], F32) for i in range(2)]
    po = [P.ps("po%d" % i, [64, 512], F32) for i in range(2)]
    pS = P.ps("pS", [64, 512], F32)

    P.load(idf, idf[:, :], identb)
    P.copy(idb[:, :], idf[:, :], [idf], [idb])
    P.load(mk, mk[:, 0:64], mask)
    P.load(mk, mk[:, 64:128], mask)
    P.op("dve", lambda: P.nc.vector.memset(ones[:, :], 1.0), [], [ones])
    P.op("dve", lambda: P.nc.vector.memset(zero[:, :], 0.0), [], [zero])
    P.op("dve", lambda: P.nc.vector.memset(S[:, :, :], 0.0), [], [S])
    P.op("dve", lambda: P.nc.vector.memset(Sb[:, :, :], 0.0), [], [Sb])
    qv = qT.rearrange("(h p) n -> p h n", p=64)
    kv = kT.rearrange("(h p) n -> p h n", p=64)
    lv = lT.rearrange("(h p) n -> p h n", p=64)
    ov = oT.rearrange("(h p) n -> p h n", p=64)
    vv = v.rearrange("(c p) e -> p c e", p=64)
    spans = [(c0, min(SP, LFULL - c0)) for c0 in range(0, LFULL, SP)]
    ch = 0
    for si, (c0, tn) in enumerate(spans):
        q_, k_, l_, v_ = qs[si % 2], ks[si % 2], ls[si % 2], vs[si % 2]
        o_ = osb[si % 2]
        nch = tn // 64
        P.load(q_, q_[:, :, 0:tn], qv[:, :, c0:c0 + tn])
        P.load(k_, k_[:, :, 0:tn], kv[:, :, c0:c0 + tn])
        P.load(l_, l_[:, :, 0:tn], lv[:, :, c0:c0 + tn])
        P.load(v_, v_[:, 0:nch, :], vv[:, c0 // 64:c0 // 64 + nch, :])
        for h in range(2):
            P.op("dve", lambda: P.nc.vector.tensor_tensor_scan(
                out=G[:, h, 0:tn], data0=ones[:, 0:tn], data1=l_[:, h, 0:tn], initial=zero[:, 0:1],
                op0=ALU.mult, op1=ALU.add), [ones, l_, zero], [G])
        for j in range(nch):
            for h in range(2):
                mid = j * 64 + 31
                P.ts(Gc[:, h, j * 64:(j + 1) * 64], G[:, h, j * 64:(j + 1) * 64], G[:, h, mid:mid + 1], None,
                     ALU.subtract, None, [G], [Gc])
        G4 = G[:, :, 0:nch * 64].rearrange("p h (j t) -> p h j t", t=64)
        P.copy(Mm[:, :, 0:1], G4[:, :, 0:1, 31], [G], [Mm])
        if nch > 1:
            P.tt(Mm[:, :, 1:nch], G4[:, :, 1:nch, 31], G4[:, :, 0:nch - 1, 63], ALU.subtract, [G], [Mm])
        P.actf(e1[:, :, 0:tn], Gc[:, :, 0:tn], AF.Exp, [Gc], [e1])
        P.actf(e2[:, :, 0:tn], Gc[:, :, 0:tn], AF.Exp, [Gc], [e2], scale=-1.0)
        P.actf(em[:, :, 0:nch], Mm[:, :, 0:nch], AF.Exp, [Mm], [em])
        e14 = e1[:, :, 0:nch * 64].rearrange("p h (j t) -> p h j t", t=64)
        P.tt(el[:, :, 0:nch], em[:, :, 0:nch], e14[:, :, :, 63], ALU.mult, [em, e1], [el])
        P.tt(qb[:, :, 0:tn], q_[:, :, 0:tn], e1[:, :, 0:tn], ALU.mult, [q_, e1], [qb])
        P.tt(kb[:, :, 0:tn], k_[:, :, 0:tn], e2[:, :, 0:tn], ALU.mult, [k_, e2], [kb])
        for j in range(nch):
            cs_ = slice(j * 64, (j + 1) * 64)
            kt_ = ktok[ch % 2]
            ab = attb[ch % 2]
            pa = patt[ch % 2]
            pp = po[ch % 2]
            pk = pkt[0]
            for h in range(2):
                P.op("pe", lambda: P.nc.tensor.transpose(out=pk[0:64, h * 64:(h + 1) * 64], in_=kb[:, h, cs_],
                                                         identity=idb[:, :]), [kb, idb], [pk])
            P.copy(kt_[:, :], pk[0:64, 0:128], [pk], [kt_], e="act")
            for h in range(2):
                P.mm(pa[0:64, h * 64:(h + 1) * 64], kb[:, h, cs_], qb[:, h, cs_], True, True, [kb, qb], [pa])
            P.ts(attc[:, :], pa[0:64, 0:128], 3.0e38, -3.0e38, ALU.min, ALU.max, [pa], [attc])
            P.tt(ab[:, :], attc[:, :], mk[:, :], ALU.mult, [attc, mk], [ab])
            for h in range(2):
                P.ts(Sb[:, h, :], S[:, h, :], em[:, h, j:j + 1], None, ALU.mult, None, [S, em], [Sb])
            for h in range(2):
                P.mm(pp[0:64, h * 64:(h + 1) * 64], v_[:, j, h * 64:(h + 1) * 64], ab[:, h * 64:(h + 1) * 64],
                     True, False, [v_, ab], [pp])
                P.mm(pp[0:64, h * 64:(h + 1) * 64], Sb[:, h, :], qb[:, h, cs_], False, True, [Sb, qb], [pp])
            P.copy(o_[:, :, cs_], pp[0:64, 0:128].rearrange("p (h t) -> p h t", h=2), [pp], [o_], e="act")
            for h in range(2):
                P.mm(pS[0:64, h * 64:(h + 1) * 64], kt_[:, h * 64:(h + 1) * 64], v_[:, j, h * 64:(h + 1) * 64],
                     True, True, [kt_, v_], [pS])
            for h in range(2):
                P.ts(Stmp[:, h, :], pS[0:64, h * 64:(h + 1) * 64], e1[:, h, j * 64 + 63:j * 64 + 64], None, ALU.mult, None,
                     [pS, e1], [Stmp])
                P.stt(S[:, h, :], S[:, h, :], el[:, h, j:j + 1], Stmp[:, h, :], ALU.mult, ALU.add, [S, el, Stmp], [S])
            ch += 1
        P.store(ov[:, :, c0:c0 + tn], o_, o_[:, :, 0:tn])
    return P.finish()


def build_merge():
    P = Prog("merge")
    T = 256
    xT = P.din("xT", [D, NTOK])
    wg = P.din("wg", [D, 4096])
    wbr = P.din("wbr", [1024, D])
    wout = P.din("wout", [D, D])
    wglu = P.din("wglu", [256, 256])
    gp = P.din("gp", [128, 8, 2])
    modc = P.din("modc", [128, 8, 6])
    pcols = P.din("pcols", [128, 4])
    blk = P.din("blk", [128, 128])
    bin_ = P.din("bin", [8, 256, NTOK])
    outT = P.dout("outT", [D, NTOK])

    wgb = P.sb("wgb", [128, 8, 4096], BF16)
    wbrb = P.sb("wbrb", [128, 8, D], BF16)
    woutb = P.sb("woutb", [128, 8, D], BF16)
    wglub = P.sb("wglub", [128, 2, 256], BF16)
    stage = [P.sb("stg%d" % i, [128, 1024], F32) for i in range(2)]
    ones_b = P.sb("ones_b", [128, 128], BF16)
    blk_f = P.sb("blk_f", [128, 128], F32)
    blk_b = P.sb("blk_b", [128, 128], BF16)
    P.eps_col = P.sb("eps_col", [128, 1], F32)
    gps = P.sb("gps", [128, 8, 2], F32)
    mods = P.sb("mods", [128, 8, 6], F32)
    cols = P.sb("cols", [128, 8, 6], F32)
    pc = P.sb("pc", [128, 4], F32)
    P.op("dve", lambda: P.nc.vector.memset(ones_b[:, :], 1.0), [], [ones_b])
    P.op("dve", lambda: P.nc.vector.memset(P.eps_col[:, :], EPS), [], [P.eps_col])
    P.load(gps, gps[:, :, :], gp)
    P.load(mods, mods[:, :, :], modc)
    P.load(pc, pc[:, :], pcols)
    P.load(blk_f, blk_f[:, :], blk)
    P.copy(blk_b[:, :], blk_f[:, :], [blk_f], [blk_b])
    mod_cols(P, cols, mods, gps, True, 1.0, 1)
    load_cast_weight(P, wgb, lambda c, c0, w: wgb[:, c, c0:c0 + w], wg, 8, 4096, stage, piece=1024)
    load_cast_weight(P, wbrb, lambda c, c0, w: wbrb[:, c, c0:c0 + w], wbr, 8, D, stage, piece=1024)
    load_cast_weight(P, woutb, lambda c, c0, w: woutb[:, c, c0:c0 + w], wout, 8, D, stage, piece=1024)
    load_cast_weight(P, wglub, lambda c, c0, w: wglub[:, c, c0:c0 + w], wglu, 2, 256, stage, piece=256)

    xs = [P.sb("x%d" % i, [128, 8, T], F32) for i in range(2)]
    bi = [P.sb("bi%d" % i, [128, 16, T], F32) for i in range(2)]
    sq = P.sb("sq", [128, 8, T], BF16)
    rstd = P.sb("rstd", [128, T], F32)
    rs2 = P.sb("rs2", [128, T], F32)
    tmp = [P.sb("tmp%d" % i, [128, T], F32) for i in range(2)]
    t3 = [P.sb("t3_%d" % i, [128, T], F32) for i in range(3)]
    hb = P.sb("hb", [128, 8, T], BF16)
    gf = P.sb("gf", [128, 2, T], F32)
    gbf = P.sb("gbf", [128, 2, T], BF16)
    yb = P.sb("yb", [128, 8, T], BF16)
    sg = [P.sb("sg%d" % i, [128, T], F32) for i in range(2)]
    acc = P.sb("acc", [128, T], F32)
    mb = P.sb("mb", [128, 8, T], BF16)
    ys = P.sb("ys", [128, 8, T], F32)
    ss_ps = P.ps("ss_ps", [128, 512], F32)
    pg = [P.ps("pg%d" % i, [128, 512], F32) for i in range(3)]
    pb = [P.ps("pb%d" % i, [128, 512], F32) for i in range(3)]

    xTv = xT.rearrange("(c p) n -> p c n", p=128)
    oTv = outT.rearrange("(c p) n -> p c n", p=128)
    bv = bin_.rearrange("k (c p) n -> p (k c) n", p=128)
    for ti, (s0, tn, ty) in enumerate(token_tiles(T)):
        x = xs[ti % 2]
        b_ = bi[ti % 2]
        co = 3 * ty
        P.load(x, x[:, :, 0:tn], xTv[:, :, s0:s0 + tn])
        P.load(b_, b_[:, 0:8, 0:tn], bv[:, 0:8, s0:s0 + tn])
        P.load(b_, b_[:, 8:16, 0:tn], bv[:, 8:16, s0:s0 + tn])
        make_h(P, x, tn, cols, co, ones_b, sq, ss_ps, rstd, tmp, hb)
        B = lambda k, c: b_[:, k * 2 + c, 0:tn]
        for c in range(2):
            t = t3[c]
            P.stt(t[:, 0:tn], B(0, c), pc[:, c:c + 1], B(1, c), ALU.mult, ALU.add, [b_, pc], [t])
            P.tt(t[:, 0:tn], t[:, 0:tn], B(2, c), ALU.add, [t, b_], [t])
            P.actf(gf[:, c, 0:tn], t[:, 0:tn], AF.Gelu, [t], [gf])
            P.copy(gbf[:, c, 0:tn], gf[:, c, 0:tn], [gf], [gbf])
        for c in range(2):
            p = pg[c]
            for kc in range(2):
                P.mm(p[:, 0:tn], wglub[:, kc, c * 128:(c + 1) * 128], gbf[:, kc, 0:tn], kc == 0, kc == 1, [wglub, gbf], [p])
            s = sg[c]
            P.actf(s[:, 0:tn], p[:, 0:tn], AF.Sigmoid, [p], [s])
            P.tt(yb[:, 0 + c, 0:tn], gf[:, c, 0:tn], s[:, 0:tn], ALU.mult, [gf, s], [yb])
        for c in range(2):
            P.copy(yb[:, 2 + c, 0:tn], B(3, c), [b_], [yb], e="pool")
            P.copy(yb[:, 6 + c, 0:tn], B(7, c), [b_], [yb], e="pool")
        for c in range(2):
            t = t3[c]
            P.tt(t[:, 0:tn], B(4, c), B(5, c), ALU.add, [b_], [t])
            P.actf(sq[:, c, 0:tn], t[:, 0:tn], AF.Square, [t], [sq])
            P.mm(ss_ps[:, 0:tn], blk_b[:, :], sq[:, c, 0:tn], True, True, [blk_b, sq], [ss_ps])
            P.actf(rs2[:, 0:tn], ss_ps[:, 0:tn], AF.Sqrt, [ss_ps], [rs2], scale=1.0 / 64, bias=P.eps_col[:, 0:1])
            P.op("dve", lambda: P.nc.vector.reciprocal(out=rs2[:, 0:tn], in_=rs2[:, 0:tn]), [rs2], [rs2])
            P.stt(t[:, 0:tn], t[:, 0:tn], pc[:, 2:3], rs2[:, 0:tn], ALU.mult, ALU.mult, [t, pc, rs2], [t])
            s = sg[c]
            P.actf(s[:, 0:tn], B(6, c), AF.Silu, [b_], [s])
            P.tt(yb[:, 4 + c, 0:tn], t[:, 0:tn], s[:, 0:tn], ALU.mult, [t, s], [yb])
        for k in range(8):
            for i in range(4):
                pg_ = pg[(k * 4 + i) % 3]
                pb_ = pb[(k * 4 + i) % 3]
                for c in range(8):
                    P.mm(pg_[:, 0:tn], wgb[:, c, i * 1024 + k * 128:i * 1024 + (k + 1) * 128], hb[:, c, 0:tn],
                         c == 0, c == 7, [wgb, hb], [pg_])
                for kc in range(2):
                    P.mm(pb_[:, 0:tn], wbrb[:, i * 2 + kc, k * 128:(k + 1) * 128], yb[:, i * 2 + kc, 0:tn],
                         kc == 0, kc == 1, [wbrb, yb], [pb_])
                s = sg[i % 2]
                P.actf(s[:, 0:tn], pg_[:, 0:tn], AF.Sigmoid, [pg_], [s])
                if i == 0:
                    P.tt(acc[:, 0:tn], s[:, 0:tn], pb_[:, 0:tn], ALU.mult, [s, pb_], [acc])
                else:
                    t = t3[i % 3]
                    P.tt(t[:, 0:tn], s[:, 0:tn], pb_[:, 0:tn], ALU.mult, [s, pb_], [t])
                    if i < 3:
                        P.tt(acc[:, 0:tn], acc[:, 0:tn], t[:, 0:tn], ALU.add, [acc, t], [acc], e="pool")
                    else:
                        P.tt(mb[:, k, 0:tn], acc[:, 0:tn], t[:, 0:tn], ALU.add, [acc, t], [mb], e="pool")
        for k in range(8):
            p = pg[k % 3]
            for c in range(8):
                P.mm(p[:, 0:tn], woutb[:, c, k * 128:(k + 1) * 128], mb[:, c, 0:tn], c == 0, c == 7, [woutb, mb], [p])
            P.copy(ys[:, k, 0:tn], p[:, 0:tn], [p], [ys], e="act")
        rms_rstd(P, lambda c: ys[:, c, 0:tn], 8, tn, ones_b, sq, ss_ps, rstd, 1.0 / D, [ys])
        for c in range(8):
            t = tmp[c % 2]
            P.stt(t[:, 0:tn], ys[:, c, 0:tn], cols[:, c, co + 2:co + 3], rstd[:, 0:tn], ALU.mult, ALU.mult,
                  [ys, cols, rstd], [t])
            P.tt(x[:, c, 0:tn], t[:, 0:tn], x[:, c, 0:tn], ALU.add, [t, x], [x], e="pool")
        P.store(oTv[:, :, s0:s0 + tn], x, x[:, :, 0:tn])
    return P.finish()


_PROGS = {}


def prog(name, fn, *a):
    key = (name,) + a
    if key not in _PROGS:
        _PROGS[key] = fn(*a)
    return _PROGS[key]


def colz(v):
    return np.ascontiguousarray(np.asarray(v, np.float32).reshape(-1, 128).T)


def core_tokens(lat, ctx, i):
    b, seg = i // 4, i % 4
    return np.concatenate([lat[b, seg * LTOK:(seg + 1) * LTOK], ctx[b, seg * CTOK:(seg + 1) * CTOK]], axis=0)


def gather_fm(outs, F):
    dt = outs[0].dtype
    lat = np.empty((2, SEQ, F), dt)
    ctx = np.empty((2, CTX, F), dt)
    for i, o in enumerate(outs):
        b, seg = i // 4, i % 4
        lat[b, seg * LTOK:(seg + 1) * LTOK] = o[:, :LTOK].T
        ctx[b, seg * CTOK:(seg + 1) * CTOK] = o[:, LTOK:].T
    return lat, ctx


def gather_tm(outs, F):
    dt = outs[0].dtype
    lat = np.empty((2, SEQ, F), dt)
    ctx = np.empty((2, CTX, F), dt)
    for i, o in enumerate(outs):
        b, seg = i // 4, i % 4
        lat[b, seg * LTOK:(seg + 1) * LTOK] = o[:LTOK]
        ctx[b, seg * CTOK:(seg + 1) * CTOK] = o[LTOK:]
    return lat, ctx


def rope_perm(r):
    q = r // 4
    j = np.arange(r)
    return np.where((j % (2 * q)) < q, j + q, j - q)


def rope_tables(r, pos):
    q = r // 4
    half = r // 2
    inv = (10000.0 ** (-np.arange(0, half, 2, dtype=np.float32) / half)).astype(np.float32)
    rows = (pos // 64).astype(np.float32)
    cols_ = (pos % 64).astype(np.float32)
    ang_r = rows[:, None] * inv
    ang_c = cols_[:, None] * inv
    ang = np.concatenate([ang_r, ang_r, ang_c, ang_c], axis=-1).astype(np.float32)
    cos = np.cos(ang).T
    sin = np.sin(ang).T
    j = np.arange(r)
    sgn = np.where((j % (2 * q)) < q, -1.0, 1.0)[:, None]
    return cos.astype(np.float32), (sin * sgn).astype(np.float32)


def scan_order(ctx, lat, d):
    if d == 0:
        return np.concatenate([ctx, lat], axis=0)
    return np.concatenate([ctx[::-1], lat[::-1]], axis=0)


def unscan(y, d):
    c, l = y[:CTX], y[CTX:]
    if d == 1:
        c, l = c[::-1], l[::-1]
    return c, l


DBG = {}


def kernel(x, c, ctx, c_ctx, w_ada, b_ada, norm_pre, norm_post, ffn_w1, ffn_w3, ffn_w2, w_in,
           s5_lambda_re, s5_lambda_im, s5_log_dt, s5_b_re, s5_b_im, s5_c_re, s5_c_im, s5_d, s5_w_glu,
           mla_q_norm, mla_w_uq, mla_kv_norm, mla_w_ukv, hgrn_lb_raw, hgrn_o_norm,
           gqa_q_norm, gqa_k_norm, w_branch, w_out, _debug=False, _layers=2):
    f32 = np.float32
    A = lambda a: np.ascontiguousarray(np.asarray(a))
    x, ctx = A(x), A(ctx)
    L = w_ada.shape[0]
    cT = np.stack([colz(c[0]), colz(c[1]), colz(c_ctx)], axis=-1)
    Wall = np.concatenate([A(w_ada[l]) for l in range(L)], axis=1)
    ball = np.concatenate([A(b_ada[l]) for l in range(L)], axis=0)
    ims = []
    for i in range(NCORE):
        sl = slice(i * MODC, (i + 1) * MODC)
        ims.append(dict(cT=A(cT), W=A(Wall[:, sl]), bias=A(np.broadcast_to(ball[sl], (3, MODC)))))
    res = run(prog("mod", build_mod), ims)
    mod = np.concatenate([r["out"] for r in res], axis=1).reshape(3, L, 9, D)

    def modc_for(i, l, j):
        b = i // 4
        vs_ = [mod[b, l, 3 * j + k] for k in range(3)] + [mod[2, l, 3 * j + k] for k in range(3)]
        return A(np.stack([colz(v_) for v_ in vs_], axis=-1))

    def gp_for(g0, g1):
        return A(np.stack([colz(g0), colz(g1)], axis=-1))

    def to_fm(lat, ctx_):
        return [A(core_tokens(lat, ctx_, i).T) for i in range(NCORE)]

    def run_ffn(XT, l, j, jj):
        ims = [dict(xT=XT[i], w1=A(ffn_w1[l, jj]), w3=A(ffn_w3[l, jj]), w2=A(ffn_w2[l, jj]),
                    gp=gp_for(norm_pre[l, j], norm_post[l, j]), modc=modc_for(i, l, j)) for i in range(NCORE)]
        return [r["outT"] for r in run(prog("ffn", build_ffn), ims)]

    blk = np.zeros((128, 128), f32)
    blk[:64, :64] = 1
    blk[64:, 64:] = 1
    sel = np.zeros((65, 64), f32)
    sel[64] = 1
    p32, p64 = rope_perm(32), rope_perm(64)
    XT = to_fm(x, ctx)
    for l in range(min(L, _layers)):
        last = l == L - 1
        XT = run_ffn(XT, l, 0, 0)
        if _debug:
            DBG["x1_%d" % l] = gather_fm(XT, D)
        W = A(w_in[l])
        wext = A(np.concatenate([W[:, :2400], W[:, 576:608][:, p32],
                                 W[:, 1888:2144].reshape(D, 4, 64)[:, :, p64].reshape(D, 256),
                                 W[:, 2144:2272].reshape(D, 2, 64)[:, :, p64].reshape(D, 128)], axis=1))
        wuq = np.zeros((192, 512), f32)
        uq = A(mla_w_uq[l]).reshape(192, 4, 96)
        for h in range(4):
            wuq[:, h * 128:h * 128 + 96] = uq[:, h]
            wuq[:, h * 128 + 96:h * 128 + 128] = uq[:, h, 64:][:, p32]
        ukv = A(mla_w_ukv[l]).reshape(128, 4, 128)
        wukv = A(np.concatenate([ukv[:, :, :64].reshape(128, 256), ukv[:, :, 64:].reshape(128, 256)], axis=1))
        pcols = np.zeros((128, NPC), f32)
        pcols[:, 0] = mla_q_norm[l][:128]
        pcols[:64, 1] = mla_q_norm[l][128:]
        pcols[:, 2] = mla_kv_norm[l]
        pcols[:, 3] = np.tile(gqa_q_norm[l], 2)
        pcols[:, 4] = np.tile(np.asarray(gqa_q_norm[l])[p64], 2)
        pcols[:, 5] = np.tile(gqa_k_norm[l], 2)
        pcols[:, 6] = np.tile(np.asarray(gqa_k_norm[l])[p64], 2)
        for d_ in range(2):
            for l2 in range(2):
                for c_ in range(2):
                    pcols[:, 7 + (d_ * 2 + l2) * 2 + c_] = hgrn_lb_raw[d_, l2, c_ * 128:(c_ + 1) * 128]
        pcols[:, 15] = 1.0 if l == 0 else 0.0
        pcols[:, 16] = 1.0 if l == 1 else 0.0
        ims = []
        for i in range(NCORE):
            seg = i % 4
            pos = np.arange(seg * LTOK, (seg + 1) * LTOK)
            rb = np.zeros((32, 2, NTOK), f32)
            rd = np.zeros((128, 2, NTOK), f32)
            rb[:, 0, LTOK:] = 1.0
            rd[:, 0, LTOK:] = 1.0
            cb_, sb_ = rope_tables(32, pos)
            cd_, sd_ = rope_tables(64, pos)
            rb[:, 0, :LTOK], rb[:, 1, :LTOK] = cb_, sb_
            rd[:, 0, :LTOK], rd[:, 1, :LTOK] = np.tile(cd_, (2, 1)), np.tile(sd_, (2, 1))
            ims.append(dict(xT=XT[i], wext=wext, gp=gp_for(norm_pre[l, 1], norm_post[l, 1]), modc=modc_for(i, l, 1),
                            pcols=pcols, wuq=wuq, wukv=wukv, ropeb=rb, roped=rd, blk=blk))
        pres = run(prog("proj", build_proj), ims)
        u_l, u_c = gather_fm([r["o_u"] for r in pres], 256)
        mq_l, mq_c = gather_fm([r["o_mq"].reshape(384, NTOK) for r in pres], 384)
        mk_l, mk_c = gather_fm([r["o_mk"].reshape(384, NTOK) for r in pres], 384)
        mv_l, mv_c = gather_tm([r["o_mv"] for r in pres], 256)
        hq_l, hq_c = gather_fm([r["o_hq"] for r in pres], 256)
        hk_l, hk_c = gather_fm([r["o_hk"].reshape(512, NTOK) for r in pres], 512)
        hl_l, hl_c = gather_fm([r["o_hl"].reshape(512, NTOK) for r in pres], 512)
        hv_l, hv_c = gather_tm([r["o_hv"] for r in pres], 256)
        hg_l, hg_c = gather_fm([r["o_hg"] for r in pres], 256)
        dq_l, dq_c = gather_fm([r["o_dq"] for r in pres], 256)
        dk_l, dk_c = gather_fm([r["o_dk"] for r in pres], 128)
        dv_l, dv_c = gather_tm([r["o_dv"] for r in pres], 128)
        if _debug:
            DBG["proj_%d" % l] = dict(u=(u_l, u_c), mq=(mq_l, mq_c), mk=(mk_l, mk_c), mv=(mv_l, mv_c), hq=(hq_l, hq_c),
                                      hk=(hk_l, hk_c), hl=(hl_l, hl_c), hv=(hv_l, hv_c), hg=(hg_l, hg_c),
                                      dq=(dq_l, dq_c), dk=(dk_l, dk_c), dv=(dv_l, dv_c))
        def attn(d, scale, q_of, k_of, v_of, nq, nk):
            ims = [dict(qT=A(q_of(i).T), kT=A(k_of(i).T), v=A(v_of(i)), sel=sel) for i in range(NCORE)]
            rs = run(prog("attn", build_attn, d, scale, nq, nk), ims)
            out = np.empty((2, nq, 256), f32)
            for i, r in enumerate(rs):
                out[i // 4, :, (i % 4) * 64:(i % 4 + 1) * 64] = r["oT"].T
            return out
        B_ = lambda i: i // 4
        H_ = lambda i: i % 4
        sm = 96 ** -0.5
        mla_l = attn(96, sm, lambda i: mq_l[B_(i)][:, H_(i) * 96:(H_(i) + 1) * 96],
                     lambda i: np.concatenate([mk_c[B_(i)], mk_l[B_(i)]], 0)[:, H_(i) * 96:(H_(i) + 1) * 96],
                     lambda i: np.concatenate([mv_c[B_(i)], mv_l[B_(i)]], 0)[:, H_(i) * 64:(H_(i) + 1) * 64], SEQ, LFULL)
        gqa_l = attn(64, 0.125, lambda i: dq_l[B_(i)][:, H_(i) * 64:(H_(i) + 1) * 64],
                     lambda i: np.concatenate([dk_c[B_(i)], dk_l[B_(i)]], 0)[:, (H_(i) // 2) * 64:(H_(i) // 2 + 1) * 64],
                     lambda i: np.concatenate([dv_c[B_(i)], dv_l[B_(i)]], 0)[:, (H_(i) // 2) * 64:(H_(i) // 2 + 1) * 64],
                     SEQ, LFULL)
        if not last:
            mla_c = attn(96, sm, lambda i: mq_c[B_(i)][:, H_(i) * 96:(H_(i) + 1) * 96],
                         lambda i: mk_c[B_(i)][:, H_(i) * 96:(H_(i) + 1) * 96],
                         lambda i: mv_c[B_(i)][:, H_(i) * 64:(H_(i) + 1) * 64], CTX, CTX)
            gqa_c = attn(64, 0.125, lambda i: dq_c[B_(i)][:, H_(i) * 64:(H_(i) + 1) * 64],
                         lambda i: dk_c[B_(i)][:, (H_(i) // 2) * 64:(H_(i) // 2 + 1) * 64],
                         lambda i: dv_c[B_(i)][:, (H_(i) // 2) * 64:(H_(i) // 2 + 1) * 64], CTX, CTX)
        else:
            mla_c = np.zeros((2, CTX, 256), f32)
            gqa_c = np.zeros((2, CTX, 256), f32)
        ims = []
        for i in range(NCORE):
            b, d_, half = i // 4, (i % 4) // 2, i % 2
            fs = slice(half * 128, (half + 1) * 128)
            useq = scan_order(u_c[b][:, fs], u_l[b][:, fs], d_)
            prm = np.zeros((128, 4, 3), f32)
            bri = np.zeros((128, 4, 2, 16), f32)
            cbl = np.zeros((128, 4, 2, 32), f32)
            for ct in range(4):
                for gl in range(2):
                    g = 8 * half + 2 * ct + gl
                    rs_ = slice(gl * 64, (gl + 1) * 64)
                    prm[rs_, ct, 0] = s5_lambda_re[l, d_, g]
                    prm[rs_, ct, 1] = s5_lambda_im[l, d_, g]
                    prm[rs_, ct, 2] = s5_log_dt[l, d_, g]
                    bri[rs_, ct, 0] = s5_b_re[l, d_, g]
                    bri[rs_, ct, 1] = s5_b_im[l, d_, g]
                    cbl[rs_, ct, 0, gl * 16:(gl + 1) * 16] = np.asarray(s5_c_re[l, d_, g]).T
                    cbl[rs_, ct, 1, gl * 16:(gl + 1) * 16] = np.asarray(s5_c_im[l, d_, g]).T
            ims.append(dict(uT=A(useq.T), prm=prm, bri=bri, cblk=cbl, ident=np.eye(128, dtype=f32)))
        rs = run(prog("s5", build_s5), ims)
        s5_l = np.zeros((2, 2, SEQ, 256), f32)
        s5_c = np.zeros((2, 2, CTX, 256), f32)
        for i, r in enumerate(rs):
            b, d_, half = i // 4, (i % 4) // 2, i % 2
            yc, yl = unscan(r["yT"].T, d_)
            s5_l[d_, b][:, half * 128:(half + 1) * 128] = yl
            s5_c[d_, b][:, half * 128:(half + 1) * 128] = yc
        ims = []
        mask = np.triu(np.ones((64, 64), f32))
        for i in range(NCORE):
            b, d_, hp = i // 4, (i % 4) // 2, i % 2
            fs = slice(hp * 128, (hp + 1) * 128)
            fk = slice(d_ * 256 + hp * 128, d_ * 256 + (hp + 1) * 128)
            ims.append(dict(qT=A(scan_order(hq_c[b][:, fs], hq_l[b][:, fs], d_).T),
                            kT=A(scan_order(hk_c[b][:, fk], hk_l[b][:, fk], d_).T),
                            lT=A(scan_order(hl_c[b][:, fk], hl_l[b][:, fk], d_).T),
                            v=A(scan_order(hv_c[b][:, fs], hv_l[b][:, fs], d_)),
                            identb=np.eye(64, dtype=f32), mask=mask))
        rs = run(prog("hgrn", build_hgrn), ims)
        ho_l = np.zeros((2, 2, SEQ, 256), f32)
        ho_c = np.zeros((2, 2, CTX, 256), f32)
        for i, r in enumerate(rs):
            b, d_, hp = i // 4, (i % 4) // 2, i % 2
            yc, yl = unscan(r["oT"].T, d_)
            ho_l[d_, b][:, hp * 128:(hp + 1) * 128] = yl
            ho_c[d_, b][:, hp * 128:(hp + 1) * 128] = yc
        if _debug:
            DBG["mix_%d" % l] = dict(mla=(mla_l, mla_c), gqa=(gqa_l, gqa_c), s5=(s5_l, s5_c), ho=(ho_l, ho_c))
        pcm = np.zeros((128, 4), f32)
        pcm[:, 0] = s5_d[l][:128]
        pcm[:, 1] = s5_d[l][128:]
        pcm[:, 2] = np.tile(hgrn_o_norm[l], 2)
        ims = []
        for i in range(NCORE):
            parts = [(u_l, u_c), (s5_l[0], s5_c[0]), (s5_l[1], s5_c[1]), (mla_l, mla_c), (ho_l[0], ho_c[0]),
                     (ho_l[1], ho_c[1]), (hg_l, hg_c), (gqa_l, gqa_c)]
            bin_ = np.stack([core_tokens(pl, pc_, i).T.astype(f32) for (pl, pc_) in parts], axis=0)
            ims.append(dict(xT=XT[i], wg=A(W[:, 2400:]), wbr=A(np.asarray(w_branch[l]).reshape(1024, D)), wout=A(w_out[l]),
                            wglu=A(s5_w_glu[l]), gp=gp_for(norm_pre[l, 1], norm_post[l, 1]), modc=modc_for(i, l, 1),
                            pcols=pcm, blk=blk, bin=A(bin_)))
        XT = [r["outT"] for r in run(prog("merge", build_merge), ims)]
        if _debug:
            DBG["x2_%d" % l] = gather_fm(XT, D)
        XT = run_ffn(XT, l, 2, 1)
        if _debug:
            DBG["x3_%d" % l] = gather_fm(XT, D)
    lat, _ = gather_fm(XT, D)
    return lat.astype(np.float32)
```

```python
from contextlib import ExitStack
import numpy as np
import ml_dtypes
import concourse.bass as bass
import concourse.mybir as mybir
from concourse.bass_utils import run_bass_kernel_spmd

F32 = mybir.dt.float32
BF16 = mybir.dt.bfloat16
AF = mybir.ActivationFunctionType
ALU = mybir.AluOpType
AX = mybir.AxisListType
NPBF = ml_dtypes.bfloat16

D = 1024
DFF = 2816
NIN = 6496
EPS = 1e-6
NCORE = 8
SKIP_OWN = False
SEQ = 16384
CTX = 256
LTOK = 4096
CTOK = 64
NTOK = LTOK + CTOK
LFULL = SEQ + CTX


class Buf:
    __slots__ = ("w", "r", "sem", "cnt", "name")

    def __init__(self, name):
        self.w = None
        self.r = {}
        self.sem = None
        self.cnt = 0
        self.name = name


class Tile:
    def __init__(self, t, name):
        self.t = t
        self.b = Buf(name)

    def __getitem__(self, idx):
        return self.t[idx]


class Prog:
    def __init__(self, name):
        self.name = name
        self.nc = bass.Bass("TRN2", target_bir_lowering=False)
        self.es = ExitStack()
        nc = self.nc
        self.eng = dict(pe=nc.tensor, act=nc.scalar, dve=nc.vector, pool=nc.gpsimd, sp=nc.sync)
        self.sem = {}
        self.cnt = {}
        for e in ("pe", "act", "dve", "pool"):
            self.sem[e] = self.es.enter_context(nc.semaphore("s_" + e))
            self.cnt[e] = 0
        self.seen = {e: {} for e in self.eng}
        self.semobj = {("s_" + e): self.sem[e] for e in self.sem}
        self.out_toks = []
        self.nsem = 4
        self.dram = {}

    def din(self, name, shape, dt=F32):
        t = self.nc.dram_tensor(name, list(shape), dt, kind="ExternalInput").ap()
        self.dram[name] = t
        return t

    def dout(self, name, shape, dt=F32):
        t = self.nc.dram_tensor(name, list(shape), dt, kind="ExternalOutput").ap()
        self.dram[name] = t
        return t

    def sb(self, name, shape, dt=F32):
        return Tile(self.es.enter_context(self.nc.sbuf_tensor(name, list(shape), dt)), name)

    def ps(self, name, shape, dt=F32):
        return Tile(self.es.enter_context(self.nc.psum_tensor(name, list(shape), dt)), name)

    def _wait(self, e, toks, skip_own=False):
        need = {}
        for (s, v) in toks:
            if v > need.get(s, 0):
                need[s] = v
        own = "s_" + e
        for s, v in need.items():
            if s == own and (skip_own or e == "pe" or v > self.cnt[e]):
                continue
            if self.seen[e].get(s, 0) < v:
                self.eng[e].wait_ge(self.semobj[s], v)
                self.seen[e][s] = v

    @staticmethod
    def _bufs(xs):
        return [x.b if isinstance(x, Tile) else x for x in xs]

    def _deps(self, reads, writes):
        toks = []
        for b in reads:
            if b.w:
                toks.append(b.w)
        for b in writes:
            if b.w:
                toks.append(b.w)
            toks.extend(b.r.items())
        return toks

    def _mark(self, tok, reads, writes):
        for b in reads:
            if b.r.get(tok[0], 0) < tok[1]:
                b.r[tok[0]] = tok[1]
        for b in writes:
            b.w = tok
            b.r = {}

    def op(self, e, fn, reads=(), writes=(), inc=True):
        reads = self._bufs(reads)
        writes = self._bufs(writes)
        self._wait(e, self._deps(reads, writes), skip_own=SKIP_OWN)
        ins = fn()
        idx = self.cnt[e] + 1
        if inc:
            ins.then_inc(self.sem[e], 1)
            self.cnt[e] = idx
        self._mark(("s_" + e, idx), reads, writes)
        return ins

    def dma(self, out, in_, sbuf, reads=(), writes=(), q="sp", is_out=False, **kw):
        b = sbuf.b
        if b.sem is None:
            b.sem = self.es.enter_context(self.nc.semaphore("d_" + b.name))
            self.semobj["d_" + b.name] = b.sem
            self.nsem += 1
        reads = self._bufs(reads)
        writes = self._bufs(writes)
        self._wait(q, self._deps(reads, writes), skip_own=False)
        ins = self.eng[q].dma_start(out=out, in_=in_, **kw)
        b.cnt += 16
        ins.then_inc(b.sem, 16)
        tok = ("d_" + b.name, b.cnt)
        self._mark(tok, reads, writes)
        if is_out:
            self.out_toks.append(tok)
        return ins

    def load(self, tile, dst_ap, src_ap, q="sp", **kw):
        return self.dma(dst_ap, src_ap, tile, reads=(), writes=(tile,), q=q, **kw)

    def store(self, dst_ap, tile, src_ap, q="pool", **kw):
        return self.dma(dst_ap, src_ap, tile, reads=(tile,), writes=(), q=q, is_out=True, **kw)

    def finish(self):
        self._wait("sp", self.out_toks)
        self.es.close()
        return self.nc

    def mm(self, out, lhsT, rhs, start, stop, reads, writes, inc=None, **kw):
        if inc is None:
            inc = stop
        return self.op("pe", lambda: self.nc.tensor.matmul(out, lhsT=lhsT, rhs=rhs, start=start, stop=stop, **kw),
                       reads, writes, inc=inc)

    def actf(self, out, in_, func, reads, writes, e="act", **kw):
        return self.op("act", lambda: self.nc.scalar.activation(out=out, in_=in_, func=func, **kw), reads, writes)

    def tt(self, out, in0, in1, op, reads, writes, e="dve"):
        return self.op(e, lambda: self.eng[e].tensor_tensor(out=out, in0=in0, in1=in1, op=op), reads, writes)

    def ts(self, out, in0, s1, s2, op0, op1, reads, writes, e="dve"):
        if op1 is None:
            return self.op(e, lambda: self.eng[e].tensor_scalar(out=out, in0=in0, scalar1=s1, scalar2=None, op0=op0),
                           reads, writes)
        return self.op(e, lambda: self.eng[e].tensor_scalar(out=out, in0=in0, scalar1=s1, scalar2=s2, op0=op0, op1=op1),
                       reads, writes)

    def stt(self, out, in0, scalar, in1, op0, op1, reads, writes):
        return self.op("dve", lambda: self.nc.vector.scalar_tensor_tensor(out=out, in0=in0, scalar=scalar, in1=in1,
                                                                          op0=op0, op1=op1), reads, writes)

    def copy(self, out, in_, reads, writes, e="dve"):
        if e == "act":
            return self.op("act", lambda: self.nc.scalar.copy(out=out, in_=in_), reads, writes)
        return self.op(e, lambda: self.eng[e].tensor_copy(out=out, in_=in_), reads, writes)


def run(prog_nc, in_maps):
    res = run_bass_kernel_spmd(prog_nc, in_maps, core_ids=list(range(NCORE)))
    return res.results


def load_cast_weight(P, dst, dst_view_fn, src, rows_chunks, ncols, stage, piece=1408):
    k = 0
    engs = ("dve", "pool", "act")
    for c in range(rows_chunks):
        for c0 in range(0, ncols, piece):
            w = min(piece, ncols - c0)
            st = stage[k % len(stage)]
            P.load(st, st[:, 0:w], src[c * 128:(c + 1) * 128, c0:c0 + w])
            e = engs[k % 3]
            P.copy(dst_view_fn(c, c0, w), st[:, 0:w], [st], [dst], e=e)
            k += 1


def rms_rstd(P, x_chunks_fn, nchunk, T, ones_b, sq, ss_ps, rstd, inv_n, reads):
    for c in range(nchunk):
        P.actf(sq[:, c, 0:T], x_chunks_fn(c), AF.Square, reads, [sq])
    for c in range(nchunk):
        P.mm(ss_ps[:, 0:T], ones_b[:, :], sq[:, c, 0:T], c == 0, c == nchunk - 1, [ones_b, sq], [ss_ps])
    P.actf(rstd[:, 0:T], ss_ps[:, 0:T], AF.Sqrt, [ss_ps], [rstd], scale=inv_n, bias=P.eps_col[:, 0:1])
    P.op("dve", lambda: P.nc.vector.reciprocal(out=rstd[:, 0:T], in_=rstd[:, 0:T]), [rstd], [rstd])


def token_tiles(T=256):
    tiles = []
    for s in range(0, LTOK, T):
        tiles.append((s, T, 0))
    tiles.append((LTOK, CTOK, 1))
    return tiles


def build_ffn():
    P = Prog("ffn")
    T = 256
    xT = P.din("xT", [D, NTOK])
    w1 = P.din("w1", [D, DFF])
    w3 = P.din("w3", [D, DFF])
    w2 = P.din("w2", [DFF, D])
    gp = P.din("gp", [128, 8, 2])
    modc = P.din("modc", [128, 8, 6])
    outT = P.dout("outT", [D, NTOK])

    w1b = P.sb("w1b", [128, 8, DFF], BF16)
    w3b = P.sb("w3b", [128, 8, DFF], BF16)
    w2b = P.sb("w2b", [128, 22, D], BF16)
    stage = [P.sb("stg%d" % i, [128, 1408], F32) for i in range(2)]
    ones_b = P.sb("ones_b", [128, 128], BF16)
    P.eps_col = P.sb("eps_col", [128, 1], F32)
    gps = P.sb("gps", [128, 8, 2], F32)
    mods = P.sb("mods", [128, 8, 6], F32)
    cols = P.sb("cols", [128, 8, 6], F32)
    P.op("dve", lambda: P.nc.vector.memset(ones_b[:, :], 1.0), [], [ones_b])
    P.op("dve", lambda: P.nc.vector.memset(P.eps_col[:, :], EPS), [], [P.eps_col])
    P.load(gps, gps[:, :, :], gp)
    P.load(mods, mods[:, :, :], modc)
    for tt in range(2):
        o = 3 * tt
        P.ts(cols[:, :, o + 0], mods[:, :, o + 1], 1.0, None, ALU.add, None, [mods], [cols])
        P.tt(cols[:, :, o + 0], cols[:, :, o + 0], gps[:, :, 0], ALU.mult, [cols, gps], [cols])
        P.copy(cols[:, :, o + 1], mods[:, :, o + 0], [mods], [cols])
        P.stt(cols[:, :, o + 2], mods[:, :, o + 2], 0.5, gps[:, :, 1], ALU.mult, ALU.mult, [mods, gps], [cols])

    load_cast_weight(P, w1b, lambda c, c0, w: w1b[:, c, c0:c0 + w], w1, 8, DFF, stage)
    load_cast_weight(P, w3b, lambda c, c0, w: w3b[:, c, c0:c0 + w], w3, 8, DFF, stage)
    load_cast_weight(P, w2b, lambda c, c0, w: w2b[:, c, c0:c0 + w], w2, 22, D, stage, piece=1024)

    NB = 2
    xs = [P.sb("x%d" % i, [128, 8, T], F32) for i in range(NB)]
    sq = P.sb("sq", [128, 8, T], BF16)
    rstd = P.sb("rstd", [128, T], F32)
    tmp = [P.sb("tmp%d" % i, [128, T], F32) for i in range(2)]
    hb = P.sb("hb", [128, 8, T], BF16)
    gb = P.sb("gb", [128, 22, T], BF16)
    sl = [P.sb("sl%d" % i, [128, T], F32) for i in range(2)]
    ys = P.sb("ys", [128, 8, T], F32)
    ss_ps = P.ps("ss_ps", [128, 512], F32)
    pa = [P.ps("pa%d" % i, [128, 512], F32) for i in range(3)]
    pb = [P.ps("pb%d" % i, [128, 512], F32) for i in range(3)]

    tiles = token_tiles(T)
    xTv = xT.rearrange("(c p) n -> p c n", p=128)
    oTv = outT.rearrange("(c p) n -> p c n", p=128)
    for ti, (s0, tn, ty) in enumerate(tiles):
        x = xs[ti % NB]
        o = x
        co = 3 * ty
        P.load(x, x[:, :, 0:tn], xTv[:, :, s0:s0 + tn])
        rms_rstd(P, lambda c: x[:, c, 0:tn], 8, tn, ones_b, sq, ss_ps, rstd, 1.0 / D, [x])
        for c in range(8):
            t = tmp[c % 2]
            P.stt(t[:, 0:tn], x[:, c, 0:tn], cols[:, c, co + 0:co + 1], rstd[:, 0:tn], ALU.mult, ALU.mult,
                  [x, cols, rstd], [t])
            P.actf(hb[:, c, 0:tn], t[:, 0:tn], AF.Identity, [t, cols], [hb], bias=cols[:, c, co + 1:co + 2], scale=1.0)
        for j in range(22):
            p1 = pa[j % 3]
            p3 = pb[j % 3]
            for c in range(8):
                P.mm(p1[:, 0:tn], w1b[:, c, j * 128:(j + 1) * 128], hb[:, c, 0:tn], c == 0, c == 7, [w1b, hb], [p1])
            for c in range(8):
                P.mm(p3[:, 0:tn], w3b[:, c, j * 128:(j + 1) * 128], hb[:, c, 0:tn], c == 0, c == 7, [w3b, hb], [p3])
            s = sl[j % 2]
            P.actf(s[:, 0:tn], p1[:, 0:tn], AF.Silu, [p1], [s])
            P.tt(gb[:, j, 0:tn], s[:, 0:tn], p3[:, 0:tn], ALU.mult, [s, p3], [gb])
        for k in range(8):
            p = pa[k % 3]
            for j in range(22):
                P.mm(p[:, 0:tn], w2b[:, j, k * 128:(k + 1) * 128], gb[:, j, 0:tn], j == 0, j == 21, [w2b, gb], [p])
            P.copy(ys[:, k, 0:tn], p[:, 0:tn], [p], [ys], e="act")
        rms_rstd(P, lambda c: ys[:, c, 0:tn], 8, tn, ones_b, sq, ss_ps, rstd, 1.0 / D, [ys])
        for c in range(8):
            t = tmp[c % 2]
            P.stt(t[:, 0:tn], ys[:, c, 0:tn], cols[:, c, co + 2:co + 3], rstd[:, 0:tn], ALU.mult, ALU.mult,
                  [ys, cols, rstd], [t])
            P.tt(o[:, c, 0:tn], t[:, 0:tn], x[:, c, 0:tn], ALU.add, [t, x], [x], e="pool")
        P.store(oTv[:, :, s0:s0 + tn], o, o[:, :, 0:tn])
    return P.finish()


MODC = 2 * 9 * D // NCORE


def build_mod():
    P = Prog("mod")
    cT = P.din("cT", [128, 8, 3])
    W = P.din("W", [D, MODC])
    bias = P.din("bias", [3, MODC])
    out = P.dout("out", [3, MODC])
    cs = P.sb("cs", [128, 8, 3], F32)
    sc = P.sb("sc", [128, 8, 3], F32)
    Ws = P.sb("Ws", [128, 8, MODC], F32)
    bs = P.sb("bs", [3, MODC], F32)
    os_ = P.sb("os", [3, MODC], F32)
    pp = [P.ps("pp%d" % i, [128, 512], F32) for i in range(2)]
    P.load(cs, cs[:, :, :], cT)
    P.load(bs, bs[:, :], bias)
    for c in range(8):
        P.load(Ws, Ws[:, c, :], W[c * 128:(c + 1) * 128, :])
    P.actf(sc[:, :, :], cs[:, :, :], AF.Silu, [cs], [sc])
    for i, c0 in enumerate(range(0, MODC, 512)):
        w = min(512, MODC - c0)
        p = pp[i % 2]
        for c in range(8):
            P.mm(p[0:3, 0:w], sc[:, c, :], Ws[:, c, c0:c0 + w], c == 0, c == 7, [sc, Ws], [p])
        P.tt(os_[:, c0:c0 + w], p[0:3, 0:w], bs[:, c0:c0 + w], ALU.add, [p, bs], [os_])
    P.store(out, os_, os_[:, :])
    return P.finish()


WEXT = 2400 + 32 + 256 + 128
NPC = 20


def make_h(P, x, tn, cols, co, ones_b, sq, ss_ps, rstd, tmp, hb):
    rms_rstd(P, lambda c: x[:, c, 0:tn], 8, tn, ones_b, sq, ss_ps, rstd, 1.0 / D, [x])
    for c in range(8):
        t = tmp[c % 2]
        P.stt(t[:, 0:tn], x[:, c, 0:tn], cols[:, c, co + 0:co + 1], rstd[:, 0:tn], ALU.mult, ALU.mult,
              [x, cols, rstd], [t])
        P.actf(hb[:, c, 0:tn], t[:, 0:tn], AF.Identity, [t, cols], [hb], bias=cols[:, c, co + 1:co + 2], scale=1.0)


def mod_cols(P, cols, mods, gps, with_gate, gate_scale=1.0, gcol=1):
    for tt in range(2):
        o = 3 * tt
        P.ts(cols[:, :, o + 0], mods[:, :, o + 1], 1.0, None, ALU.add, None, [mods], [cols])
        P.tt(cols[:, :, o + 0], cols[:, :, o + 0], gps[:, :, 0], ALU.mult, [cols, gps], [cols])
        P.copy(cols[:, :, o + 1], mods[:, :, o + 0], [mods], [cols])
        if with_gate:
            P.stt(cols[:, :, o + 2], mods[:, :, o + 2], gate_scale, gps[:, :, gcol], ALU.mult, ALU.mult,
                  [mods, gps], [cols])


def build_proj():
    P = Prog("proj")
    T = 256
    xT = P.din("xT", [D, NTOK])
    wext = P.din("wext", [D, WEXT])
    gp = P.din("gp", [128, 8, 2])
    modc = P.din("modc", [128, 8, 6])
    pcols = P.din("pcols", [128, NPC])
    wuq = P.din("wuq", [192, 512])
    wukv = P.din("wukv", [128, 512])
    ropeb = P.din("ropeb", [32, 2, NTOK])
    roped = P.din("roped", [128, 2, NTOK])
    blk = P.din("blk", [128, 128])
    o_u = P.dout("o_u", [256, NTOK])
    o_mq = P.dout("o_mq", [4, 96, NTOK], BF16)
    o_mk = P.dout("o_mk", [4, 96, NTOK], BF16)
    o_mv = P.dout("o_mv", [NTOK, 256], BF16)
    o_hq = P.dout("o_hq", [256, NTOK])
    o_hk = P.dout("o_hk", [2, 256, NTOK])
    o_hl = P.dout("o_hl", [2, 256, NTOK])
    o_hv = P.dout("o_hv", [NTOK, 256], BF16)
    o_hg = P.dout("o_hg", [256, NTOK])
    o_dq = P.dout("o_dq", [256, NTOK], BF16)
    o_dk = P.dout("o_dk", [128, NTOK], BF16)
    o_dv = P.dout("o_dv", [NTOK, 128], BF16)

    wb = P.sb("wb", [128, 8, WEXT], BF16)
    stage = [P.sb("stg%d" % i, [128, 1408], F32) for i in range(2)]
    ones_b = P.sb("ones_b", [128, 128], BF16)
    blk_f = P.sb("blk_f", [128, 128], F32)
    blk_b = P.sb("blk_b", [128, 128], BF16)
    P.eps_col = P.sb("eps_col", [128, 1], F32)
    gps = P.sb("gps", [128, 8, 2], F32)
    mods = P.sb("mods", [128, 8, 6], F32)
    cols = P.sb("cols", [128, 8, 6], F32)
    pc = P.sb("pc", [128, NPC], F32)
    lbc = P.sb("lbc", [128, 4, 8], F32)
    wuq_f = P.sb("wuq_f", [128, 2, 512], F32)
    wuq_b = P.sb("wuq_b", [128, 2, 512], BF16)
    wukv_f = P.sb("wukv_f", [128, 512], F32)
    wukv_b = P.sb("wukv_b", [128, 512], BF16)
    P.op("dve", lambda: P.nc.vector.memset(ones_b[:, :], 1.0), [], [ones_b])
    P.op("dve", lambda: P.nc.vector.memset(P.eps_col[:, :], EPS), [], [P.eps_col])
    P.load(gps, gps[:, :, :], gp)
    P.load(mods, mods[:, :, :], modc)
    P.load(pc, pc[:, :], pcols)
    P.load(blk_f, blk_f[:, :], blk)
    P.copy(blk_b[:, :], blk_f[:, :], [blk_f], [blk_b])
    P.load(wuq_f, wuq_f[:, 0, :], wuq[0:128, :])
    P.load(wuq_f, wuq_f[0:64, 1, :], wuq[128:192, :])
    P.copy(wuq_b[:, 0, :], wuq_f[:, 0, :], [wuq_f], [wuq_b])
    P.copy(wuq_b[0:64, 1, :], wuq_f[0:64, 1, :], [wuq_f], [wuq_b])
    P.load(wukv_f, wukv_f[:, :], wukv)
    P.copy(wukv_b[:, :], wukv_f[:, :], [wukv_f], [wukv_b])
    mod_cols(P, cols, mods, gps, False)
    for d in range(2):
        for c in range(2):
            k = d * 2 + c
            r0 = pc[:, 7 + (d * 2 + 0) * 2 + c:7 + (d * 2 + 0) * 2 + c + 1]
            r1 = pc[:, 7 + (d * 2 + 1) * 2 + c:7 + (d * 2 + 1) * 2 + c + 1]
            L = lambda j: lbc[:, k, j:j + 1]
            P.tt(L(2), r0, r1, ALU.subtract, [pc], [lbc])
            P.actf(L(3), L(2), AF.Sigmoid, [lbc], [lbc])
            P.actf(L(4), L(2), AF.Sigmoid, [lbc], [lbc], scale=-1.0)
            P.tt(L(5), L(3), L(3), ALU.subtract, [lbc], [lbc])
            P.tt(L(6), L(3), L(4), ALU.add, [lbc], [lbc])
            P.tt(L(6), L(6), L(3), ALU.subtract, [lbc], [lbc])
            P.ts(L(5), L(5), 0.0, 1.0, ALU.max, ALU.min, [lbc], [lbc])
            P.ts(L(6), L(6), 0.0, 1.0, ALU.max, ALU.min, [lbc], [lbc])
            P.tt(L(5), L(5), pc[:, 15:16], ALU.mult, [lbc, pc], [lbc])
            P.stt(L(0), L(6), pc[:, 16:17], L(5), ALU.mult, ALU.add, [lbc, pc], [lbc])
            P.ts(L(1), L(0), -1.0, 1.0, ALU.mult, ALU.add, [lbc], [lbc])
    load_cast_weight(P, wb, lambda c, c0, w: wb[:, c, c0:c0 + w], wext, 8, WEXT, stage)

    xs = [P.sb("x%d" % i, [128, 8, T], F32) for i in range(2)]
    sq = P.sb("sq", [128, 8, T], BF16)
    rstd = P.sb("rstd", [128, T], F32)
    rs2 = P.sb("rs2", [128, T], F32)
    tmp = [P.sb("tmp%d" % i, [128, T], F32) for i in range(2)]
    t3 = [P.sb("t3_%d" % i, [128, T], F32) for i in range(2)]
    hb = P.sb("hb", [128, 8, T], BF16)
    rb = P.sb("rb", [32, 2, T], F32)
    rd = P.sb("rd", [128, 2, T], F32)
    cqn = P.sb("cqn", [128, 2, T], BF16)
    ckvn = P.sb("ckvn", [128, T], BF16)
    s_u = P.sb("s_u", [128, 2, T], F32)
    s_mqn = P.sb("s_mqn", [64, 4, T], BF16)
    s_mqr = P.sb("s_mqr", [32, 4, T], BF16)
    s_mkn = P.sb("s_mkn", [64, 4, T], BF16)
    s_mkr = P.sb("s_mkr", [32, T], BF16)
    s_mv = P.sb("s_mv", [128, 2, 256], BF16)
    s_hq = P.sb("s_hq", [128, 2, T], F32)
    s_hk = P.sb("s_hk", [128, 4, T], F32)
    s_hl = P.sb("s_hl", [128, 4, T], F32)
    s_hv = P.sb("s_hv", [128, 2, 256], BF16)
    s_hg = P.sb("s_hg", [128, 2, T], F32)
    s_dq = P.sb("s_dq", [128, 2, T], BF16)
    s_dk = P.sb("s_dk", [128, T], BF16)
    s_dv = P.sb("s_dv", [128, 2, 128], BF16)
    ss_ps = P.ps("ss_ps", [128, 512], F32)
    pq = [P.ps("pq%d" % i, [128, 512], F32) for i in range(6)]
    pi = [0]

    def nextp():
        pi[0] += 1
        return pq[pi[0] % 6]

    def proj(p, m, c0, ncol, tn):
        for c in range(8):
            P.mm(p[0:ncol, 0:tn], wb[:, c, c0:c0 + ncol], hb[:, c, 0:tn], c == 0, c == 7, [wb, hb], [p])

    def proj_tok(p, sub, c0, ncol, tn):
        n = min(128, tn - sub * 128)
        for c in range(8):
            P.mm(p[0:n, 0:ncol], hb[:, c, sub * 128:sub * 128 + n], wb[:, c, c0:c0 + ncol], c == 0, c == 7, [wb, hb], [p])
        return n

    def rstd_from(ps_, rows, tn, inv_n, out):
        P.actf(out[0:rows, 0:tn], ps_[0:rows, 0:tn], AF.Sqrt, [ps_], [out], scale=inv_n, bias=P.eps_col[0:rows, 0:1])
        P.op("dve", lambda: P.nc.vector.reciprocal(out=out[0:rows, 0:tn], in_=out[0:rows, 0:tn]), [out], [out])

    xTv = xT.rearrange("(c p) n -> p c n", p=128)
    for ti, (s0, tn, ty) in enumerate(token_tiles(T)):
        x = xs[ti % 2]
        co = 3 * ty
        nsub = (tn + 127) // 128
        P.load(x, x[:, :, 0:tn], xTv[:, :, s0:s0 + tn])
        P.load(rb, rb[:, :, 0:tn], ropeb[:, :, s0:s0 + tn])
        P.load(rd, rd[:, :, 0:tn], roped[:, :, s0:s0 + tn])
        make_h(P, x, tn, cols, co, ones_b, sq, ss_ps, rstd, tmp, hb)
        for c in range(2):
            p = nextp()
            proj(p, 128, c * 128, 128, tn)
            P.copy(s_u[:, c, 0:tn], p[:, 0:tn], [p], [s_u], e="act")
        P.store(o_u.rearrange("(c p) n -> p c n", p=128)[:, :, s0:s0 + tn], s_u, s_u[:, :, 0:tn])
        pA = nextp()
        proj(pA, 128, 256, 128, tn)
        pB = nextp()
        proj(pB, 64, 384, 64, tn)
        P.actf(sq[:, 0, 0:tn], pA[:, 0:tn], AF.Square, [pA], [sq])
        P.actf(sq[0:64, 1, 0:tn], pB[0:64, 0:tn], AF.Square, [pB], [sq])
        P.mm(ss_ps[:, 0:tn], ones_b[:, :], sq[:, 0, 0:tn], True, False, [ones_b, sq], [ss_ps])
        P.mm(ss_ps[:, 0:tn], ones_b[0:64, :], sq[0:64, 1, 0:tn], False, True, [ones_b, sq], [ss_ps])
        rstd_from(ss_ps, 128, tn, 1.0 / 192, rs2)
        P.stt(cqn[:, 0, 0:tn], pA[:, 0:tn], pc[:, 0:1], rs2[:, 0:tn], ALU.mult, ALU.mult, [pA, pc, rs2], [cqn])
        P.stt(cqn[0:64, 1, 0:tn], pB[0:64, 0:tn], pc[0:64, 1:2], rs2[0:64, 0:tn], ALU.mult, ALU.mult, [pB, pc, rs2], [cqn])
        for h in range(4):
            def qmm(p, rows, c0):
                P.mm(p[0:rows, 0:tn], wuq_b[:, 0, c0:c0 + rows], cqn[:, 0, 0:tn], True, False, [wuq_b, cqn], [p])
                P.mm(p[0:rows, 0:tn], wuq_b[0:64, 1, c0:c0 + rows], cqn[0:64, 1, 0:tn], False, True, [wuq_b, cqn], [p])
            pn = nextp()
            qmm(pn, 64, h * 128)
            P.copy(s_mqn[:, h, 0:tn], pn[0:64, 0:tn], [pn], [s_mqn], e="act")
            pr = nextp()
            qmm(pr, 32, h * 128 + 64)
            pp_ = nextp()
            qmm(pp_, 32, h * 128 + 96)
            ta, tb = t3[0], t3[1]
            P.tt(ta[0:32, 0:tn], pr[0:32, 0:tn], rb[:, 0, 0:tn], ALU.mult, [pr, rb], [ta])
            P.tt(tb[0:32, 0:tn], pp_[0:32, 0:tn], rb[:, 1, 0:tn], ALU.mult, [pp_, rb], [tb])
            P.tt(s_mqr[:, h, 0:tn], ta[0:32, 0:tn], tb[0:32, 0:tn], ALU.add, [ta, tb], [s_mqr])
        P.store(o_mq[:, 0:64, s0:s0 + tn].rearrange("h p n -> p h n"), s_mqn, s_mqn[:, :, 0:tn])
        P.store(o_mq[:, 64:96, s0:s0 + tn].rearrange("h p n -> p h n"), s_mqr, s_mqr[:, :, 0:tn])
        pK = nextp()
        proj(pK, 128, 448, 128, tn)
        P.actf(sq[:, 0, 0:tn], pK[:, 0:tn], AF.Square, [pK], [sq])
        P.mm(ss_ps[:, 0:tn], ones_b[:, :], sq[:, 0, 0:tn], True, True, [ones_b, sq], [ss_ps])
        rstd_from(ss_ps, 128, tn, 1.0 / 128, rs2)
        P.stt(ckvn[:, 0:tn], pK[:, 0:tn], pc[:, 2:3], rs2[:, 0:tn], ALU.mult, ALU.mult, [pK, pc, rs2], [ckvn])
        for h in range(4):
            pn = nextp()
            P.mm(pn[0:64, 0:tn], wukv_b[:, h * 64:(h + 1) * 64], ckvn[:, 0:tn], True, True, [wukv_b, ckvn], [pn])
            P.copy(s_mkn[:, h, 0:tn], pn[0:64, 0:tn], [pn], [s_mkn], e="act")
        P.store(o_mk[:, 0:64, s0:s0 + tn].rearrange("h p n -> p h n"), s_mkn, s_mkn[:, :, 0:tn])
        pr = nextp()
        proj(pr, 32, 576, 32, tn)
        pp_ = nextp()
        proj(pp_, 32, 2400, 32, tn)
        ta, tb = t3[0], t3[1]
        P.tt(ta[0:32, 0:tn], pr[0:32, 0:tn], rb[:, 0, 0:tn], ALU.mult, [pr, rb], [ta])
        P.tt(tb[0:32, 0:tn], pp_[0:32, 0:tn], rb[:, 1, 0:tn], ALU.mult, [pp_, rb], [tb])
        P.tt(s_mkr[:, 0:tn], ta[0:32, 0:tn], tb[0:32, 0:tn], ALU.add, [ta, tb], [s_mkr])
        for h in range(4):
            P.store(o_mk[h, 64:96, s0:s0 + tn], s_mkr, s_mkr[:, 0:tn])
        for sub in range(nsub):
            n = min(128, tn - sub * 128)
            pv = nextp()
            P.mm(pv[0:n, 0:256], ckvn[:, sub * 128:sub * 128 + n], wukv_b[:, 256:512], True, True, [wukv_b, ckvn], [pv])
            P.copy(s_mv[0:n, sub, :], pv[0:n, 0:256], [pv], [s_mv], e="act")
            P.store(o_mv[s0 + sub * 128:s0 + sub * 128 + n, :], s_mv, s_mv[0:n, sub, :])
        for c in range(2):
            p = nextp()
            proj(p, 128, 608 + c * 128, 128, tn)
            P.copy(s_hq[:, c, 0:tn], p[:, 0:tn], [p], [s_hq], e="act")
            p = nextp()
            proj(p, 128, 1632 + c * 128, 128, tn)
            P.copy(s_hg[:, c, 0:tn], p[:, 0:tn], [p], [s_hg], e="act")
            for d in range(2):
                k = d * 2 + c
                p = nextp()
                proj(p, 128, 1120 + d * 256 + c * 128, 128, tn)
                ta, tb = t3[0], t3[1]
                P.actf(ta[:, 0:tn], p[:, 0:tn], AF.Sigmoid, [p], [ta])
                P.ts(ta[:, 0:tn], ta[:, 0:tn], lbc[:, k, 1:2], lbc[:, k, 0:1], ALU.mult, ALU.add, [ta, lbc], [ta])
                P.ts(ta[:, 0:tn], ta[:, 0:tn], 1e-20, None, ALU.max, None, [ta], [ta])
                P.actf(s_hl[:, k, 0:tn], ta[:, 0:tn], AF.Ln, [ta], [s_hl])
                P.actf(tb[:, 0:tn], p[:, 0:tn], AF.Sigmoid, [p], [tb], scale=-1.0)
                P.ts(s_hk[:, k, 0:tn], tb[:, 0:tn], lbc[:, k, 1:2], None, ALU.mult, None, [tb, lbc], [s_hk])
        P.store(o_hq.rearrange("(c p) n -> p c n", p=128)[:, :, s0:s0 + tn], s_hq, s_hq[:, :, 0:tn])
        P.store(o_hg.rearrange("(c p) n -> p c n", p=128)[:, :, s0:s0 + tn], s_hg, s_hg[:, :, 0:tn])
        P.store(o_hk.rearrange("d (c p) n -> p (d c) n", p=128)[:, :, s0:s0 + tn], s_hk, s_hk[:, :, 0:tn])
        P.store(o_hl.rearrange("d (c p) n -> p (d c) n", p=128)[:, :, s0:s0 + tn], s_hl, s_hl[:, :, 0:tn])
        for sub in range(nsub):
            pv = nextp()
            n = proj_tok(pv, sub, 864, 256, tn)
            P.copy(s_hv[0:n, sub, :], pv[0:n, 0:256], [pv], [s_hv], e="act")
            P.store(o_hv[s0 + sub * 128:s0 + sub * 128 + n, :], s_hv, s_hv[0:n, sub, :])
        for (dst, dcol, c0, cp0, gcol) in ((s_dq, 0, 1888, 2432, 3), (s_dq, 1, 2016, 2560, 3), (s_dk, None, 2144, 2688, 5)):
            pz = nextp()
            proj(pz, 128, c0, 128, tn)
            pzp = nextp()
            proj(pzp, 128, cp0, 128, tn)
            P.actf(sq[:, 0, 0:tn], pz[:, 0:tn], AF.Square, [pz], [sq])
            P.mm(ss_ps[:, 0:tn], blk_b[:, :], sq[:, 0, 0:tn], True, True, [blk_b, sq], [ss_ps])
            rstd_from(ss_ps, 128, tn, 1.0 / 64, rs2)
            ta, tb = t3[0], t3[1]
            P.stt(ta[:, 0:tn], pz[:, 0:tn], pc[:, gcol:gcol + 1], rs2[:, 0:tn], ALU.mult, ALU.mult, [pz, pc, rs2], [ta])
            P.stt(tb[:, 0:tn], pzp[:, 0:tn], pc[:, gcol + 1:gcol + 2], rs2[:, 0:tn], ALU.mult, ALU.mult, [pzp, pc, rs2], [tb])
            P.tt(ta[:, 0:tn], ta[:, 0:tn], rd[:, 0, 0:tn], ALU.mult, [ta, rd], [ta])
            P.tt(tb[:, 0:tn], tb[:, 0:tn], rd[:, 1, 0:tn], ALU.mult, [tb, rd], [tb])
            dv_ = dst[:, dcol, 0:tn] if dcol is not None else dst[:, 0:tn]
            P.tt(dv_, ta[:, 0:tn], tb[:, 0:tn], ALU.add, [ta, tb], [dst])
        P.store(o_dq.rearrange("(c p) n -> p c n", p=128)[:, :, s0:s0 + tn], s_dq, s_dq[:, :, 0:tn])
        P.store(o_dk[:, s0:s0 + tn], s_dk, s_dk[:, 0:tn])
        for sub in range(nsub):
            pv = nextp()
            n = proj_tok(pv, sub, 2272, 128, tn)
            P.copy(s_dv[0:n, sub, :], pv[0:n, 0:128], [pv], [s_dv], e="act")
            P.store(o_dv[s0 + sub * 128:s0 + sub * 128 + n, :], s_dv, s_dv[0:n, sub, :])
    return P.finish()


def build_attn(d, scale, nq=SEQ, nk=LFULL):
    NBUF = 5
    P = Prog("attn%d_%d" % (d, nq))
    NKT = nk // 128
    QW = min(512, nq)
    NQT = nq // QW
    qT = P.din("qT", [d, nq], BF16)
    kT = P.din("kT", [d, nk], BF16)
    v = P.din("v", [nk, 64], BF16)
    sel = P.din("sel", [65, 64])
    oT = P.dout("oT", [64, nq])
    qs = P.sb("qs", [128, nq], BF16)
    ks = P.sb("ks", [128, nk], BF16)
    vs = P.sb("vs", [128, NKT, 65], BF16)
    sels = P.sb("sels", [65, 64], F32)
    ones_b = P.sb("ones_b", [128, 128], BF16)
    sqb = [P.sb("sqb%d" % i, [128, 512], BF16) for i in range(2)]
    mx = P.sb("mx", [128, 8], F32)
    pts = [P.sb("pt%d" % i, [128, 512], BF16) for i in range(NBUF)]
    osb = [P.sb("osb%d" % i, [65, 512], F32) for i in range(2)]
    rec = P.sb("rec", [64, 512], F32)
    ob = [P.sb("ob%d" % i, [64, 512], F32) for i in range(2)]
    pss = [P.ps("pss%d" % i, [128, 512], F32) for i in range(NBUF)]
    pos = [P.ps("pos%d" % i, [128, 512], F32) for i in range(2)]
    pden = P.ps("pden", [128, 512], F32)

    P.op("dve", lambda: P.nc.vector.memset(ones_b[:, :], 1.0), [], [ones_b])
    P.op("dve", lambda: P.nc.vector.memset(vs[:, :, :], 1.0), [], [vs])
    P.op("dve", lambda: P.nc.vector.memset(mx[:, :], 0.0), [], [mx])
    if d < 128:
        P.op("pool", lambda: P.nc.gpsimd.memset(qs[64:128, :], 0.0), [], [qs])
        P.op("pool", lambda: P.nc.gpsimd.memset(ks[64:128, :], 0.0), [], [ks])
    P.load(sels, sels[:, :], sel)
    for c0 in range(0, nq, 4096):
        w_ = min(4096, nq - c0)
        P.load(qs, qs[0:d, c0:c0 + w_], qT[:, c0:c0 + w_])
    for c0 in range(0, nk, 4160):
        w_ = min(4160, nk - c0)
        P.load(ks, ks[0:d, c0:c0 + w_], kT[:, c0:c0 + w_])
    vv = v.rearrange("(t p) e -> p t e", p=128)
    for t0 in range(0, NKT, 26):
        n_ = min(26, NKT - t0)
        P.load(vs, vs[:, t0:t0 + n_, 0:64], vv[:, t0:t0 + n_, :])
    i = 0
    for (src, n, col) in ((qs, nq, 0), (ks, nk, 1)):
        for c0 in range(0, n, 512):
            w = min(512, n - c0)
            sq = sqb[i % 2]
            pp = pss[i % NBUF]
            P.actf(sq[:, 0:w], src[:, c0:c0 + w], AF.Square, [src], [sq])
            P.mm(pp[:, 0:w], ones_b[:, :], sq[:, 0:w], True, True, [ones_b, sq], [pp])
            P.op("dve", lambda: P.nc.vector.tensor_reduce(out=mx[:, 2:3], in_=pp[:, 0:w], axis=AX.X, op=ALU.max),
                 [pp], [mx])
            P.tt(mx[:, col:col + 1], mx[:, col:col + 1], mx[:, 2:3], ALU.max, [mx], [mx])
            i += 1
    P.tt(mx[:, 3:4], mx[:, 0:1], mx[:, 1:2], ALU.mult, [mx], [mx])
    P.actf(mx[:, 4:5], mx[:, 3:4], AF.Sqrt, [mx], [mx], scale=scale * scale)
    P.ts(mx[:, 5:6], mx[:, 4:5], -1.0, None, ALU.mult, None, [mx], [mx])
    steps = [(qt, kt) for qt in range(NQT) for kt in range(NKT)]
    n = len(steps)
    LA = 3
    deferred = {}

    def epilogue_a(qt):
        P.copy(osb[qt % 2][:, 0:QW], pos[qt % 2][0:65, 0:QW], [pos[qt % 2]], [osb[qt % 2]])

    def epilogue_b(qt):
        o_s = osb[qt % 2]
        o_b = ob[qt % 2]
        qsl = slice(qt * QW, (qt + 1) * QW)
        P.mm(pden[0:64, 0:QW], sels[:, :], o_s[:, 0:QW], True, True, [sels, o_s], [pden])
        P.op("dve", lambda: P.nc.vector.reciprocal(out=rec[:, 0:QW], in_=pden[0:64, 0:QW]), [pden], [rec])
        P.tt(o_b[:, 0:QW], o_s[0:64, 0:QW], rec[:, 0:QW], ALU.mult, [o_s, rec], [o_b])
        P.store(oT[:, qsl], o_b, o_b[:, 0:QW])

    for i in range(n + LA + 4):
        if i < n:
            qt, kt = steps[i]
            ps_ = pss[i % NBUF]
            pt = pts[i % NBUF]
            P.mm(ps_[:, 0:QW], ks[:, kt * 128:(kt + 1) * 128], qs[:, qt * QW:(qt + 1) * QW], True, True, [ks, qs], [ps_])
            P.actf(pt[:, 0:QW], ps_[:, 0:QW], AF.Exp, [ps_, mx], [pt], scale=scale, bias=mx[:, 5:6])
        j = i - LA
        if 0 <= j < n:
            qt, kt = steps[j]
            po = pos[qt % 2]
            P.mm(po[0:65, 0:QW], vs[:, kt, :], pts[j % NBUF][:, 0:QW], kt == 0, kt == NKT - 1, [vs, pts[j % NBUF]], [po])
            if kt == NKT - 1:
                epilogue_a(qt)
                deferred[i + 3] = qt
        if i in deferred:
            epilogue_b(deferred.pop(i))
    assert not deferred
    return P.finish()


S5C = 512


def build_s5():
    P = Prog("s5")
    TC = S5C
    uT = P.din("uT", [128, LFULL])
    prm = P.din("prm", [128, 4, 3])
    bri = P.din("bri", [128, 4, 2, 16])
    cblk = P.din("cblk", [128, 4, 2, 32])
    ident = P.din("ident", [128, 128])
    yT = P.dout("yT", [128, LFULL])

    pr = P.sb("pr", [128, 4, 3], F32)
    br = P.sb("br", [128, 4, 2, 16], F32)
    cb = P.sb("cb", [128, 4, 2, 32], F32)
    cbb = P.sb("cbb", [128, 4, 2, 32], BF16)
    idf = P.sb("idf", [128, 128], F32)
    w = P.sb("w", [128, 24, 4], F32)
    wblk = P.sb("wblk", [128, 4, 2, 32], F32)
    wT = P.sb("wT", [32, 4, 2, 128], BF16)
    Ec = P.sb("Ec", [128, 4, TC], F32)
    Es = P.sb("Es", [128, 4, TC], F32)
    rf = P.sb("rf", [128, 4, TC], F32)
    tsc = [P.sb("tsc%d" % i, [128, TC], F32) for i in range(4)]
    zero = P.sb("zero", [128, 1], F32)
    uf = [P.sb("uf%d" % i, [32, 4, TC], F32) for i in range(2)]
    ub = [P.sb("ub%d" % i, [32, 4, TC], BF16) for i in range(2)]
    bp = [P.sb("bp%d" % i, [128, 4, TC], F32) for i in range(2)]
    gg = [P.sb("gg%d" % i, [128, 4, TC], F32) for i in range(2)]
    xx = [P.sb("xx%d" % i, [128, 4, TC], F32) for i in range(2)]
    xb = [P.sb("xb%d" % i, [128, 4, TC], BF16) for i in range(2)]
    ysb = [P.sb("ysb%d" % i, [32, 4, TC], F32) for i in range(2)]
    pbu = [P.ps("pbu%d" % i, [128, 512], F32) for i in range(4)]
    py = [P.ps("py%d" % i, [128, 512], F32) for i in range(2)]
    ptr = P.ps("ptr", [128, 512], F32)

    P.load(pr, pr[:, :, :], prm)
    P.load(br, br[:, :, :, :], bri)
    P.load(cb, cb[:, :, :, :], cblk)
    P.load(idf, idf[:, :], ident)
    P.op("dve", lambda: P.nc.vector.memset(zero[:, :], 0.0), [], [zero])
    P.op("dve", lambda: P.nc.vector.memset(wblk[:, :, :, :], 0.0), [], [wblk])
    P.copy(cbb[:, :, 0, :], cb[:, :, 0, :], [cb], [cbb])
    P.ts(cbb[:, :, 1, :], cb[:, :, 1, :], -1.0, None, ALU.mult, None, [cb], [cbb])
    W = lambda k: w[:, k, :]
    rw = [w]
    P.ts(W(0), pr[:, :, 0], -1e-4, None, ALU.min, None, [pr], rw)
    P.actf(W(1), pr[:, :, 2], AF.Exp, [pr], rw)
    P.tt(W(2), W(0), W(1), ALU.mult, rw, rw)
    P.actf(W(3), W(2), AF.Exp, rw, rw)
    P.tt(W(4), pr[:, :, 1], W(1), ALU.mult, [pr] + rw, rw)
    P.actf(W(6), W(4), AF.Sin, rw, rw, scale=1.0 / 32)
    P.ts(W(7), W(4), 1.0 / 32, 0.5 * np.pi, ALU.mult, ALU.add, rw, rw)
    P.actf(W(5), W(7), AF.Sin, rw, rw)
    for _ in range(5):
        P.tt(W(7), W(5), W(5), ALU.mult, rw, rw)
        P.tt(W(8), W(6), W(6), ALU.mult, rw, rw)
        P.tt(W(9), W(5), W(6), ALU.mult, rw, rw)
        P.tt(W(5), W(7), W(8), ALU.subtract, rw, rw)
        P.ts(W(6), W(9), 2.0, None, ALU.mult, None, rw, rw)
    P.tt(W(10), W(3), W(5), ALU.mult, rw, rw)
    P.tt(W(11), W(3), W(6), ALU.mult, rw, rw)
    P.tt(W(12), W(0), W(0), ALU.mult, rw, rw)
    P.tt(W(13), pr[:, :, 1], pr[:, :, 1], ALU.mult, [pr], rw)
    P.tt(W(12), W(12), W(13), ALU.add, rw, rw)
    P.op("dve", lambda: P.nc.vector.reciprocal(out=W(12), in_=W(12)), rw, rw)
    P.ts(W(13), W(10), -1.0, None, ALU.add, None, rw, rw)
    P.tt(W(14), W(13), W(0), ALU.mult, rw, rw)
    P.tt(W(15), W(11), pr[:, :, 1], ALU.mult, [pr] + rw, rw)
    P.tt(W(14), W(14), W(15), ALU.add, rw, rw)
    P.tt(W(14), W(14), W(12), ALU.mult, rw, rw)
    P.tt(W(15), W(11), W(0), ALU.mult, rw, rw)
    P.tt(W(16), W(13), pr[:, :, 1], ALU.mult, [pr] + rw, rw)
    P.tt(W(15), W(15), W(16), ALU.subtract, rw, rw)
    P.tt(W(15), W(15), W(12), ALU.mult, rw, rw)
    P.ts(W(17), W(15), -1.0, None, ALU.mult, None, rw, rw)
    for ct in range(4):
        fre = w[:, 14, ct:ct + 1]
        fim = w[:, 15, ct:ct + 1]
        nfim = w[:, 17, ct:ct + 1]
        for half in range(2):
            rows = slice(half * 64, half * 64 + 64)
            cs_ = slice(half * 16, half * 16 + 16)
            P.ts(wblk[rows, ct, 0, cs_], br[rows, ct, 1, :], nfim[rows], None, ALU.mult, None, [br] + rw, [wblk])
            P.stt(wblk[rows, ct, 0, cs_], br[rows, ct, 0, :], fre[rows], wblk[rows, ct, 0, cs_], ALU.mult, ALU.add,
                  [br, wblk] + rw, [wblk])
            P.ts(wblk[rows, ct, 1, cs_], br[rows, ct, 0, :], fim[rows], None, ALU.mult, None, [br] + rw, [wblk])
            P.stt(wblk[rows, ct, 1, cs_], br[rows, ct, 1, :], fre[rows], wblk[rows, ct, 1, cs_], ALU.mult, ALU.add,
                  [br, wblk] + rw, [wblk])
        for ri in range(2):
            P.op("pe", lambda: P.nc.tensor.transpose(out=ptr[0:32, 0:128], in_=wblk[:, ct, ri, :], identity=idf[:, :]),
                 [wblk, idf], [ptr])
            P.copy(wT[:, ct, ri, :], ptr[0:32, 0:128], [ptr], [wT])
        P.copy(Ec[:, ct, 0:1], w[:, 5, ct:ct + 1], rw, [Ec])
        P.copy(Es[:, ct, 0:1], w[:, 6, ct:ct + 1], rw, [Es])
        k = 1
        while k < TC:
            ck = Ec[:, ct, k - 1:k]
            sk = Es[:, ct, k - 1:k]
            t0_, t1_ = tsc[0], tsc[1]
            P.ts(t0_[:, 0:k], Es[:, ct, 0:k], sk, None, ALU.mult, None, [Es], [t0_])
            P.ts(t1_[:, 0:k], Es[:, ct, 0:k], ck, None, ALU.mult, None, [Es, Ec], [t1_])
            P.stt(Ec[:, ct, k:2 * k], Ec[:, ct, 0:k], ck, t0_[:, 0:k], ALU.mult, ALU.subtract, [Ec, t0_], [Ec])
            P.stt(Es[:, ct, k:2 * k], Ec[:, ct, 0:k], sk, t1_[:, 0:k], ALU.mult, ALU.add, [Ec, Es, t1_], [Es])
            k *= 2
        P.ts(rf[:, ct, :], Ec[:, ct, :], 0.0, w[:, 3, ct:ct + 1], ALU.mult, ALU.add, [Ec] + rw, [rf])
    uv = uT.rearrange("(ct p) n -> p ct n", p=32)
    yv = yT.rearrange("(ct p) n -> p ct n", p=32)
    chunks = [(c0, min(TC, LFULL - c0)) for c0 in range(0, LFULL, TC)]
    prev = None
    for ci, (c0, tn) in enumerate(chunks):
        u_f = uf[ci % 2]
        u_b = ub[ci % 2]
        y_s = ysb[ci % 2]
        P.load(u_f, u_f[:, :, 0:tn], uv[:, :, c0:c0 + tn])
        P.copy(u_b[:, :, 0:tn], u_f[:, :, 0:tn], [u_f], [u_b], e="pool")
        for ct in range(4):
            pre = pbu[(2 * ct) % 4]
            pim = pbu[(2 * ct + 1) % 4]
            P.mm(pre[:, 0:tn], wT[:, ct, 0, :], u_b[:, ct, 0:tn], True, True, [wT, u_b], [pre])
            P.mm(pim[:, 0:tn], wT[:, ct, 1, :], u_b[:, ct, 0:tn], True, True, [wT, u_b], [pim])
            c_ = Ec[:, ct, 0:tn]
            s_ = Es[:, ct, 0:tn]
            t0_, t1_, t2_, t3_ = tsc
            P.tt(t0_[:, 0:tn], pre[:, 0:tn], c_, ALU.mult, [pre, Ec], [t0_])
            P.tt(t1_[:, 0:tn], pim[:, 0:tn], s_, ALU.mult, [pim, Es], [t1_])
            P.tt(bp[0][:, ct, 0:tn], t0_[:, 0:tn], t1_[:, 0:tn], ALU.add, [t0_, t1_], [bp[0]])
            P.tt(t2_[:, 0:tn], pim[:, 0:tn], c_, ALU.mult, [pim, Ec], [t2_])
            P.tt(t3_[:, 0:tn], pre[:, 0:tn], s_, ALU.mult, [pre, Es], [t3_])
            P.tt(bp[1][:, ct, 0:tn], t2_[:, 0:tn], t3_[:, 0:tn], ALU.subtract, [t2_, t3_], [bp[1]])
        for ct in range(4):
            for ri in range(2):
                init = zero[:, 0:1] if prev is None else xx[ri][:, ct, prev - 1:prev]
                P.op("dve", lambda: P.nc.vector.tensor_tensor_scan(
                    out=gg[ri][:, ct, 0:tn], data0=rf[:, ct, 0:tn], data1=bp[ri][:, ct, 0:tn], initial=init,
                    op0=ALU.mult, op1=ALU.add), [rf, bp[ri], xx[ri], zero], [gg[ri]])
        for ct in range(4):
            c_ = Ec[:, ct, 0:tn]
            s_ = Es[:, ct, 0:tn]
            t0_, t1_, t2_, t3_ = tsc
            P.tt(t0_[:, 0:tn], gg[0][:, ct, 0:tn], c_, ALU.mult, [gg[0], Ec], [t0_])
            P.tt(t1_[:, 0:tn], gg[1][:, ct, 0:tn], s_, ALU.mult, [gg[1], Es], [t1_])
            P.tt(xx[0][:, ct, 0:tn], t0_[:, 0:tn], t1_[:, 0:tn], ALU.subtract, [t0_, t1_], [xx[0]])
            P.tt(t2_[:, 0:tn], gg[0][:, ct, 0:tn], s_, ALU.mult, [gg[0], Es], [t2_])
            P.tt(t3_[:, 0:tn], gg[1][:, ct, 0:tn], c_, ALU.mult, [gg[1], Ec], [t3_])
            P.tt(xx[1][:, ct, 0:tn], t2_[:, 0:tn], t3_[:, 0:tn], ALU.add, [t2_, t3_], [xx[1]])
            P.copy(xb[0][:, ct, 0:tn], xx[0][:, ct, 0:tn], [xx[0]], [xb[0]], e="act")
            P.copy(xb[1][:, ct, 0:tn], xx[1][:, ct, 0:tn], [xx[1]], [xb[1]], e="act")
            pp = py[ct % 2]
            P.mm(pp[0:32, 0:tn], cbb[:, ct, 0, :], xb[0][:, ct, 0:tn], True, False, [cbb, xb[0]], [pp])
            P.mm(pp[0:32, 0:tn], cbb[:, ct, 1, :], xb[1][:, ct, 0:tn], False, True, [cbb, xb[1]], [pp])
            P.copy(y_s[:, ct, 0:tn], pp[0:32, 0:tn], [pp], [y_s], e="act")
        P.store(yv[:, :, c0:c0 + tn], y_s, y_s[:, :, 0:tn])
        prev = tn
    return P.finish()


def build_hgrn():
    P = Prog("hgrn")
    SP = 512
    qT = P.din("qT", [128, LFULL])
    kT = P.din("kT", [128, LFULL])
    lT = P.din("lT", [128, LFULL])
    v = P.din("v", [LFULL, 128], BF16)
    identb = P.din("identb", [64, 64])
    mask = P.din("mask", [64, 64])
    oT = P.dout("oT", [128, LFULL])

    idf = P.sb("idf", [64, 64], F32)
    idb = P.sb("idb", [64, 64], BF16)
    mk = P.sb("mk", [64, 128], F32)
    ones = P.sb("ones", [64, 2 * SP], F32)
    zero = P.sb("zero", [64, 1], F32)
    S = P.sb("S", [64, 2, 64], F32)
    Sb = P.sb("Sb", [64, 2, 64], BF16)
    Stmp = P.sb("Stmp", [64, 2, 64], F32)
    Mm = P.sb("Mm", [64, 2, 8], F32)
    em = P.sb("em", [64, 2, 8], F32)
    el = P.sb("el", [64, 2, 8], F32)
    attc = P.sb("attc", [64, 128], F32)
    qs = [P.sb("qs%d" % i, [64, 2, SP], F32) for i in range(2)]
    ks = [P.sb("ks%d" % i, [64, 2, SP], F32) for i in range(2)]
    ls = [P.sb("ls%d" % i, [64, 2, SP], F32) for i in range(2)]
    vs = [P.sb("vs%d" % i, [64, 8, 128], BF16) for i in range(2)]
    G = P.sb("G", [64, 2, SP], F32)
    Gc = P.sb("Gc", [64, 2, SP], F32)
    e1 = P.sb("e1", [64, 2, SP], F32)
    e2 = P.sb("e2", [64, 2, SP], F32)
    qb = P.sb("qb", [64, 2, SP], BF16)
    kb = P.sb("kb", [64, 2, SP], BF16)
    ktok = [P.sb("ktok%d" % i, [64, 128], BF16) for i in range(2)]
    attb = [P.sb("attb%d" % i, [64, 128], BF16) for i in range(2)]
    osb = [P.sb("osb%d" % i, [64, 2, SP], F32) for i in range(2)]
    pkt = [P.ps("pkt%d" % i, [64, 1024], BF16) for i in range(1)]
    patt = [P.ps("patt%d" % i, [64, 512], F32) for i in range(2)]
    po = [P.ps("po%d" % i, [64, 512], F32) for i in range(2)]
    pS = P.ps("pS", [64, 512], F32)

    P.load(idf, idf[:, :], identb)
    P.copy(idb[:, :], idf[:, :], [idf], [idb])
    P.load(mk, mk[:, 0:64], mask)
    P.load(mk, mk[:, 64:128], mask)
    P.op("dve", lambda: P.nc.vector.memset(ones[:, :], 1.0), [], [ones])
    P.op("dve", lambda: P.nc.vector.memset(zero[:, :], 0.0), [], [zero])
    P.op("dve", lambda: P.nc.vector.memset(S[:, :, :], 0.0), [], [S])
    P.op("dve", lambda: P.nc.vector.memset(Sb[:, :, :], 0.0), [], [Sb])
    qv = qT.rearrange("(h p) n -> p h n", p=64)
    kv = kT.rearrange("(h p) n -> p h n", p=64)
    lv = lT.rearrange("(h p) n -> p h n", p=64)
    ov = oT.rearrange("(h p) n -> p h n", p=64)
    vv = v.rearrange("(c p) e -> p c e", p=64)
    spans = [(c0, min(SP, LFULL - c0)) for c0 in range(0, LFULL, SP)]
    ch = 0
    for si, (c0, tn) in enumerate(spans):
        q_, k_, l_, v_ = qs[si % 2], ks[si % 2], ls[si % 2], vs[si % 2]
        o_ = osb[si % 2]
        nch = tn // 64
        P.load(q_, q_[:, :, 0:tn], qv[:, :, c0:c0 + tn])
        P.load(k_, k_[:, :, 0:tn], kv[:, :, c0:c0 + tn])
        P.load(l_, l_[:, :, 0:tn], lv[:, :, c0:c0 + tn])
        P.load(v_, v_[:, 0:nch, :], vv[:, c0 // 64:c0 // 64 + nch, :])
        for h in range(2):
            P.op("dve", lambda: P.nc.vector.tensor_tensor_scan(
                out=G[:, h, 0:tn], data0=ones[:, 0:tn], data1=l_[:, h, 0:tn], initial=zero[:, 0:1],
                op0=ALU.mult, op1=ALU.add), [ones, l_, zero], [G])
        for j in range(nch):
            for h in range(2):
                mid = j * 64 + 31
                P.ts(Gc[:, h, j * 64:(j + 1) * 64], G[:, h, j * 64:(j + 1) * 64], G[:, h, mid:mid + 1], None,
                     ALU.subtract, None, [G], [Gc])
        G4 = G[:, :, 0:nch * 64].rearrange("p h (j t) -> p h j t", t=64)
        P.copy(Mm[:, :, 0:1], G4[:, :, 0:1, 31], [G], [Mm])
        if nch > 1:
            P.tt(Mm[:, :, 1:nch], G4[:, :, 1:nch, 31], G4[:, :, 0:nch - 1, 63], ALU.subtract, [G], [Mm])
        P.actf(e1[:, :, 0:tn], Gc[:, :, 0:tn], AF.Exp, [Gc], [e1])
        P.actf(e2[:, :, 0:tn], Gc[:, :, 0:tn], AF.Exp, [Gc], [e2], scale=-1.0)
        P.actf(em[:, :, 0:nch], Mm[:, :, 0:nch], AF.Exp, [Mm], [em])
        e14 = e1[:, :, 0:nch * 64].rearrange("p h (j t) -> p h j t", t=64)
        P.tt(el[:, :, 0:nch], em[:, :, 0:nch], e14[:, :, :, 63], ALU.mult, [em, e1], [el])
        P.tt(qb[:, :, 0:tn], q_[:, :, 0:tn], e1[:, :, 0:tn], ALU.mult, [q_, e1], [qb])
        P.tt(kb[:, :, 0:tn], k_[:, :, 0:tn], e2[:, :, 0:tn], ALU.mult, [k_, e2], [kb])
        for j in range(nch):
            cs_ = slice(j * 64, (j + 1) * 64)
            kt_ = ktok[ch % 2]
            ab = attb[ch % 2]
            pa = patt[ch % 2]
            pp = po[ch % 2]
            pk = pkt[0]
            for h in range(2):
                P.op("pe", lambda: P.nc.tensor.transpose(out=pk[0:64, h * 64:(h + 1) * 64], in_=kb[:, h, cs_],
                                                         identity=idb[:, :]), [kb, idb], [pk])
            P.copy(kt_[:, :], pk[0:64, 0:128], [pk], [kt_], e="act")
            for h in range(2):
                P.mm(pa[0:64, h * 64:(h + 1) * 64], kb[:, h, cs_], qb[:, h, cs_], True, True, [kb, qb], [pa])
            P.ts(attc[:, :], pa[0:64, 0:128], 3.0e38, -3.0e38, ALU.min, ALU.max, [pa], [attc])
            P.tt(ab[:, :], attc[:, :], mk[:, :], ALU.mult, [attc, mk], [ab])
            for h in range(2):
                P.ts(Sb[:, h, :], S[:, h, :], em[:, h, j:j + 1], None, ALU.mult, None, [S, em], [Sb])
            for h in range(2):
                P.mm(pp[0:64, h * 64:(h + 1) * 64], v_[:, j, h * 64:(h + 1) * 64], ab[:, h * 64:(h + 1) * 64],
                     True, False, [v_, ab], [pp])
                P.mm(pp[0:64, h * 64:(h + 1) * 64], Sb[:, h, :], qb[:, h, cs_], False, True, [Sb, qb], [pp])
            P.copy(o_[:, :, cs_], pp[0:64, 0:128].rearrange("p (h t) -> p h t", h=2), [pp], [o_], e="act")
            for h in range(2):
                P.mm(pS[0:64, h * 64:(h + 1) * 64], kt_[:, h * 64:(h + 1) * 64], v_[:, j, h * 64:(h + 1) * 64],
                     True, True, [kt_, v_], [pS])
            for h in range(2):
                P.ts(Stmp[:, h, :], pS[0:64, h * 64:(h + 1) * 64], e1[:, h, j * 64 + 63:j * 64 + 64], None, ALU.mult, None,
                     [pS, e1], [Stmp])
                P.stt(S[:, h, :], S[:, h, :], el[:, h, j:j + 1], Stmp[:, h, :], ALU.mult, ALU.add, [S, el, Stmp], [S])
            ch += 1
        P.store(ov[:, :, c0:c0 + tn], o_, o_[:, :, 0:tn])
    return P.finish()


def build_merge():
    P = Prog("merge")
    T = 256
    xT = P.din("xT", [D, NTOK])
    wg = P.din("wg", [D, 4096])
    wbr = P.din("wbr", [1024, D])
    wout = P.din("wout", [D, D])
    wglu = P.din("wglu", [256, 256])
    gp = P.din("gp", [128, 8, 2])
    modc = P.din("modc", [128, 8, 6])
    pcols = P.din("pcols", [128, 4])
    blk = P.din("blk", [128, 128])
    bin_ = P.din("bin", [8, 256, NTOK])
    outT = P.dout("outT", [D, NTOK])

    wgb = P.sb("wgb", [128, 8, 4096], BF16)
    wbrb = P.sb("wbrb", [128, 8, D], BF16)
    woutb = P.sb("woutb", [128, 8, D], BF16)
    wglub = P.sb("wglub", [128, 2, 256], BF16)
    stage = [P.sb("stg%d" % i, [128, 1024], F32) for i in range(2)]
    ones_b = P.sb("ones_b", [128, 128], BF16)
    blk_f = P.sb("blk_f", [128, 128], F32)
    blk_b = P.sb("blk_b", [128, 128], BF16)
    P.eps_col = P.sb("eps_col", [128, 1], F32)
    gps = P.sb("gps", [128, 8, 2], F32)
    mods = P.sb("mods", [128, 8, 6], F32)
    cols = P.sb("cols", [128, 8, 6], F32)
    pc = P.sb("pc", [128, 4], F32)
    P.op("dve", lambda: P.nc.vector.memset(ones_b[:, :], 1.0), [], [ones_b])
    P.op("dve", lambda: P.nc.vector.memset(P.eps_col[:, :], EPS), [], [P.eps_col])
    P.load(gps, gps[:, :, :], gp)
    P.load(mods, mods[:, :, :], modc)
    P.load(pc, pc[:, :], pcols)
    P.load(blk_f, blk_f[:, :], blk)
    P.copy(blk_b[:, :], blk_f[:, :], [blk_f], [blk_b])
    mod_cols(P, cols, mods, gps, True, 1.0, 1)
    load_cast_weight(P, wgb, lambda c, c0, w: wgb[:, c, c0:c0 + w], wg, 8, 4096, stage, piece=1024)
    load_cast_weight(P, wbrb, lambda c, c0, w: wbrb[:, c, c0:c0 + w], wbr, 8, D, stage, piece=1024)
    load_cast_weight(P, woutb, lambda c, c0, w: woutb[:, c, c0:c0 + w], wout, 8, D, stage, piece=1024)
    load_cast_weight(P, wglub, lambda c, c0, w: wglub[:, c, c0:c0 + w], wglu, 2, 256, stage, piece=256)

    xs = [P.sb("x%d" % i, [128, 8, T], F32) for i in range(2)]
    bi = [P.sb("bi%d" % i, [128, 16, T], F32) for i in range(2)]
    sq = P.sb("sq", [128, 8, T], BF16)
    rstd = P.sb("rstd", [128, T], F32)
    rs2 = P.sb("rs2", [128, T], F32)
    tmp = [P.sb("tmp%d" % i, [128, T], F32) for i in range(2)]
    t3 = [P.sb("t3_%d" % i, [128, T], F32) for i in range(3)]
    hb = P.sb("hb", [128, 8, T], BF16)
    gf = P.sb("gf", [128, 2, T], F32)
    gbf = P.sb("gbf", [128, 2, T], BF16)
    yb = P.sb("yb", [128, 8, T], BF16)
    sg = [P.sb("sg%d" % i, [128, T], F32) for i in range(2)]
    acc = P.sb("acc", [128, T], F32)
    mb = P.sb("mb", [128, 8, T], BF16)
    ys = P.sb("ys", [128, 8, T], F32)
    ss_ps = P.ps("ss_ps", [128, 512], F32)
    pg = [P.ps("pg%d" % i, [128, 512], F32) for i in range(3)]
    pb = [P.ps("pb%d" % i, [128, 512], F32) for i in range(3)]

    xTv = xT.rearrange("(c p) n -> p c n", p=128)
    oTv = outT.rearrange("(c p) n -> p c n", p=128)
    bv = bin_.rearrange("k (c p) n -> p (k c) n", p=128)
    for ti, (s0, tn, ty) in enumerate(token_tiles(T)):
        x = xs[ti % 2]
        b_ = bi[ti % 2]
        co = 3 * ty
        P.load(x, x[:, :, 0:tn], xTv[:, :, s0:s0 + tn])
        P.load(b_, b_[:, 0:8, 0:tn], bv[:, 0:8, s0:s0 + tn])
        P.load(b_, b_[:, 8:16, 0:tn], bv[:, 8:16, s0:s0 + tn])
        make_h(P, x, tn, cols, co, ones_b, sq, ss_ps, rstd, tmp, hb)
        B = lambda k, c: b_[:, k * 2 + c, 0:tn]
        for c in range(2):
            t = t3[c]
            P.stt(t[:, 0:tn], B(0, c), pc[:, c:c + 1], B(1, c), ALU.mult, ALU.add, [b_, pc], [t])
            P.tt(t[:, 0:tn], t[:, 0:tn], B(2, c), ALU.add, [t, b_], [t])
            P.actf(gf[:, c, 0:tn], t[:, 0:tn], AF.Gelu, [t], [gf])
            P.copy(gbf[:, c, 0:tn], gf[:, c, 0:tn], [gf], [gbf])
        for c in range(2):
            p = pg[c]
            for kc in range(2):
                P.mm(p[:, 0:tn], wglub[:, kc, c * 128:(c + 1) * 128], gbf[:, kc, 0:tn], kc == 0, kc == 1, [wglub, gbf], [p])
            s = sg[c]
            P.actf(s[:, 0:tn], p[:, 0:tn], AF.Sigmoid, [p], [s])
            P.tt(yb[:, 0 + c, 0:tn], gf[:, c, 0:tn], s[:, 0:tn], ALU.mult, [gf, s], [yb])
        for c in range(2):
            P.copy(yb[:, 2 + c, 0:tn], B(3, c), [b_], [yb], e="pool")
            P.copy(yb[:, 6 + c, 0:tn], B(7, c), [b_], [yb], e="pool")
        for c in range(2):
            t = t3[c]
            P.tt(t[:, 0:tn], B(4, c), B(5, c), ALU.add, [b_], [t])
            P.actf(sq[:, c, 0:tn], t[:, 0:tn], AF.Square, [t], [sq])
            P.mm(ss_ps[:, 0:tn], blk_b[:, :], sq[:, c, 0:tn], True, True, [blk_b, sq], [ss_ps])
            P.actf(rs2[:, 0:tn], ss_ps[:, 0:tn], AF.Sqrt, [ss_ps], [rs2], scale=1.0 / 64, bias=P.eps_col[:, 0:1])
            P.op("dve", lambda: P.nc.vector.reciprocal(out=rs2[:, 0:tn], in_=rs2[:, 0:tn]), [rs2], [rs2])
            P.stt(t[:, 0:tn], t[:, 0:tn], pc[:, 2:3], rs2[:, 0:tn], ALU.mult, ALU.mult, [t, pc, rs2], [t])
            s = sg[c]
            P.actf(s[:, 0:tn], B(6, c), AF.Silu, [b_], [s])
            P.tt(yb[:, 4 + c, 0:tn], t[:, 0:tn], s[:, 0:tn], ALU.mult, [t, s], [yb])
        for k in range(8):
            for i in range(4):
                pg_ = pg[(k * 4 + i) % 3]
                pb_ = pb[(k * 4 + i) % 3]
                for c in range(8):
                    P.mm(pg_[:, 0:tn], wgb[:, c, i * 1024 + k * 128:i * 1024 + (k + 1) * 128], hb[:, c, 0:tn],
                         c == 0, c == 7, [wgb, hb], [pg_])
                for kc in range(2):
                    P.mm(pb_[:, 0:tn], wbrb[:, i * 2 + kc, k * 128:(k + 1) * 128], yb[:, i * 2 + kc, 0:tn],
                         kc == 0, kc == 1, [wbrb, yb], [pb_])
                s = sg[i % 2]
                P.actf(s[:, 0:tn], pg_[:, 0:tn], AF.Sigmoid, [pg_], [s])
                if i == 0:
                    P.tt(acc[:, 0:tn], s[:, 0:tn], pb_[:, 0:tn], ALU.mult, [s, pb_], [acc])
                else:
                    t = t3[i % 3]
                    P.tt(t[:, 0:tn], s[:, 0:tn], pb_[:, 0:tn], ALU.mult, [s, pb_], [t])
                    if i < 3:
                        P.tt(acc[:, 0:tn], acc[:, 0:tn], t[:, 0:tn], ALU.add, [acc, t], [acc], e="pool")
                    else:
                        P.tt(mb[:, k, 0:tn], acc[:, 0:tn], t[:, 0:tn], ALU.add, [acc, t], [mb], e="pool")
        for k in range(8):
            p = pg[k % 3]
            for c in range(8):
                P.mm(p[:, 0:tn], woutb[:, c, k * 128:(k + 1) * 128], mb[:, c, 0:tn], c == 0, c == 7, [woutb, mb], [p])
            P.copy(ys[:, k, 0:tn], p[:, 0:tn], [p], [ys], e="act")
        rms_rstd(P, lambda c: ys[:, c, 0:tn], 8, tn, ones_b, sq, ss_ps, rstd, 1.0 / D, [ys])
        for c in range(8):
            t = tmp[c % 2]
            P.stt(t[:, 0:tn], ys[:, c, 0:tn], cols[:, c, co + 2:co + 3], rstd[:, 0:tn], ALU.mult, ALU.mult,
                  [ys, cols, rstd], [t])
            P.tt(x[:, c, 0:tn], t[:, 0:tn], x[:, c, 0:tn], ALU.add, [t, x], [x], e="pool")
        P.store(oTv[:, :, s0:s0 + tn], x, x[:, :, 0:tn])
    return P.finish()


_PROGS = {}


def prog(name, fn, *a):
    key = (name,) + a
    if key not in _PROGS:
        _PROGS[key] = fn(*a)
    return _PROGS[key]


def colz(v):
    return np.ascontiguousarray(np.asarray(v, np.float32).reshape(-1, 128).T)


def core_tokens(lat, ctx, i):
    b, seg = i // 4, i % 4
    return np.concatenate([lat[b, seg * LTOK:(seg + 1) * LTOK], ctx[b, seg * CTOK:(seg + 1) * CTOK]], axis=0)


def gather_fm(outs, F):
    dt = outs[0].dtype
    lat = np.empty((2, SEQ, F), dt)
    ctx = np.empty((2, CTX, F), dt)
    for i, o in enumerate(outs):
        b, seg = i // 4, i % 4
        lat[b, seg * LTOK:(seg + 1) * LTOK] = o[:, :LTOK].T
        ctx[b, seg * CTOK:(seg + 1) * CTOK] = o[:, LTOK:].T
    return lat, ctx


def gather_tm(outs, F):
    dt = outs[0].dtype
    lat = np.empty((2, SEQ, F), dt)
    ctx = np.empty((2, CTX, F), dt)
    for i, o in enumerate(outs):
        b, seg = i // 4, i % 4
        lat[b, seg * LTOK:(seg + 1) * LTOK] = o[:LTOK]
        ctx[b, seg * CTOK:(seg + 1) * CTOK] = o[LTOK:]
    return lat, ctx


def rope_perm(r):
    q = r // 4
    j = np.arange(r)
    return np.where((j % (2 * q)) < q, j + q, j - q)


def rope_tables(r, pos):
    q = r // 4
    half = r // 2
    inv = (10000.0 ** (-np.arange(0, half, 2, dtype=np.float32) / half)).astype(np.float32)
    rows = (pos // 64).astype(np.float32)
    cols_ = (pos % 64).astype(np.float32)
    ang_r = rows[:, None] * inv
    ang_c = cols_[:, None] * inv
    ang = np.concatenate([ang_r, ang_r, ang_c, ang_c], axis=-1).astype(np.float32)
    cos = np.cos(ang).T
    sin = np.sin(ang).T
    j = np.arange(r)
    sgn = np.where((j % (2 * q)) < q, -1.0, 1.0)[:, None]
    return cos.astype(np.float32), (sin * sgn).astype(np.float32)


def scan_order(ctx, lat, d):
    if d == 0:
        return np.concatenate([ctx, lat], axis=0)
    return np.concatenate([ctx[::-1], lat[::-1]], axis=0)


def unscan(y, d):
    c, l = y[:CTX], y[CTX:]
    if d == 1:
        c, l = c[::-1], l[::-1]
    return c, l


DBG = {}


def kernel(x, c, ctx, c_ctx, w_ada, b_ada, norm_pre, norm_post, ffn_w1, ffn_w3, ffn_w2, w_in,
           s5_lambda_re, s5_lambda_im, s5_log_dt, s5_b_re, s5_b_im, s5_c_re, s5_c_im, s5_d, s5_w_glu,
           mla_q_norm, mla_w_uq, mla_kv_norm, mla_w_ukv, hgrn_lb_raw, hgrn_o_norm,
           gqa_q_norm, gqa_k_norm, w_branch, w_out, _debug=False, _layers=2):
    f32 = np.float32
    A = lambda a: np.ascontiguousarray(np.asarray(a))
    x, ctx = A(x), A(ctx)
    L = w_ada.shape[0]
    cT = np.stack([colz(c[0]), colz(c[1]), colz(c_ctx)], axis=-1)
    Wall = np.concatenate([A(w_ada[l]) for l in range(L)], axis=1)
    ball = np.concatenate([A(b_ada[l]) for l in range(L)], axis=0)
    ims = []
    for i in range(NCORE):
        sl = slice(i * MODC, (i + 1) * MODC)
        ims.append(dict(cT=A(cT), W=A(Wall[:, sl]), bias=A(np.broadcast_to(ball[sl], (3, MODC)))))
    res = run(prog("mod", build_mod), ims)
    mod = np.concatenate([r["out"] for r in res], axis=1).reshape(3, L, 9, D)

    def modc_for(i, l, j):
        b = i // 4
        vs_ = [mod[b, l, 3 * j + k] for k in range(3)] + [mod[2, l, 3 * j + k] for k in range(3)]
        return A(np.stack([colz(v_) for v_ in vs_], axis=-1))

    def gp_for(g0, g1):
        return A(np.stack([colz(g0), colz(g1)], axis=-1))

    def to_fm(lat, ctx_):
        return [A(core_tokens(lat, ctx_, i).T) for i in range(NCORE)]

    def run_ffn(XT, l, j, jj):
        ims = [dict(xT=XT[i], w1=A(ffn_w1[l, jj]), w3=A(ffn_w3[l, jj]), w2=A(ffn_w2[l, jj]),
                    gp=gp_for(norm_pre[l, j], norm_post[l, j]), modc=modc_for(i, l, j)) for i in range(NCORE)]
        return [r["outT"] for r in run(prog("ffn", build_ffn), ims)]

    blk = np.zeros((128, 128), f32)
    blk[:64, :64] = 1
    blk[64:, 64:] = 1
    sel = np.zeros((65, 64), f32)
    sel[64] = 1
    p32, p64 = rope_perm(32), rope_perm(64)
    XT = to_fm(x, ctx)
    for l in range(min(L, _layers)):
        last = l == L - 1
        XT = run_ffn(XT, l, 0, 0)
        if _debug:
            DBG["x1_%d" % l] = gather_fm(XT, D)
        W = A(w_in[l])
        wext = A(np.concatenate([W[:, :2400], W[:, 576:608][:, p32],
                                 W[:, 1888:2144].reshape(D, 4, 64)[:, :, p64].reshape(D, 256),
                                 W[:, 2144:2272].reshape(D, 2, 64)[:, :, p64].reshape(D, 128)], axis=1))
        wuq = np.zeros((192, 512), f32)
        uq = A(mla_w_uq[l]).reshape(192, 4, 96)
        for h in range(4):
            wuq[:, h * 128:h * 128 + 96] = uq[:, h]
            wuq[:, h * 128 + 96:h * 128 + 128] = uq[:, h, 64:][:, p32]
        ukv = A(mla_w_ukv[l]).reshape(128, 4, 128)
        wukv = A(np.concatenate([ukv[:, :, :64].reshape(128, 256), ukv[:, :, 64:].reshape(128, 256)], axis=1))
        pcols = np.zeros((128, NPC), f32)
        pcols[:, 0] = mla_q_norm[l][:128]
        pcols[:64, 1] = mla_q_norm[l][128:]
        pcols[:, 2] = mla_kv_norm[l]
        pcols[:, 3] = np.tile(gqa_q_norm[l], 2)
        pcols[:, 4] = np.tile(np.asarray(gqa_q_norm[l])[p64], 2)
        pcols[:, 5] = np.tile(gqa_k_norm[l], 2)
        pcols[:, 6] = np.tile(np.asarray(gqa_k_norm[l])[p64], 2)
        for d_ in range(2):
            for l2 in range(2):
                for c_ in range(2):
                    pcols[:, 7 + (d_ * 2 + l2) * 2 + c_] = hgrn_lb_raw[d_, l2, c_ * 128:(c_ + 1) * 128]
        pcols[:, 15] = 1.0 if l == 0 else 0.0
        pcols[:, 16] = 1.0 if l == 1 else 0.0
        ims = []
        for i in range(NCORE):
            seg = i % 4
            pos = np.arange(seg * LTOK, (seg + 1) * LTOK)
            rb = np.zeros((32, 2, NTOK), f32)
            rd = np.zeros((128, 2, NTOK), f32)
            rb[:, 0, LTOK:] = 1.0
            rd[:, 0, LTOK:] = 1.0
            cb_, sb_ = rope_tables(32, pos)
            cd_, sd_ = rope_tables(64, pos)
            rb[:, 0, :LTOK], rb[:, 1, :LTOK] = cb_, sb_
            rd[:, 0, :LTOK], rd[:, 1, :LTOK] = np.tile(cd_, (2, 1)), np.tile(sd_, (2, 1))
            ims.append(dict(xT=XT[i], wext=wext, gp=gp_for(norm_pre[l, 1], norm_post[l, 1]), modc=modc_for(i, l, 1),
                            pcols=pcols, wuq=wuq, wukv=wukv, ropeb=rb, roped=rd, blk=blk))
        pres = run(prog("proj", build_proj), ims)
        u_l, u_c = gather_fm([r["o_u"] for r in pres], 256)
        mq_l, mq_c = gather_fm([r["o_mq"].reshape(384, NTOK) for r in pres], 384)
        mk_l, mk_c = gather_fm([r["o_mk"].reshape(384, NTOK) for r in pres], 384)
        mv_l, mv_c = gather_tm([r["o_mv"] for r in pres], 256)
        hq_l, hq_c = gather_fm([r["o_hq"] for r in pres], 256)
        hk_l, hk_c = gather_fm([r["o_hk"].reshape(512, NTOK) for r in pres], 512)
        hl_l, hl_c = gather_fm([r["o_hl"].reshape(512, NTOK) for r in pres], 512)
        hv_l, hv_c = gather_tm([r["o_hv"] for r in pres], 256)
        hg_l, hg_c = gather_fm([r["o_hg"] for r in pres], 256)
        dq_l, dq_c = gather_fm([r["o_dq"] for r in pres], 256)
        dk_l, dk_c = gather_fm([r["o_dk"] for r in pres], 128)
        dv_l, dv_c = gather_tm([r["o_dv"] for r in pres], 128)
        if _debug:
            DBG["proj_%d" % l] = dict(u=(u_l, u_c), mq=(mq_l, mq_c), mk=(mk_l, mk_c), mv=(mv_l, mv_c), hq=(hq_l, hq_c),
                                      hk=(hk_l, hk_c), hl=(hl_l, hl_c), hv=(hv_l, hv_c), hg=(hg_l, hg_c),
                                      dq=(dq_l, dq_c), dk=(dk_l, dk_c), dv=(dv_l, dv_c))
        def attn(d, scale, q_of, k_of, v_of, nq, nk):
            ims = [dict(qT=A(q_of(i).T), kT=A(k_of(i).T), v=A(v_of(i)), sel=sel) for i in range(NCORE)]
            rs = run(prog("attn", build_attn, d, scale, nq, nk), ims)
            out = np.empty((2, nq, 256), f32)
            for i, r in enumerate(rs):
                out[i // 4, :, (i % 4) * 64:(i % 4 + 1) * 64] = r["oT"].T
            return out
        B_ = lambda i: i // 4
        H_ = lambda i: i % 4
        sm = 96 ** -0.5
        mla_l = attn(96, sm, lambda i: mq_l[B_(i)][:, H_(i) * 96:(H_(i) + 1) * 96],
                     lambda i: np.concatenate([mk_c[B_(i)], mk_l[B_(i)]], 0)[:, H_(i) * 96:(H_(i) + 1) * 96],
                     lambda i: np.concatenate([mv_c[B_(i)], mv_l[B_(i)]], 0)[:, H_(i) * 64:(H_(i) + 1) * 64], SEQ, LFULL)
        gqa_l = attn(64, 0.125, lambda i: dq_l[B_(i)][:, H_(i) * 64:(H_(i) + 1) * 64],
                     lambda i: np.concatenate([dk_c[B_(i)], dk_l[B_(i)]], 0)[:, (H_(i) // 2) * 64:(H_(i) // 2 + 1) * 64],
                     lambda i: np.concatenate([dv_c[B_(i)], dv_l[B_(i)]], 0)[:, (H_(i) // 2) * 64:(H_(i) // 2 + 1) * 64],
                     SEQ, LFULL)
        if not last:
            mla_c = attn(96, sm, lambda i: mq_c[B_(i)][:, H_(i) * 96:(H_(i) + 1) * 96],
                         lambda i: mk_c[B_(i)][:, H_(i) * 96:(H_(i) + 1) * 96],
                         lambda i: mv_c[B_(i)][:, H_(i) * 64:(H_(i) + 1) * 64], CTX, CTX)
            gqa_c = attn(64, 0.125, lambda i: dq_c[B_(i)][:, H_(i) * 64:(H_(i) + 1) * 64],
                         lambda i: dk_c[B_(i)][:, (H_(i) // 2) * 64:(H_(i) // 2 + 1) * 64],
                         lambda i: dv_c[B_(i)][:, (H_(i) // 2) * 64:(H_(i) // 2 + 1) * 64], CTX, CTX)
        else:
            mla_c = np.zeros((2, CTX, 256), f32)
            gqa_c = np.zeros((2, CTX, 256), f32)
        ims = []
        for i in range(NCORE):
            b, d_, half = i // 4, (i % 4) // 2, i % 2
            fs = slice(half * 128, (half + 1) * 128)
            useq = scan_order(u_c[b][:, fs], u_l[b][:, fs], d_)
            prm = np.zeros((128, 4, 3), f32)
            bri = np.zeros((128, 4, 2, 16), f32)
            cbl = np.zeros((128, 4, 2, 32), f32)
            for ct in range(4):
                for gl in range(2):
                    g = 8 * half + 2 * ct + gl
                    rs_ = slice(gl * 64, (gl + 1) * 64)
                    prm[rs_, ct, 0] = s5_lambda_re[l, d_, g]
                    prm[rs_, ct, 1] = s5_lambda_im[l, d_, g]
                    prm[rs_, ct, 2] = s5_log_dt[l, d_, g]
                    bri[rs_, ct, 0] = s5_b_re[l, d_, g]
                    bri[rs_, ct, 1] = s5_b_im[l, d_, g]
                    cbl[rs_, ct, 0, gl * 16:(gl + 1) * 16] = np.asarray(s5_c_re[l, d_, g]).T
                    cbl[rs_, ct, 1, gl * 16:(gl + 1) * 16] = np.asarray(s5_c_im[l, d_, g]).T
            ims.append(dict(uT=A(useq.T), prm=prm, bri=bri, cblk=cbl, ident=np.eye(128, dtype=f32)))
        rs = run(prog("s5", build_s5), ims)
        s5_l = np.zeros((2, 2, SEQ, 256), f32)
        s5_c = np.zeros((2, 2, CTX, 256), f32)
        for i, r in enumerate(rs):
            b, d_, half = i // 4, (i % 4) // 2, i % 2
            yc, yl = unscan(r["yT"].T, d_)
            s5_l[d_, b][:, half * 128:(half + 1) * 128] = yl
            s5_c[d_, b][:, half * 128:(half + 1) * 128] = yc
        ims = []
        mask = np.triu(np.ones((64, 64), f32))
        for i in range(NCORE):
            b, d_, hp = i // 4, (i % 4) // 2, i % 2
            fs = slice(hp * 128, (hp + 1) * 128)
            fk = slice(d_ * 256 + hp * 128, d_ * 256 + (hp + 1) * 128)
            ims.append(dict(qT=A(scan_order(hq_c[b][:, fs], hq_l[b][:, fs], d_).T),
                            kT=A(scan_order(hk_c[b][:, fk], hk_l[b][:, fk], d_).T),
                            lT=A(scan_order(hl_c[b][:, fk], hl_l[b][:, fk], d_).T),
                            v=A(scan_order(hv_c[b][:, fs], hv_l[b][:, fs], d_)),
                            identb=np.eye(64, dtype=f32), mask=mask))
        rs = run(prog("hgrn", build_hgrn), ims)
        ho_l = np.zeros((2, 2, SEQ, 256), f32)
        ho_c = np.zeros((2, 2, CTX, 256), f32)
        for i, r in enumerate(rs):
            b, d_, hp = i // 4, (i % 4) // 2, i % 2
            yc, yl = unscan(r["oT"].T, d_)
            ho_l[d_, b][:, hp * 128:(hp + 1) * 128] = yl
            ho_c[d_, b][:, hp * 128:(hp + 1) * 128] = yc
        if _debug:
            DBG["mix_%d" % l] = dict(mla=(mla_l, mla_c), gqa=(gqa_l, gqa_c), s5=(s5_l, s5_c), ho=(ho_l, ho_c))
        pcm = np.zeros((128, 4), f32)
        pcm[:, 0] = s5_d[l][:128]
        pcm[:, 1] = s5_d[l][128:]
        pcm[:, 2] = np.tile(hgrn_o_norm[l], 2)
        ims = []
        for i in range(NCORE):
            parts = [(u_l, u_c), (s5_l[0], s5_c[0]), (s5_l[1], s5_c[1]), (mla_l, mla_c), (ho_l[0], ho_c[0]),
                     (ho_l[1], ho_c[1]), (hg_l, hg_c), (gqa_l, gqa_c)]
            bin_ = np.stack([core_tokens(pl, pc_, i).T.astype(f32) for (pl, pc_) in parts], axis=0)
            ims.append(dict(xT=XT[i], wg=A(W[:, 2400:]), wbr=A(np.asarray(w_branch[l]).reshape(1024, D)), wout=A(w_out[l]),
                            wglu=A(s5_w_glu[l]), gp=gp_for(norm_pre[l, 1], norm_post[l, 1]), modc=modc_for(i, l, 1),
                            pcols=pcm, blk=blk, bin=A(bin_)))
        XT = [r["outT"] for r in run(prog("merge", build_merge), ims)]
        if _debug:
            DBG["x2_%d" % l] = gather_fm(XT, D)
        XT = run_ffn(XT, l, 2, 1)
        if _debug:
            DBG["x3_%d" % l] = gather_fm(XT, D)
    lat, _ = gather_fm(XT, D)
    return lat.astype(np.float32)
```

```python
from contextlib import ExitStack
import numpy as np
import ml_dtypes
import concourse.bass as bass
import concourse.mybir as mybir
from concourse.bass_utils import run_bass_kernel_spmd

F32 = mybir.dt.float32
BF16 = mybir.dt.bfloat16
AF = mybir.ActivationFunctionType
ALU = mybir.AluOpType
AX = mybir.AxisListType
NPBF = ml_dtypes.bfloat16

D = 1024
DFF = 2816
NIN = 6496
EPS = 1e-6
NCORE = 8
SKIP_OWN = False
SEQ = 16384
CTX = 256
LTOK = 4096
CTOK = 64
NTOK = LTOK + CTOK
LFULL = SEQ + CTX


class Buf:
    __slots__ = ("w", "r", "sem", "cnt", "name", "semname")

    def __init__(self, name):
        self.w = None
        self.r = {}
        self.sem = None
        self.semname = None
        self.cnt = 0
        self.name = name


class Tile:
    def __init__(self, t, name):
        self.t = t
        self.b = Buf(name)

    def __getitem__(self, idx):
        return self.t[idx]


class Prog:
    def __init__(self, name):
        self.name = name
        self.nc = bass.Bass("TRN2", target_bir_lowering=False)
        self.es = ExitStack()
        nc = self.nc
        self.eng = dict(pe=nc.tensor, act=nc.scalar, dve=nc.vector, pool=nc.gpsimd, sp=nc.sync)
        self.sem = {}
        self.cnt = {}
        for e in ("pe", "act", "dve", "pool"):
            self.sem[e] = self.es.enter_context(nc.semaphore("s_" + e))
            self.cnt[e] = 0
        self.seen = {e: {} for e in self.eng}
        self.semobj = {("s_" + e): self.sem[e] for e in self.sem}
        self.out_toks = []
        self.nsem = 4
        self.dram = {}
        self.root = self.es
        self.pre = ""
        self.pool_sems = []
        self.stage_bufs = None

    def stage(self, pre):
        P = self

        class _S:
            def __enter__(s_):
                s_.outer = (P.es, P.stage_bufs, P.pre)
                P.es = ExitStack()
                P.stage_bufs = []
                P.pre = pre
                return P

            def __exit__(s_, *exc):
                if exc[0] is not None:
                    return False
                toks = [("s_" + e, P.cnt[e]) for e in P.cnt if P.cnt[e] > 0]
                toks += [(b.semname, b.cnt) for b in P.stage_bufs]
                for e in P.eng:
                    P._wait(e, toks)
                for b in P.stage_bufs:
                    P.pool_sems.append((b.semname, b.sem, b.cnt))
                P.es.close()
                P.es, P.stage_bufs, P.pre = s_.outer
                return False
        return _S()

    def din(self, name, shape, dt=F32):
        name = self.pre + name
        t = self.nc.dram_tensor(name, list(shape), dt, kind="ExternalInput").ap()
        self.dram[name] = t
        return t

    def dout(self, name, shape, dt=F32):
        name = self.pre + name
        t = self.nc.dram_tensor(name, list(shape), dt, kind="ExternalOutput").ap()
        self.dram[name] = t
        return t

    def sb(self, name, shape, dt=F32):
        name = self.pre + name
        return Tile(self.es.enter_context(self.nc.sbuf_tensor(name, list(shape), dt)), name)

    def ps(self, name, shape, dt=F32):
        name = self.pre + name
        return Tile(self.es.enter_context(self.nc.psum_tensor(name, list(shape), dt)), name)

    def _wait(self, e, toks, skip_own=False):
        need = {}
        for (s, v) in toks:
            if v > need.get(s, 0):
                need[s] = v
        own = "s_" + e
        for s, v in need.items():
            if s == own and (skip_own or e == "pe" or v > self.cnt[e]):
                continue
            if self.seen[e].get(s, 0) < v:
                self.eng[e].wait_ge(self.semobj[s], v)
                self.seen[e][s] = v

    @staticmethod
    def _bufs(xs):
        return [x.b if isinstance(x, Tile) else x for x in xs]

    def _deps(self, reads, writes):
        toks = []
        for b in reads:
            if b.w:
                toks.append(b.w)
        for b in writes:
            if b.w:
                toks.append(b.w)
            toks.extend(b.r.items())
        return toks

    def _mark(self, tok, reads, writes):
        for b in reads:
            if b.r.get(tok[0], 0) < tok[1]:
                b.r[tok[0]] = tok[1]
        for b in writes:
            b.w = tok
            b.r = {}

    def op(self, e, fn, reads=(), writes=(), inc=True):
        reads = self._bufs(reads)
        writes = self._bufs(writes)
        self._wait(e, self._deps(reads, writes), skip_own=SKIP_OWN)
        ins = fn()
        idx = self.cnt[e] + 1
        if inc:
            ins.then_inc(self.sem[e], 1)
            self.cnt[e] = idx
        self._mark(("s_" + e, idx), reads, writes)
        return ins

    def dma(self, out, in_, sbuf, reads=(), writes=(), q="sp", is_out=False, **kw):
        b = sbuf.b
        if b.sem is None:
            if self.pool_sems:
                b.semname, b.sem, b.cnt = self.pool_sems.pop()
            else:
                b.semname = "d%d" % self.nsem
                b.sem = self.root.enter_context(self.nc.semaphore(b.semname))
                self.semobj[b.semname] = b.sem
                self.nsem += 1
            if self.stage_bufs is not None:
                self.stage_bufs.append(b)
        reads = self._bufs(reads)
        writes = self._bufs(writes)
        self._wait(q, self._deps(reads, writes), skip_own=False)
        ins = self.eng[q].dma_start(out=out, in_=in_, **kw)
        b.cnt += 16
        ins.then_inc(b.sem, 16)
        tok = (b.semname, b.cnt)
        self._mark(tok, reads, writes)
        if is_out:
            self.out_toks.append(tok)
        return ins

    def load(self, tile, dst_ap, src_ap, q="sp", **kw):
        return self.dma(dst_ap, src_ap, tile, reads=(), writes=(tile,), q=q, **kw)

    def store(self, dst_ap, tile, src_ap, q="pool", **kw):
        return self.dma(dst_ap, src_ap, tile, reads=(tile,), writes=(), q=q, is_out=True, **kw)

    def finish(self):
        self._wait("sp", self.out_toks)
        self.root.close()
        return self.nc

    def mm(self, out, lhsT, rhs, start, stop, reads, writes, inc=None, **kw):
        if inc is None:
            inc = stop
        return self.op("pe", lambda: self.nc.tensor.matmul(out, lhsT=lhsT, rhs=rhs, start=start, stop=stop, **kw),
                       reads, writes, inc=inc)

    def actf(self, out, in_, func, reads, writes, e="act", **kw):
        return self.op("act", lambda: self.nc.scalar.activation(out=out, in_=in_, func=func, **kw), reads, writes)

    def tt(self, out, in0, in1, op, reads, writes, e="dve"):
        return self.op(e, lambda: self.eng[e].tensor_tensor(out=out, in0=in0, in1=in1, op=op), reads, writes)

    def ts(self, out, in0, s1, s2, op0, op1, reads, writes, e="dve"):
        if op1 is None:
            return self.op(e, lambda: self.eng[e].tensor_scalar(out=out, in0=in0, scalar1=s1, scalar2=None, op0=op0),
                           reads, writes)
        return self.op(e, lambda: self.eng[e].tensor_scalar(out=out, in0=in0, scalar1=s1, scalar2=s2, op0=op0, op1=op1),
                       reads, writes)

    def stt(self, out, in0, scalar, in1, op0, op1, reads, writes):
        return self.op("dve", lambda: self.nc.vector.scalar_tensor_tensor(out=out, in0=in0, scalar=scalar, in1=in1,
                                                                          op0=op0, op1=op1), reads, writes)

    def copy(self, out, in_, reads, writes, e="dve"):
        if e == "act":
            return self.op("act", lambda: self.nc.scalar.copy(out=out, in_=in_), reads, writes)
        return self.op(e, lambda: self.eng[e].tensor_copy(out=out, in_=in_), reads, writes)


def run(prog_nc, in_maps):
    res = run_bass_kernel_spmd(prog_nc, in_maps, core_ids=list(range(NCORE)))
    return res.results


def load_cast_weight(P, dst, dst_view_fn, src, rows_chunks, ncols, stage, piece=1408):
    k = 0
    engs = ("dve", "pool", "act")
    for c in range(rows_chunks):
        for c0 in range(0, ncols, piece):
            w = min(piece, ncols - c0)
            st = stage[k % len(stage)]
            P.load(st, st[:, 0:w], src[c * 128:(c + 1) * 128, c0:c0 + w])
            e = engs[k % 3]
            P.copy(dst_view_fn(c, c0, w), st[:, 0:w], [st], [dst], e=e)
            k += 1


def rms_rstd(P, x_chunks_fn, nchunk, T, ones_b, sq, ss_ps, rstd, inv_n, reads):
    for c in range(nchunk):
        P.actf(sq[:, c, 0:T], x_chunks_fn(c), AF.Square, reads, [sq])
    for c in range(nchunk):
        P.mm(ss_ps[:, 0:T], ones_b[:, :], sq[:, c, 0:T], c == 0, c == nchunk - 1, [ones_b, sq], [ss_ps])
    P.actf(rstd[:, 0:T], ss_ps[:, 0:T], AF.Sqrt, [ss_ps], [rstd], scale=inv_n, bias=P.eps_col[:, 0:1])
    P.op("dve", lambda: P.nc.vector.reciprocal(out=rstd[:, 0:T], in_=rstd[:, 0:T]), [rstd], [rstd])


def token_tiles(T=256):
    tiles = []
    for s in range(0, LTOK, T):
        tiles.append((s, T, 0))
    tiles.append((LTOK, CTOK, 1))
    return tiles


def emit_ffn(P, x_in=None):
    T = 256
    xT = x_in if x_in is not None else P.din("xT", [D, NTOK])
    w1 = P.din("w1", [D, DFF])
    w3 = P.din("w3", [D, DFF])
    w2 = P.din("w2", [DFF, D])
    gp = P.din("gp", [128, 8, 2])
    modc = P.din("modc", [128, 8, 6])
    outT = P.dout("outT", [D, NTOK])

    w1b = P.sb("w1b", [128, 8, DFF], BF16)
    w3b = P.sb("w3b", [128, 8, DFF], BF16)
    w2b = P.sb("w2b", [128, 22, D], BF16)
    stage = [P.sb("stg%d" % i, [128, 1408], F32) for i in range(2)]
    ones_b = P.sb("ones_b", [128, 128], BF16)
    P.eps_col = P.sb("eps_col", [128, 1], F32)
    gps = P.sb("gps", [128, 8, 2], F32)
    mods = P.sb("mods", [128, 8, 6], F32)
    cols = P.sb("cols", [128, 8, 6], F32)
    P.op("dve", lambda: P.nc.vector.memset(ones_b[:, :], 1.0), [], [ones_b])
    P.op("dve", lambda: P.nc.vector.memset(P.eps_col[:, :], EPS), [], [P.eps_col])
    P.load(gps, gps[:, :, :], gp)
    P.load(mods, mods[:, :, :], modc)
    for tt in range(2):
        o = 3 * tt
        P.ts(cols[:, :, o + 0], mods[:, :, o + 1], 1.0, None, ALU.add, None, [mods], [cols])
        P.tt(cols[:, :, o + 0], cols[:, :, o + 0], gps[:, :, 0], ALU.mult, [cols, gps], [cols])
        P.copy(cols[:, :, o + 1], mods[:, :, o + 0], [mods], [cols])
        P.stt(cols[:, :, o + 2], mods[:, :, o + 2], 0.5, gps[:, :, 1], ALU.mult, ALU.mult, [mods, gps], [cols])

    load_cast_weight(P, w1b, lambda c, c0, w: w1b[:, c, c0:c0 + w], w1, 8, DFF, stage)
    load_cast_weight(P, w3b, lambda c, c0, w: w3b[:, c, c0:c0 + w], w3, 8, DFF, stage)
    load_cast_weight(P, w2b, lambda c, c0, w: w2b[:, c, c0:c0 + w], w2, 22, D, stage, piece=1024)

    NB = 2
    xs = [P.sb("x%d" % i, [128, 8, T], F32) for i in range(NB)]
    sq = P.sb("sq", [128, 8, T], BF16)
    rstd = P.sb("rstd", [128, T], F32)
    tmp = [P.sb("tmp%d" % i, [128, T], F32) for i in range(2)]
    hb = P.sb("hb", [128, 8, T], BF16)
    gb = P.sb("gb", [128, 22, T], BF16)
    sl = [P.sb("sl%d" % i, [128, T], F32) for i in range(2)]
    ys = P.sb("ys", [128, 8, T], F32)
    ss_ps = P.ps("ss_ps", [128, 512], F32)
    pa = [P.ps("pa%d" % i, [128, 512], F32) for i in range(3)]
    pb = [P.ps("pb%d" % i, [128, 512], F32) for i in range(3)]

    tiles = token_tiles(T)
    xTv = xT.rearrange("(c p) n -> p c n", p=128)
    oTv = outT.rearrange("(c p) n -> p c n", p=128)
    for ti, (s0, tn, ty) in enumerate(tiles):
        x = xs[ti % NB]
        o = x
        co = 3 * ty
        P.load(x, x[:, :, 0:tn], xTv[:, :, s0:s0 + tn])
        rms_rstd(P, lambda c: x[:, c, 0:tn], 8, tn, ones_b, sq, ss_ps, rstd, 1.0 / D, [x])
        for c in range(8):
            t = tmp[c % 2]
            P.stt(t[:, 0:tn], x[:, c, 0:tn], cols[:, c, co + 0:co + 1], rstd[:, 0:tn], ALU.mult, ALU.mult,
                  [x, cols, rstd], [t])
            P.actf(hb[:, c, 0:tn], t[:, 0:tn], AF.Identity, [t, cols], [hb], bias=cols[:, c, co + 1:co + 2], scale=1.0)
        for j in range(22):
            p1 = pa[j % 3]
            p3 = pb[j % 3]
            for c in range(8):
                P.mm(p1[:, 0:tn], w1b[:, c, j * 128:(j + 1) * 128], hb[:, c, 0:tn], c == 0, c == 7, [w1b, hb], [p1])
            for c in range(8):
                P.mm(p3[:, 0:tn], w3b[:, c, j * 128:(j + 1) * 128], hb[:, c, 0:tn], c == 0, c == 7, [w3b, hb], [p3])
            s = sl[j % 2]
            P.actf(s[:, 0:tn], p1[:, 0:tn], AF.Silu, [p1], [s])
            P.tt(gb[:, j, 0:tn], s[:, 0:tn], p3[:, 0:tn], ALU.mult, [s, p3], [gb])
        for k in range(8):
            p = pa[k % 3]
            for j in range(22):
                P.mm(p[:, 0:tn], w2b[:, j, k * 128:(k + 1) * 128], gb[:, j, 0:tn], j == 0, j == 21, [w2b, gb], [p])
            P.copy(ys[:, k, 0:tn], p[:, 0:tn], [p], [ys], e="act")
        rms_rstd(P, lambda c: ys[:, c, 0:tn], 8, tn, ones_b, sq, ss_ps, rstd, 1.0 / D, [ys])
        for c in range(8):
            t = tmp[c % 2]
            P.stt(t[:, 0:tn], ys[:, c, 0:tn], cols[:, c, co + 2:co + 3], rstd[:, 0:tn], ALU.mult, ALU.mult,
                  [ys, cols, rstd], [t])
            P.tt(o[:, c, 0:tn], t[:, 0:tn], x[:, c, 0:tn], ALU.add, [t, x], [x], e="pool")
        P.store(oTv[:, :, s0:s0 + tn], o, o[:, :, 0:tn])
    return outT


MODC = 2 * 9 * D // NCORE


def build_mod():
    P = Prog("mod")
    cT = P.din("cT", [128, 8, 3])
    W = P.din("W", [D, MODC])
    bias = P.din("bias", [3, MODC])
    out = P.dout("out", [3, MODC])
    cs = P.sb("cs", [128, 8, 3], F32)
    sc = P.sb("sc", [128, 8, 3], F32)
    Ws = P.sb("Ws", [128, 8, MODC], F32)
    bs = P.sb("bs", [3, MODC], F32)
    os_ = P.sb("os", [3, MODC], F32)
    pp = [P.ps("pp%d" % i, [128, 512], F32) for i in range(2)]
    P.load(cs, cs[:, :, :], cT)
    P.load(bs, bs[:, :], bias)
    for c in range(8):
        P.load(Ws, Ws[:, c, :], W[c * 128:(c + 1) * 128, :])
    P.actf(sc[:, :, :], cs[:, :, :], AF.Silu, [cs], [sc])
    for i, c0 in enumerate(range(0, MODC, 512)):
        w = min(512, MODC - c0)
        p = pp[i % 2]
        for c in range(8):
            P.mm(p[0:3, 0:w], sc[:, c, :], Ws[:, c, c0:c0 + w], c == 0, c == 7, [sc, Ws], [p])
        P.tt(os_[:, c0:c0 + w], p[0:3, 0:w], bs[:, c0:c0 + w], ALU.add, [p, bs], [os_])
    P.store(out, os_, os_[:, :])
    return P.finish()


WEXT = 2400 + 32 + 256 + 128
NPC = 20


def make_h(P, x, tn, cols, co, ones_b, sq, ss_ps, rstd, tmp, hb):
    rms_rstd(P, lambda c: x[:, c, 0:tn], 8, tn, ones_b, sq, ss_ps, rstd, 1.0 / D, [x])
    for c in range(8):
        t = tmp[c % 2]
        P.stt(t[:, 0:tn], x[:, c, 0:tn], cols[:, c, co + 0:co + 1], rstd[:, 0:tn], ALU.mult, ALU.mult,
              [x, cols, rstd], [t])
        P.actf(hb[:, c, 0:tn], t[:, 0:tn], AF.Identity, [t, cols], [hb], bias=cols[:, c, co + 1:co + 2], scale=1.0)


def mod_cols(P, cols, mods, gps, with_gate, gate_scale=1.0, gcol=1):
    for tt in range(2):
        o = 3 * tt
        P.ts(cols[:, :, o + 0], mods[:, :, o + 1], 1.0, None, ALU.add, None, [mods], [cols])
        P.tt(cols[:, :, o + 0], cols[:, :, o + 0], gps[:, :, 0], ALU.mult, [cols, gps], [cols])
        P.copy(cols[:, :, o + 1], mods[:, :, o + 0], [mods], [cols])
        if with_gate:
            P.stt(cols[:, :, o + 2], mods[:, :, o + 2], gate_scale, gps[:, :, gcol], ALU.mult, ALU.mult,
                  [mods, gps], [cols])


def emit_proj(P, x_in=None):
    T = 256
    xT = x_in if x_in is not None else P.din("xT", [D, NTOK])
    wext = P.din("wext", [D, WEXT])
    gp = P.din("gp", [128, 8, 2])
    modc = P.din("modc", [128, 8, 6])
    pcols = P.din("pcols", [128, NPC])
    wuq = P.din("wuq", [192, 512])
    wukv = P.din("wukv", [128, 512])
    ropeb = P.din("ropeb", [32, 2, NTOK])
    roped = P.din("roped", [128, 2, NTOK])
    blk = P.din("blk", [128, 128])
    o_u = P.dout("o_u", [256, NTOK])
    o_mq = P.dout("o_mq", [4, 96, NTOK], BF16)
    o_mk = P.dout("o_mk", [4, 96, NTOK], BF16)
    o_mv = P.dout("o_mv", [NTOK, 256], BF16)
    o_hq = P.dout("o_hq", [256, NTOK])
    o_hk = P.dout("o_hk", [2, 256, NTOK])
    o_hl = P.dout("o_hl", [2, 256, NTOK])
    o_hv = P.dout("o_hv", [NTOK, 256], BF16)
    o_hg = P.dout("o_hg", [256, NTOK])
    o_dq = P.dout("o_dq", [256, NTOK], BF16)
    o_dk = P.dout("o_dk", [128, NTOK], BF16)
    o_dv = P.dout("o_dv", [NTOK, 128], BF16)

    wb = P.sb("wb", [128, 8, WEXT], BF16)
    stage = [P.sb("stg%d" % i, [128, 1408], F32) for i in range(2)]
    ones_b = P.sb("ones_b", [128, 128], BF16)
    blk_f = P.sb("blk_f", [128, 128], F32)
    blk_b = P.sb("blk_b", [128, 128], BF16)
    P.eps_col = P.sb("eps_col", [128, 1], F32)
    gps = P.sb("gps", [128, 8, 2], F32)
    mods = P.sb("mods", [128, 8, 6], F32)
    cols = P.sb("cols", [128, 8, 6], F32)
    pc = P.sb("pc", [128, NPC], F32)
    lbc = P.sb("lbc", [128, 4, 8], F32)
    wuq_f = P.sb("wuq_f", [128, 2, 512], F32)
    wuq_b = P.sb("wuq_b", [128, 2, 512], BF16)
    wukv_f = P.sb("wukv_f", [128, 512], F32)
    wukv_b = P.sb("wukv_b", [128, 512], BF16)
    P.op("dve", lambda: P.nc.vector.memset(ones_b[:, :], 1.0), [], [ones_b])
    P.op("dve", lambda: P.nc.vector.memset(P.eps_col[:, :], EPS), [], [P.eps_col])
    P.load(gps, gps[:, :, :], gp)
    P.load(mods, mods[:, :, :], modc)
    P.load(pc, pc[:, :], pcols)
    P.load(blk_f, blk_f[:, :], blk)
    P.copy(blk_b[:, :], blk_f[:, :], [blk_f], [blk_b])
    P.load(wuq_f, wuq_f[:, 0, :], wuq[0:128, :])
    P.load(wuq_f, wuq_f[0:64, 1, :], wuq[128:192, :])
    P.copy(wuq_b[:, 0, :], wuq_f[:, 0, :], [wuq_f], [wuq_b])
    P.copy(wuq_b[0:64, 1, :], wuq_f[0:64, 1, :], [wuq_f], [wuq_b])
    P.load(wukv_f, wukv_f[:, :], wukv)
    P.copy(wukv_b[:, :], wukv_f[:, :], [wukv_f], [wukv_b])
    mod_cols(P, cols, mods, gps, False)
    for d in range(2):
        for c in range(2):
            k = d * 2 + c
            r0 = pc[:, 7 + (d * 2 + 0) * 2 + c:7 + (d * 2 + 0) * 2 + c + 1]
            r1 = pc[:, 7 + (d * 2 + 1) * 2 + c:7 + (d * 2 + 1) * 2 + c + 1]
            L = lambda j: lbc[:, k, j:j + 1]
            P.tt(L(2), r0, r1, ALU.subtract, [pc], [lbc])
            P.actf(L(3), L(2), AF.Sigmoid, [lbc], [lbc])
            P.actf(L(4), L(2), AF.Sigmoid, [lbc], [lbc], scale=-1.0)
            P.tt(L(5), L(3), L(3), ALU.subtract, [lbc], [lbc])
            P.tt(L(6), L(3), L(4), ALU.add, [lbc], [lbc])
            P.tt(L(6), L(6), L(3), ALU.subtract, [lbc], [lbc])
            P.ts(L(5), L(5), 0.0, 1.0, ALU.max, ALU.min, [lbc], [lbc])
            P.ts(L(6), L(6), 0.0, 1.0, ALU.max, ALU.min, [lbc], [lbc])
            P.tt(L(5), L(5), pc[:, 15:16], ALU.mult, [lbc, pc], [lbc])
            P.stt(L(0), L(6), pc[:, 16:17], L(5), ALU.mult, ALU.add, [lbc, pc], [lbc])
            P.ts(L(1), L(0), -1.0, 1.0, ALU.mult, ALU.add, [lbc], [lbc])
    load_cast_weight(P, wb, lambda c, c0, w: wb[:, c, c0:c0 + w], wext, 8, WEXT, stage)

    xs = [P.sb("x%d" % i, [128, 8, T], F32) for i in range(2)]
    sq = P.sb("sq", [128, 8, T], BF16)
    rstd = P.sb("rstd", [128, T], F32)
    rs2 = P.sb("rs2", [128, T], F32)
    tmp = [P.sb("tmp%d" % i, [128, T], F32) for i in range(2)]
    t3 = [P.sb("t3_%d" % i, [128, T], F32) for i in range(2)]
    hb = P.sb("hb", [128, 8, T], BF16)
    rb = P.sb("rb", [32, 2, T], F32)
    rd = P.sb("rd", [128, 2, T], F32)
    cqn = P.sb("cqn", [128, 2, T], BF16)
    ckvn = P.sb("ckvn", [128, T], BF16)
    s_u = P.sb("s_u", [128, 2, T], F32)
    s_mqn = P.sb("s_mqn", [64, 4, T], BF16)
    s_mqr = P.sb("s_mqr", [32, 4, T], BF16)
    s_mkn = P.sb("s_mkn", [64, 4, T], BF16)
    s_mkr = P.sb("s_mkr", [32, T], BF16)
    s_mv = P.sb("s_mv", [128, 2, 256], BF16)
    s_hq = P.sb("s_hq", [128, 2, T], F32)
    s_hk = P.sb("s_hk", [128, 4, T], F32)
    s_hl = P.sb("s_hl", [128, 4, T], F32)
    s_hv = P.sb("s_hv", [128, 2, 256], BF16)
    s_hg = P.sb("s_hg", [128, 2, T], F32)
    s_dq = P.sb("s_dq", [128, 2, T], BF16)
    s_dk = P.sb("s_dk", [128, T], BF16)
    s_dv = P.sb("s_dv", [128, 2, 128], BF16)
    ss_ps = P.ps("ss_ps", [128, 512], F32)
    pq = [P.ps("pq%d" % i, [128, 512], F32) for i in range(6)]
    pi = [0]

    def nextp():
        pi[0] += 1
        return pq[pi[0] % 6]

    def proj(p, m, c0, ncol, tn):
        for c in range(8):
            P.mm(p[0:ncol, 0:tn], wb[:, c, c0:c0 + ncol], hb[:, c, 0:tn], c == 0, c == 7, [wb, hb], [p])

    def proj_tok(p, sub, c0, ncol, tn):
        n = min(128, tn - sub * 128)
        for c in range(8):
            P.mm(p[0:n, 0:ncol], hb[:, c, sub * 128:sub * 128 + n], wb[:, c, c0:c0 + ncol], c == 0, c == 7, [wb, hb], [p])
        return n

    def rstd_from(ps_, rows, tn, inv_n, out):
        P.actf(out[0:rows, 0:tn], ps_[0:rows, 0:tn], AF.Sqrt, [ps_], [out], scale=inv_n, bias=P.eps_col[0:rows, 0:1])
        P.op("dve", lambda: P.nc.vector.reciprocal(out=out[0:rows, 0:tn], in_=out[0:rows, 0:tn]), [out], [out])

    xTv = xT.rearrange("(c p) n -> p c n", p=128)
    for ti, (s0, tn, ty) in enumerate(token_tiles(T)):
        x = xs[ti % 2]
        co = 3 * ty
        nsub = (tn + 127) // 128
        P.load(x, x[:, :, 0:tn], xTv[:, :, s0:s0 + tn])
        P.load(rb, rb[:, :, 0:tn], ropeb[:, :, s0:s0 + tn])
        P.load(rd, rd[:, :, 0:tn], roped[:, :, s0:s0 + tn])
        make_h(P, x, tn, cols, co, ones_b, sq, ss_ps, rstd, tmp, hb)
        for c in range(2):
            p = nextp()
            proj(p, 128, c * 128, 128, tn)
            P.copy(s_u[:, c, 0:tn], p[:, 0:tn], [p], [s_u], e="act")
        P.store(o_u.rearrange("(c p) n -> p c n", p=128)[:, :, s0:s0 + tn], s_u, s_u[:, :, 0:tn])
        pA = nextp()
        proj(pA, 128, 256, 128, tn)
        pB = nextp()
        proj(pB, 64, 384, 64, tn)
        P.actf(sq[:, 0, 0:tn], pA[:, 0:tn], AF.Square, [pA], [sq])
        P.actf(sq[0:64, 1, 0:tn], pB[0:64, 0:tn], AF.Square, [pB], [sq])
        P.mm(ss_ps[:, 0:tn], ones_b[:, :], sq[:, 0, 0:tn], True, False, [ones_b, sq], [ss_ps])
        P.mm(ss_ps[:, 0:tn], ones_b[0:64, :], sq[0:64, 1, 0:tn], False, True, [ones_b, sq], [ss_ps])
        rstd_from(ss_ps, 128, tn, 1.0 / 192, rs2)
        P.stt(cqn[:, 0, 0:tn], pA[:, 0:tn], pc[:, 0:1], rs2[:, 0:tn], ALU.mult, ALU.mult, [pA, pc, rs2], [cqn])
        P.stt(cqn[0:64, 1, 0:tn], pB[0:64, 0:tn], pc[0:64, 1:2], rs2[0:64, 0:tn], ALU.mult, ALU.mult, [pB, pc, rs2], [cqn])
        for h in range(4):
            def qmm(p, rows, c0):
                P.mm(p[0:rows, 0:tn], wuq_b[:, 0, c0:c0 + rows], cqn[:, 0, 0:tn], True, False, [wuq_b, cqn], [p])
                P.mm(p[0:rows, 0:tn], wuq_b[0:64, 1, c0:c0 + rows], cqn[0:64, 1, 0:tn], False, True, [wuq_b, cqn], [p])
            pn = nextp()
            qmm(pn, 64, h * 128)
            P.copy(s_mqn[:, h, 0:tn], pn[0:64, 0:tn], [pn], [s_mqn], e="act")
            pr = nextp()
            qmm(pr, 32, h * 128 + 64)
            pp_ = nextp()
            qmm(pp_, 32, h * 128 + 96)
            ta, tb = t3[0], t3[1]
            P.tt(ta[0:32, 0:tn], pr[0:32, 0:tn], rb[:, 0, 0:tn], ALU.mult, [pr, rb], [ta])
            P.tt(tb[0:32, 0:tn], pp_[0:32, 0:tn], rb[:, 1, 0:tn], ALU.mult, [pp_, rb], [tb])
            P.tt(s_mqr[:, h, 0:tn], ta[0:32, 0:tn], tb[0:32, 0:tn], ALU.add, [ta, tb], [s_mqr])
        P.store(o_mq[:, 0:64, s0:s0 + tn].rearrange("h p n -> p h n"), s_mqn, s_mqn[:, :, 0:tn])
        P.store(o_mq[:, 64:96, s0:s0 + tn].rearrange("h p n -> p h n"), s_mqr, s_mqr[:, :, 0:tn])
        pK = nextp()
        proj(pK, 128, 448, 128, tn)
        P.actf(sq[:, 0, 0:tn], pK[:, 0:tn], AF.Square, [pK], [sq])
        P.mm(ss_ps[:, 0:tn], ones_b[:, :], sq[:, 0, 0:tn], True, True, [ones_b, sq], [ss_ps])
        rstd_from(ss_ps, 128, tn, 1.0 / 128, rs2)
        P.stt(ckvn[:, 0:tn], pK[:, 0:tn], pc[:, 2:3], rs2[:, 0:tn], ALU.mult, ALU.mult, [pK, pc, rs2], [ckvn])
        for h in range(4):
            pn = nextp()
            P.mm(pn[0:64, 0:tn], wukv_b[:, h * 64:(h + 1) * 64], ckvn[:, 0:tn], True, True, [wukv_b, ckvn], [pn])
            P.copy(s_mkn[:, h, 0:tn], pn[0:64, 0:tn], [pn], [s_mkn], e="act")
        P.store(o_mk[:, 0:64, s0:s0 + tn].rearrange("h p n -> p h n"), s_mkn, s_mkn[:, :, 0:tn])
        pr = nextp()
        proj(pr, 32, 576, 32, tn)
        pp_ = nextp()
        proj(pp_, 32, 2400, 32, tn)
        ta, tb = t3[0], t3[1]
        P.tt(ta[0:32, 0:tn], pr[0:32, 0:tn], rb[:, 0, 0:tn], ALU.mult, [pr, rb], [ta])
        P.tt(tb[0:32, 0:tn], pp_[0:32, 0:tn], rb[:, 1, 0:tn], ALU.mult, [pp_, rb], [tb])
        P.tt(s_mkr[:, 0:tn], ta[0:32, 0:tn], tb[0:32, 0:tn], ALU.add, [ta, tb], [s_mkr])
        for h in range(4):
            P.store(o_mk[h, 64:96, s0:s0 + tn], s_mkr, s_mkr[:, 0:tn])
        for sub in range(nsub):
            n = min(128, tn - sub * 128)
            pv = nextp()
            P.mm(pv[0:n, 0:256], ckvn[:, sub * 128:sub * 128 + n], wukv_b[:, 256:512], True, True, [wukv_b, ckvn], [pv])
            P.copy(s_mv[0:n, sub, :], pv[0:n, 0:256], [pv], [s_mv], e="act")
            P.store(o_mv[s0 + sub * 128:s0 + sub * 128 + n, :], s_mv, s_mv[0:n, sub, :])
        for c in range(2):
            p = nextp()
            proj(p, 128, 608 + c * 128, 128, tn)
            P.copy(s_hq[:, c, 0:tn], p[:, 0:tn], [p], [s_hq], e="act")
            p = nextp()
            proj(p, 128, 1632 + c * 128, 128, tn)
            P.copy(s_hg[:, c, 0:tn], p[:, 0:tn], [p], [s_hg], e="act")
            for d in range(2):
                k = d * 2 + c
                p = nextp()
                proj(p, 128, 1120 + d * 256 + c * 128, 128, tn)
                ta, tb = t3[0], t3[1]
                P.actf(ta[:, 0:tn], p[:, 0:tn], AF.Sigmoid, [p], [ta])
                P.ts(ta[:, 0:tn], ta[:, 0:tn], lbc[:, k, 1:2], lbc[:, k, 0:1], ALU.mult, ALU.add, [ta, lbc], [ta])
                P.ts(ta[:, 0:tn], ta[:, 0:tn], 1e-20, None, ALU.max, None, [ta], [ta])
                P.actf(s_hl[:, k, 0:tn], ta[:, 0:tn], AF.Ln, [ta], [s_hl])
                P.actf(tb[:, 0:tn], p[:, 0:tn], AF.Sigmoid, [p], [tb], scale=-1.0)
                P.ts(s_hk[:, k, 0:tn], tb[:, 0:tn], lbc[:, k, 1:2], None, ALU.mult, None, [tb, lbc], [s_hk])
        P.store(o_hq.rearrange("(c p) n -> p c n", p=128)[:, :, s0:s0 + tn], s_hq, s_hq[:, :, 0:tn])
        P.store(o_hg.rearrange("(c p) n -> p c n", p=128)[:, :, s0:s0 + tn], s_hg, s_hg[:, :, 0:tn])
        P.store(o_hk.rearrange("d (c p) n -> p (d c) n", p=128)[:, :, s0:s0 + tn], s_hk, s_hk[:, :, 0:tn])
        P.store(o_hl.rearrange("d (c p) n -> p (d c) n", p=128)[:, :, s0:s0 + tn], s_hl, s_hl[:, :, 0:tn])
        for sub in range(nsub):
            pv = nextp()
            n = proj_tok(pv, sub, 864, 256, tn)
            P.copy(s_hv[0:n, sub, :], pv[0:n, 0:256], [pv], [s_hv], e="act")
            P.store(o_hv[s0 + sub * 128:s0 + sub * 128 + n, :], s_hv, s_hv[0:n, sub, :])
        for (dst, dcol, c0, cp0, gcol) in ((s_dq, 0, 1888, 2432, 3), (s_dq, 1, 2016, 2560, 3), (s_dk, None, 2144, 2688, 5)):
            pz = nextp()
            proj(pz, 128, c0, 128, tn)
            pzp = nextp()
            proj(pzp, 128, cp0, 128, tn)
            P.actf(sq[:, 0, 0:tn], pz[:, 0:tn], AF.Square, [pz], [sq])
            P.mm(ss_ps[:, 0:tn], blk_b[:, :], sq[:, 0, 0:tn], True, True, [blk_b, sq], [ss_ps])
            rstd_from(ss_ps, 128, tn, 1.0 / 64, rs2)
            ta, tb = t3[0], t3[1]
            P.stt(ta[:, 0:tn], pz[:, 0:tn], pc[:, gcol:gcol + 1], rs2[:, 0:tn], ALU.mult, ALU.mult, [pz, pc, rs2], [ta])
            P.stt(tb[:, 0:tn], pzp[:, 0:tn], pc[:, gcol + 1:gcol + 2], rs2[:, 0:tn], ALU.mult, ALU.mult, [pzp, pc, rs2], [tb])
            P.tt(ta[:, 0:tn], ta[:, 0:tn], rd[:, 0, 0:tn], ALU.mult, [ta, rd], [ta])
            P.tt(tb[:, 0:tn], tb[:, 0:tn], rd[:, 1, 0:tn], ALU.mult, [tb, rd], [tb])
            dv_ = dst[:, dcol, 0:tn] if dcol is not None else dst[:, 0:tn]
            P.tt(dv_, ta[:, 0:tn], tb[:, 0:tn], ALU.add, [ta, tb], [dst])
        P.store(o_dq.rearrange("(c p) n -> p c n", p=128)[:, :, s0:s0 + tn], s_dq, s_dq[:, :, 0:tn])
        P.store(o_dk[:, s0:s0 + tn], s_dk, s_dk[:, 0:tn])
        for sub in range(nsub):
            pv = nextp()
            n = proj_tok(pv, sub, 2272, 128, tn)
            P.copy(s_dv[0:n, sub, :], pv[0:n, 0:128], [pv], [s_dv], e="act")
            P.store(o_dv[s0 + sub * 128:s0 + sub * 128 + n, :], s_dv, s_dv[0:n, sub, :])
    return None


def emit_attn(P, d, scale, nq=SEQ, nk=LFULL):
    NBUF = 5
    NKT = nk // 128
    QW = min(512, nq)
    NQT = nq // QW
    qT = P.din("qT", [d, nq], BF16)
    kT = P.din("kT", [d, nk], BF16)
    v = P.din("v", [nk, 64], BF16)
    sel = P.din("sel", [65, 64])
    oT = P.dout("oT", [64, nq])
    qs = P.sb("qs", [128, nq], BF16)
    ks = P.sb("ks", [128, nk], BF16)
    vs = P.sb("vs", [128, NKT, 65], BF16)
    sels = P.sb("sels", [65, 64], F32)
    ones_b = P.sb("ones_b", [128, 128], BF16)
    sqb = [P.sb("sqb%d" % i, [128, 512], BF16) for i in range(2)]
    mx = P.sb("mx", [128, 8], F32)
    pts = [P.sb("pt%d" % i, [128, 512], BF16) for i in range(NBUF)]
    osb = [P.sb("osb%d" % i, [65, 512], F32) for i in range(2)]
    rec = P.sb("rec", [64, 512], F32)
    ob = [P.sb("ob%d" % i, [64, 512], F32) for i in range(2)]
    pss = [P.ps("pss%d" % i, [128, 512], F32) for i in range(NBUF)]
    pos = [P.ps("pos%d" % i, [128, 512], F32) for i in range(2)]
    pden = P.ps("pden", [128, 512], F32)

    P.op("dve", lambda: P.nc.vector.memset(ones_b[:, :], 1.0), [], [ones_b])
    P.op("dve", lambda: P.nc.vector.memset(vs[:, :, :], 1.0), [], [vs])
    P.op("dve", lambda: P.nc.vector.memset(mx[:, :], 0.0), [], [mx])
    if d < 128:
        P.op("pool", lambda: P.nc.gpsimd.memset(qs[64:128, :], 0.0), [], [qs])
        P.op("pool", lambda: P.nc.gpsimd.memset(ks[64:128, :], 0.0), [], [ks])
    P.load(sels, sels[:, :], sel)
    for c0 in range(0, nq, 4096):
        w_ = min(4096, nq - c0)
        P.load(qs, qs[0:d, c0:c0 + w_], qT[:, c0:c0 + w_])
    for c0 in range(0, nk, 4160):
        w_ = min(4160, nk - c0)
        P.load(ks, ks[0:d, c0:c0 + w_], kT[:, c0:c0 + w_])
    vv = v.rearrange("(t p) e -> p t e", p=128)
    for t0 in range(0, NKT, 26):
        n_ = min(26, NKT - t0)
        P.load(vs, vs[:, t0:t0 + n_, 0:64], vv[:, t0:t0 + n_, :])
    i = 0
    for (src, n, col) in ((qs, nq, 0), (ks, nk, 1)):
        for c0 in range(0, n, 512):
            w = min(512, n - c0)
            sq = sqb[i % 2]
            pp = pss[i % NBUF]
            P.actf(sq[:, 0:w], src[:, c0:c0 + w], AF.Square, [src], [sq])
            P.mm(pp[:, 0:w], ones_b[:, :], sq[:, 0:w], True, True, [ones_b, sq], [pp])
            P.op("dve", lambda: P.nc.vector.tensor_reduce(out=mx[:, 2:3], in_=pp[:, 0:w], axis=AX.X, op=ALU.max),
                 [pp], [mx])
            P.tt(mx[:, col:col + 1], mx[:, col:col + 1], mx[:, 2:3], ALU.max, [mx], [mx])
            i += 1
    P.tt(mx[:, 3:4], mx[:, 0:1], mx[:, 1:2], ALU.mult, [mx], [mx])
    P.actf(mx[:, 4:5], mx[:, 3:4], AF.Sqrt, [mx], [mx], scale=scale * scale)
    P.ts(mx[:, 5:6], mx[:, 4:5], -1.0, None, ALU.mult, None, [mx], [mx])
    steps = [(qt, kt) for qt in range(NQT) for kt in range(NKT)]
    n = len(steps)
    LA = 3
    deferred = {}

    def epilogue_a(qt):
        P.copy(osb[qt % 2][:, 0:QW], pos[qt % 2][0:65, 0:QW], [pos[qt % 2]], [osb[qt % 2]])

    def epilogue_b(qt):
        o_s = osb[qt % 2]
        o_b = ob[qt % 2]
        qsl = slice(qt * QW, (qt + 1) * QW)
        P.mm(pden[0:64, 0:QW], sels[:, :], o_s[:, 0:QW], True, True, [sels, o_s], [pden])
        P.op("dve", lambda: P.nc.vector.reciprocal(out=rec[:, 0:QW], in_=pden[0:64, 0:QW]), [pden], [rec])
        P.tt(o_b[:, 0:QW], o_s[0:64, 0:QW], rec[:, 0:QW], ALU.mult, [o_s, rec], [o_b])
        P.store(oT[:, qsl], o_b, o_b[:, 0:QW])

    for i in range(n + LA + 4):
        if i < n:
            qt, kt = steps[i]
            ps_ = pss[i % NBUF]
            pt = pts[i % NBUF]
            P.mm(ps_[:, 0:QW], ks[:, kt * 128:(kt + 1) * 128], qs[:, qt * QW:(qt + 1) * QW], True, True, [ks, qs], [ps_])
            P.actf(pt[:, 0:QW], ps_[:, 0:QW], AF.Exp, [ps_, mx], [pt], scale=scale, bias=mx[:, 5:6])
        j = i - LA
        if 0 <= j < n:
            qt, kt = steps[j]
            po = pos[qt % 2]
            P.mm(po[0:65, 0:QW], vs[:, kt, :], pts[j % NBUF][:, 0:QW], kt == 0, kt == NKT - 1, [vs, pts[j % NBUF]], [po])
            if kt == NKT - 1:
                epilogue_a(qt)
                deferred[i + 3] = qt
        if i in deferred:
            epilogue_b(deferred.pop(i))
    assert not deferred
    return None


S5C = 512


def emit_s5(P):
    TC = S5C
    uT = P.din("uT", [128, LFULL])
    prm = P.din("prm", [128, 4, 3])
    bri = P.din("bri", [128, 4, 2, 16])
    cblk = P.din("cblk", [128, 4, 2, 32])
    ident = P.din("ident", [128, 128])
    yT = P.dout("yT", [128, LFULL])

    pr = P.sb("pr", [128, 4, 3], F32)
    br = P.sb("br", [128, 4, 2, 16], F32)
    cb = P.sb("cb", [128, 4, 2, 32], F32)
    cbb = P.sb("cbb", [128, 4, 2, 32], BF16)
    idf = P.sb("idf", [128, 128], F32)
    w = P.sb("w", [128, 24, 4], F32)
    wblk = P.sb("wblk", [128, 4, 2, 32], F32)
    wT = P.sb("wT", [32, 4, 2, 128], BF16)
    Ec = P.sb("Ec", [128, 4, TC], F32)
    Es = P.sb("Es", [128, 4, TC], F32)
    rf = P.sb("rf", [128, 4, TC], F32)
    tsc = [P.sb("tsc%d" % i, [128, TC], F32) for i in range(4)]
    zero = P.sb("zero", [128, 1], F32)
    uf = [P.sb("uf%d" % i, [32, 4, TC], F32) for i in range(2)]
    ub = [P.sb("ub%d" % i, [32, 4, TC], BF16) for i in range(2)]
    bp = [P.sb("bp%d" % i, [128, 4, TC], F32) for i in range(2)]
    gg = [P.sb("gg%d" % i, [128, 4, TC], F32) for i in range(2)]
    xx = [P.sb("xx%d" % i, [128, 4, TC], F32) for i in range(2)]
    xb = [P.sb("xb%d" % i, [128, 4, TC], BF16) for i in range(2)]
    ysb = [P.sb("ysb%d" % i, [32, 4, TC], F32) for i in range(2)]
    pbu = [P.ps("pbu%d" % i, [128, 512], F32) for i in range(4)]
    py = [P.ps("py%d" % i, [128, 512], F32) for i in range(2)]
    ptr = P.ps("ptr", [128, 512], F32)

    P.load(pr, pr[:, :, :], prm)
    P.load(br, br[:, :, :, :], bri)
    P.load(cb, cb[:, :, :, :], cblk)
    P.load(idf, idf[:, :], ident)
    P.op("dve", lambda: P.nc.vector.memset(zero[:, :], 0.0), [], [zero])
    P.op("dve", lambda: P.nc.vector.memset(wblk[:, :, :, :], 0.0), [], [wblk])
    P.copy(cbb[:, :, 0, :], cb[:, :, 0, :], [cb], [cbb])
    P.ts(cbb[:, :, 1, :], cb[:, :, 1, :], -1.0, None, ALU.mult, None, [cb], [cbb])
    W = lambda k: w[:, k, :]
    rw = [w]
    P.ts(W(0), pr[:, :, 0], -1e-4, None, ALU.min, None, [pr], rw)
    P.actf(W(1), pr[:, :, 2], AF.Exp, [pr], rw)
    P.tt(W(2), W(0), W(1), ALU.mult, rw, rw)
    P.actf(W(3), W(2), AF.Exp, rw, rw)
    P.tt(W(4), pr[:, :, 1], W(1), ALU.mult, [pr] + rw, rw)
    P.actf(W(6), W(4), AF.Sin, rw, rw, scale=1.0 / 32)
    P.ts(W(7), W(4), 1.0 / 32, 0.5 * np.pi, ALU.mult, ALU.add, rw, rw)
    P.actf(W(5), W(7), AF.Sin, rw, rw)
    for _ in range(5):
        P.tt(W(7), W(5), W(5), ALU.mult, rw, rw)
        P.tt(W(8), W(6), W(6), ALU.mult, rw, rw)
        P.tt(W(9), W(5), W(6), ALU.mult, rw, rw)
        P.tt(W(5), W(7), W(8), ALU.subtract, rw, rw)
        P.ts(W(6), W(9), 2.0, None, ALU.mult, None, rw, rw)
    P.tt(W(10), W(3), W(5), ALU.mult, rw, rw)
    P.tt(W(11), W(3), W(6), ALU.mult, rw, rw)
    P.tt(W(12), W(0), W(0), ALU.mult, rw, rw)
    P.tt(W(13), pr[:, :, 1], pr[:, :, 1], ALU.mult, [pr], rw)
    P.tt(W(12), W(12), W(13), ALU.add, rw, rw)
    P.op("dve", lambda: P.nc.vector.reciprocal(out=W(12), in_=W(12)), rw, rw)
    P.ts(W(13), W(10), -1.0, None, ALU.add, None, rw, rw)
    P.tt(W(14), W(13), W(0), ALU.mult, rw, rw)
    P.tt(W(15), W(11), pr[:, :, 1], ALU.mult, [pr] + rw, rw)
    P.tt(W(14), W(14), W(15), ALU.add, rw, rw)
    P.tt(W(14), W(14), W(12), ALU.mult, rw, rw)
    P.tt(W(15), W(11), W(0), ALU.mult, rw, rw)
    P.tt(W(16), W(13), pr[:, :, 1], ALU.mult, [pr] + rw, rw)
    P.tt(W(15), W(15), W(16), ALU.subtract, rw, rw)
    P.tt(W(15), W(15), W(12), ALU.mult, rw, rw)
    P.ts(W(17), W(15), -1.0, None, ALU.mult, None, rw, rw)
    for ct in range(4):
        fre = w[:, 14, ct:ct + 1]
        fim = w[:, 15, ct:ct + 1]
        nfim = w[:, 17, ct:ct + 1]
        for half in range(2):
            rows = slice(half * 64, half * 64 + 64)
            cs_ = slice(half * 16, half * 16 + 16)
            P.ts(wblk[rows, ct, 0, cs_], br[rows, ct, 1, :], nfim[rows], None, ALU.mult, None, [br] + rw, [wblk])
            P.stt(wblk[rows, ct, 0, cs_], br[rows, ct, 0, :], fre[rows], wblk[rows, ct, 0, cs_], ALU.mult, ALU.add,
                  [br, wblk] + rw, [wblk])
            P.ts(wblk[rows, ct, 1, cs_], br[rows, ct, 0, :], fim[rows], None, ALU.mult, None, [br] + rw, [wblk])
            P.stt(wblk[rows, ct, 1, cs_], br[rows, ct, 1, :], fre[rows], wblk[rows, ct, 1, cs_], ALU.mult, ALU.add,
                  [br, wblk] + rw, [wblk])
        for ri in range(2):
            P.op("pe", lambda: P.nc.tensor.transpose(out=ptr[0:32, 0:128], in_=wblk[:, ct, ri, :], identity=idf[:, :]),
                 [wblk, idf], [ptr])
            P.copy(wT[:, ct, ri, :], ptr[0:32, 0:128], [ptr], [wT])
        P.copy(Ec[:, ct, 0:1], w[:, 5, ct:ct + 1], rw, [Ec])
        P.copy(Es[:, ct, 0:1], w[:, 6, ct:ct + 1], rw, [Es])
        k = 1
        while k < TC:
            ck = Ec[:, ct, k - 1:k]
            sk = Es[:, ct, k - 1:k]
            t0_, t1_ = tsc[0], tsc[1]
            P.ts(t0_[:, 0:k], Es[:, ct, 0:k], sk, None, ALU.mult, None, [Es], [t0_])
            P.ts(t1_[:, 0:k], Es[:, ct, 0:k], ck, None, ALU.mult, None, [Es, Ec], [t1_])
            P.stt(Ec[:, ct, k:2 * k], Ec[:, ct, 0:k], ck, t0_[:, 0:k], ALU.mult, ALU.subtract, [Ec, t0_], [Ec])
            P.stt(Es[:, ct, k:2 * k], Ec[:, ct, 0:k], sk, t1_[:, 0:k], ALU.mult, ALU.add, [Ec, Es, t1_], [Es])
            k *= 2
        P.ts(rf[:, ct, :], Ec[:, ct, :], 0.0, w[:, 3, ct:ct + 1], ALU.mult, ALU.add, [Ec] + rw, [rf])
    uv = uT.rearrange("(ct p) n -> p ct n", p=32)
    yv = yT.rearrange("(ct p) n -> p ct n", p=32)
    chunks = [(c0, min(TC, LFULL - c0)) for c0 in range(0, LFULL, TC)]
    prev = None
    for ci, (c0, tn) in enumerate(chunks):
        u_f = uf[ci % 2]
        u_b = ub[ci % 2]
        y_s = ysb[ci % 2]
        P.load(u_f, u_f[:, :, 0:tn], uv[:, :, c0:c0 + tn])
        P.copy(u_b[:, :, 0:tn], u_f[:, :, 0:tn], [u_f], [u_b], e="pool")
        for ct in range(4):
            pre = pbu[(2 * ct) % 4]
            pim = pbu[(2 * ct + 1) % 4]
            P.mm(pre[:, 0:tn], wT[:, ct, 0, :], u_b[:, ct, 0:tn], True, True, [wT, u_b], [pre])
            P.mm(pim[:, 0:tn], wT[:, ct, 1, :], u_b[:, ct, 0:tn], True, True, [wT, u_b], [pim])
            c_ = Ec[:, ct, 0:tn]
            s_ = Es[:, ct, 0:tn]
            t0_, t1_, t2_, t3_ = tsc
            P.tt(t0_[:, 0:tn], pre[:, 0:tn], c_, ALU.mult, [pre, Ec], [t0_])
            P.tt(t1_[:, 0:tn], pim[:, 0:tn], s_, ALU.mult, [pim, Es], [t1_])
            P.tt(bp[0][:, ct, 0:tn], t0_[:, 0:tn], t1_[:, 0:tn], ALU.add, [t0_, t1_], [bp[0]])
            P.tt(t2_[:, 0:tn], pim[:, 0:tn], c_, ALU.mult, [pim, Ec], [t2_])
            P.tt(t3_[:, 0:tn], pre[:, 0:tn], s_, ALU.mult, [pre, Es], [t3_])
            P.tt(bp[1][:, ct, 0:tn], t2_[:, 0:tn], t3_[:, 0:tn], ALU.subtract, [t2_, t3_], [bp[1]])
        for ct in range(4):
            for ri in range(2):
                init = zero[:, 0:1] if prev is None else xx[ri][:, ct, prev - 1:prev]
                P.op("dve", lambda: P.nc.vector.tensor_tensor_scan(
                    out=gg[ri][:, ct, 0:tn], data0=rf[:, ct, 0:tn], data1=bp[ri][:, ct, 0:tn], initial=init,
                    op0=ALU.mult, op1=ALU.add), [rf, bp[ri], xx[ri], zero], [gg[ri]])
        for ct in range(4):
            c_ = Ec[:, ct, 0:tn]
            s_ = Es[:, ct, 0:tn]
            t0_, t1_, t2_, t3_ = tsc
            P.tt(t0_[:, 0:tn], gg[0][:, ct, 0:tn], c_, ALU.mult, [gg[0], Ec], [t0_])
            P.tt(t1_[:, 0:tn], gg[1][:, ct, 0:tn], s_, ALU.mult, [gg[1], Es], [t1_])
            P.tt(xx[0][:, ct, 0:tn], t0_[:, 0:tn], t1_[:, 0:tn], ALU.subtract, [t0_, t1_], [xx[0]])
            P.tt(t2_[:, 0:tn], gg[0][:, ct, 0:tn], s_, ALU.mult, [gg[0], Es], [t2_])
            P.tt(t3_[:, 0:tn], gg[1][:, ct, 0:tn], c_, ALU.mult, [gg[1], Ec], [t3_])
            P.tt(xx[1][:, ct, 0:tn], t2_[:, 0:tn], t3_[:, 0:tn], ALU.add, [t2_, t3_], [xx[1]])
            P.copy(xb[0][:, ct, 0:tn], xx[0][:, ct, 0:tn], [xx[0]], [xb[0]], e="act")
            P.copy(xb[1][:, ct, 0:tn], xx[1][:, ct, 0:tn], [xx[1]], [xb[1]], e="act")
            pp = py[ct % 2]
            P.mm(pp[0:32, 0:tn], cbb[:, ct, 0, :], xb[0][:, ct, 0:tn], True, False, [cbb, xb[0]], [pp])
            P.mm(pp[0:32, 0:tn], cbb[:, ct, 1, :], xb[1][:, ct, 0:tn], False, True, [cbb, xb[1]], [pp])
            P.copy(y_s[:, ct, 0:tn], pp[0:32, 0:tn], [pp], [y_s], e="act")
        P.store(yv[:, :, c0:c0 + tn], y_s, y_s[:, :, 0:tn])
        prev = tn
    return None


def emit_hgrn(P):
    SP = 512
    qT = P.din("qT", [128, LFULL])
    kT = P.din("kT", [128, LFULL])
    lT = P.din("lT", [128, LFULL])
    v = P.din("v", [LFULL, 128], BF16)
    identb = P.din("identb", [64, 64])
    mask = P.din("mask", [64, 64])
    oT = P.dout("oT", [128, LFULL])

    idf = P.sb("idf", [64, 64], F32)
    idb = P.sb("idb", [64, 64], BF16)
    mk = P.sb("mk", [64, 128], F32)
    ones = P.sb("ones", [64, 2 * SP], F32)
    zero = P.sb("zero", [64, 1], F32)
    S = P.sb("S", [64, 2, 64], F32)
    Sb = P.sb("Sb", [64, 2, 64], BF16)
    Stmp = P.sb("Stmp", [64, 2, 64], F32)
    Mm = P.sb("Mm", [64, 2, 8], F32)
    em = P.sb("em", [64, 2, 8], F32)
    el = P.sb("el", [64, 2, 8], F32)
    attc = P.sb("attc", [64, 128], F32)
    qs = [P.sb("qs%d" % i, [64, 2, SP], F32) for i in range(2)]
    ks = [P.sb("ks%d" % i, [64, 2, SP], F32) for i in range(2)]
    ls = [P.sb("ls%d" % i, [64, 2, SP], F32) for i in range(2)]
    vs = [P.sb("vs%d" % i, [64, 8, 128], BF16) for i in range(2)]
    G = P.sb("G", [64, 2, SP], F32)
    Gc = P.sb("Gc", [64, 2, SP], F32)
    e1 = P.sb("e1", [64, 2, SP], F32)
    e2 = P.sb("e2", [64, 2, SP], F32)
    qb = P.sb("qb", [64, 2, SP], BF16)
    kb = P.sb("kb", [64, 2, SP], BF16)
    ktok = [P.sb("ktok%d" % i, [64, 128], BF16) for i in range(2)]
    attb = [P.sb("attb%d" % i, [64, 128], BF16) for i in range(2)]
    osb = [P.sb("osb%d" % i, [64, 2, SP], F32) for i in range(2)]
    pkt = [P.ps("pkt%d" % i, [64, 1024], BF16) for i in range(1)]
    patt = [P.ps("patt%d" % i, [64, 512], F32) for i in range(2)]
    po = [P.ps("po%d" % i, [64, 512], F32) for i in range(2)]
    pS = P.ps("pS", [64, 512], F32)

    P.load(idf, idf[:, :], identb)
    P.copy(idb[:, :], idf[:, :], [idf], [idb])
    P.load(mk, mk[:, 0:64], mask)
    P.load(mk, mk[:, 64:128], mask)
    P.op("dve", lambda: P.nc.vector.memset(ones[:, :], 1.0), [], [ones])
    P.op("dve", lambda: P.nc.vector.memset(zero[:, :], 0.0), [], [zero])
    P.op("dve", lambda: P.nc.vector.memset(S[:, :, :], 0.0), [], [S])
    P.op("dve", lambda: P.nc.vector.memset(Sb[:, :, :], 0.0), [], [Sb])
    qv = qT.rearrange("(h p) n -> p h n", p=64)
    kv = kT.rearrange("(h p) n -> p h n", p=64)
    lv = lT.rearrange("(h p) n -> p h n", p=64)
    ov = oT.rearrange("(h p) n -> p h n", p=64)
    vv = v.rearrange("(c p) e -> p c e", p=64)
    spans = [(c0, min(SP, LFULL - c0)) for c0 in range(0, LFULL, SP)]
    ch = 0
    for si, (c0, tn) in enumerate(spans):
        q_, k_, l_, v_ = qs[si % 2], ks[si % 2], ls[si % 2], vs[si % 2]
        o_ = osb[si % 2]
        nch = tn // 64
        P.load(q_, q_[:, :, 0:tn], qv[:, :, c0:c0 + tn])
        P.load(k_, k_[:, :, 0:tn], kv[:, :, c0:c0 + tn])
        P.load(l_, l_[:, :, 0:tn], lv[:, :, c0:c0 + tn])
        P.load(v_, v_[:, 0:nch, :], vv[:, c0 // 64:c0 // 64 + nch, :])
        for h in range(2):
            P.op("dve", lambda: P.nc.vector.tensor_tensor_scan(
                out=G[:, h, 0:tn], data0=ones[:, 0:tn], data1=l_[:, h, 0:tn], initial=zero[:, 0:1],
                op0=ALU.mult, op1=ALU.add), [ones, l_, zero], [G])
        for j in range(nch):
            for h in range(2):
                mid = j * 64 + 31
                P.ts(Gc[:, h, j * 64:(j + 1) * 64], G[:, h, j * 64:(j + 1) * 64], G[:, h, mid:mid + 1], None,
                     ALU.subtract, None, [G], [Gc])
        G4 = G[:, :, 0:nch * 64].rearrange("p h (j t) -> p h j t", t=64)
        P.copy(Mm[:, :, 0:1], G4[:, :, 0:1, 31], [G], [Mm])
        if nch > 1:
            P.tt(Mm[:, :, 1:nch], G4[:, :, 1:nch, 31], G4[:, :, 0:nch - 1, 63], ALU.subtract, [G], [Mm])
        P.actf(e1[:, :, 0:tn], Gc[:, :, 0:tn], AF.Exp, [Gc], [e1])
        P.actf(e2[:, :, 0:tn], Gc[:, :, 0:tn], AF.Exp, [Gc], [e2], scale=-1.0)
        P.actf(em[:, :, 0:nch], Mm[:, :, 0:nch], AF.Exp, [Mm], [em])
        e14 = e1[:, :, 0:nch * 64].rearrange("p h (j t) -> p h j t", t=64)
        P.tt(el[:, :, 0:nch], em[:, :, 0:nch], e14[:, :, :, 63], ALU.mult, [em, e1], [el])
        P.tt(qb[:, :, 0:tn], q_[:, :, 0:tn], e1[:, :, 0:tn], ALU.mult, [q_, e1], [qb])
        P.tt(kb[:, :, 0:tn], k_[:, :, 0:tn], e2[:, :, 0:tn], ALU.mult, [k_, e2], [kb])
        for j in range(nch):
            cs_ = slice(j * 64, (j + 1) * 64)
            kt_ = ktok[ch % 2]
            ab = attb[ch % 2]
            pa = patt[ch % 2]
            pp = po[ch % 2]
            pk = pkt[0]
            for h in range(2):
                P.op("pe", lambda: P.nc.tensor.transpose(out=pk[0:64, h * 64:(h + 1) * 64], in_=kb[:, h, cs_],
                                                         identity=idb[:, :]), [kb, idb], [pk])
            P.copy(kt_[:, :], pk[0:64, 0:128], [pk], [kt_], e="act")
            for h in range(2):
                P.mm(pa[0:64, h * 64:(h + 1) * 64], kb[:, h, cs_], qb[:, h, cs_], True, True, [kb, qb], [pa])
            P.ts(attc[:, :], pa[0:64, 0:128], 3.0e38, -3.0e38, ALU.min, ALU.max, [pa], [attc])
            P.tt(ab[:, :], attc[:, :], mk[:, :], ALU.mult, [attc, mk], [ab])
            for h in range(2):
                P.ts(Sb[:, h, :], S[:, h, :], em[:, h, j:j + 1], None, ALU.mult, None, [S, em], [Sb])
            for h in range(2):
                P.mm(pp[0:64, h * 64:(h + 1) * 64], v_[:, j, h * 64:(h + 1) * 64], ab[:, h * 64:(h + 1) * 64],
                     True, False, [v_, ab], [pp])
                P.mm(pp[0:64, h * 64:(h + 1) * 64], Sb[:, h, :], qb[:, h, cs_], False, True, [Sb, qb], [pp])
            P.copy(o_[:, :, cs_], pp[0:64, 0:128].rearrange("p (h t) -> p h t", h=2), [pp], [o_], e="act")
            for h in range(2):
                P.mm(pS[0:64, h * 64:(h + 1) * 64], kt_[:, h * 64:(h + 1) * 64], v_[:, j, h * 64:(h + 1) * 64],
                     True, True, [kt_, v_], [pS])
            for h in range(2):
                P.ts(Stmp[:, h, :], pS[0:64, h * 64:(h + 1) * 64], e1[:, h, j * 64 + 63:j * 64 + 64], None, ALU.mult, None,
                     [pS, e1], [Stmp])
                P.stt(S[:, h, :], S[:, h, :], el[:, h, j:j + 1], Stmp[:, h, :], ALU.mult, ALU.add, [S, el, Stmp], [S])
            ch += 1
        P.store(ov[:, :, c0:c0 + tn], o_, o_[:, :, 0:tn])
    return None


def emit_merge(P, x_in=None):
    T = 256
    xT = x_in if x_in is not None else P.din("xT", [D, NTOK])
    wg = P.din("wg", [D, 4096])
    wbr = P.din("wbr", [1024, D])
    wout = P.din("wout", [D, D])
    wglu = P.din("wglu", [256, 256])
    gp = P.din("gp", [128, 8, 2])
    modc = P.din("modc", [128, 8, 6])
    pcols = P.din("pcols", [128, 4])
    blk = P.din("blk", [128, 128])
    bin_ = P.din("bin", [8, 256, NTOK])
    outT = P.dout("outT", [D, NTOK])

    wgb = P.sb("wgb", [128, 8, 4096], BF16)
    wbrb = P.sb("wbrb", [128, 8, D], BF16)
    woutb = P.sb("woutb", [128, 8, D], BF16)
    wglub = P.sb("wglub", [128, 2, 256], BF16)
    stage = [P.sb("stg%d" % i, [128, 1024], F32) for i in range(2)]
    ones_b = P.sb("ones_b", [128, 128], BF16)
    blk_f = P.sb("blk_f", [128, 128], F32)
    blk_b = P.sb("blk_b", [128, 128], BF16)
    P.eps_col = P.sb("eps_col", [128, 1], F32)
    gps = P.sb("gps", [128, 8, 2], F32)
    mods = P.sb("mods", [128, 8, 6], F32)
    cols = P.sb("cols", [128, 8, 6], F32)
    pc = P.sb("pc", [128, 4], F32)
    P.op("dve", lambda: P.nc.vector.memset(ones_b[:, :], 1.0), [], [ones_b])
    P.op("dve", lambda: P.nc.vector.memset(P.eps_col[:, :], EPS), [], [P.eps_col])
    P.load(gps, gps[:, :, :], gp)
    P.load(mods, mods[:, :, :], modc)
    P.load(pc, pc[:, :], pcols)
    P.load(blk_f, blk_f[:, :], blk)
    P.copy(blk_b[:, :], blk_f[:, :], [blk_f], [blk_b])
    mod_cols(P, cols, mods, gps, True, 1.0, 1)
    load_cast_weight(P, wgb, lambda c, c0, w: wgb[:, c, c0:c0 + w], wg, 8, 4096, stage, piece=1024)
    load_cast_weight(P, wbrb, lambda c, c0, w: wbrb[:, c, c0:c0 + w], wbr, 8, D, stage, piece=1024)
    load_cast_weight(P, woutb, lambda c, c0, w: woutb[:, c, c0:c0 + w], wout, 8, D, stage, piece=1024)
    load_cast_weight(P, wglub, lambda c, c0, w: wglub[:, c, c0:c0 + w], wglu, 2, 256, stage, piece=256)

    xs = [P.sb("x%d" % i, [128, 8, T], F32) for i in range(2)]
    bi = [P.sb("bi%d" % i, [128, 16, T], F32) for i in range(2)]
    sq = P.sb("sq", [128, 8, T], BF16)
    rstd = P.sb("rstd", [128, T], F32)
    rs2 = P.sb("rs2", [128, T], F32)
    tmp = [P.sb("tmp%d" % i, [128, T], F32) for i in range(2)]
    t3 = [P.sb("t3_%d" % i, [128, T], F32) for i in range(3)]
    hb = P.sb("hb", [128, 8, T], BF16)
    gf = P.sb("gf", [128, 2, T], F32)
    gbf = P.sb("gbf", [128, 2, T], BF16)
    yb = P.sb("yb", [128, 8, T], BF16)
    sg = [P.sb("sg%d" % i, [128, T], F32) for i in range(2)]
    acc = P.sb("acc", [128, T], F32)
    mb = P.sb("mb", [128, 8, T], BF16)
    ys = P.sb("ys", [128, 8, T], F32)
    ss_ps = P.ps("ss_ps", [128, 512], F32)
    pg = [P.ps("pg%d" % i, [128, 512], F32) for i in range(3)]
    pb = [P.ps("pb%d" % i, [128, 512], F32) for i in range(3)]

    xTv = xT.rearrange("(c p) n -> p c n", p=128)
    oTv = outT.rearrange("(c p) n -> p c n", p=128)
    bv = bin_.rearrange("k (c p) n -> p (k c) n", p=128)
    for ti, (s0, tn, ty) in enumerate(token_tiles(T)):
        x = xs[ti % 2]
        b_ = bi[ti % 2]
        co = 3 * ty
        P.load(x, x[:, :, 0:tn], xTv[:, :, s0:s0 + tn])
        P.load(b_, b_[:, 0:8, 0:tn], bv[:, 0:8, s0:s0 + tn])
        P.load(b_, b_[:, 8:16, 0:tn], bv[:, 8:16, s0:s0 + tn])
        make_h(P, x, tn, cols, co, ones_b, sq, ss_ps, rstd, tmp, hb)
        B = lambda k, c: b_[:, k * 2 + c, 0:tn]
        for c in range(2):
            t = t3[c]
            P.stt(t[:, 0:tn], B(0, c), pc[:, c:c + 1], B(1, c), ALU.mult, ALU.add, [b_, pc], [t])
            P.tt(t[:, 0:tn], t[:, 0:tn], B(2, c), ALU.add, [t, b_], [t])
            P.actf(gf[:, c, 0:tn], t[:, 0:tn], AF.Gelu, [t], [gf])
            P.copy(gbf[:, c, 0:tn], gf[:, c, 0:tn], [gf], [gbf])
        for c in range(2):
            p = pg[c]
            for kc in range(2):
                P.mm(p[:, 0:tn], wglub[:, kc, c * 128:(c + 1) * 128], gbf[:, kc, 0:tn], kc == 0, kc == 1, [wglub, gbf], [p])
            s = sg[c]
            P.actf(s[:, 0:tn], p[:, 0:tn], AF.Sigmoid, [p], [s])
            P.tt(yb[:, 0 + c, 0:tn], gf[:, c, 0:tn], s[:, 0:tn], ALU.mult, [gf, s], [yb])
        for c in range(2):
            P.copy(yb[:, 2 + c, 0:tn], B(3, c), [b_], [yb], e="pool")
            P.copy(yb[:, 6 + c, 0:tn], B(7, c), [b_], [yb], e="pool")
        for c in range(2):
            t = t3[c]
            P.tt(t[:, 0:tn], B(4, c), B(5, c), ALU.add, [b_], [t])
            P.actf(sq[:, c, 0:tn], t[:, 0:tn], AF.Square, [t], [sq])
            P.mm(ss_ps[:, 0:tn], blk_b[:, :], sq[:, c, 0:tn], True, True, [blk_b, sq], [ss_ps])
            P.actf(rs2[:, 0:tn], ss_ps[:, 0:tn], AF.Sqrt, [ss_ps], [rs2], scale=1.0 / 64, bias=P.eps_col[:, 0:1])
            P.op("dve", lambda: P.nc.vector.reciprocal(out=rs2[:, 0:tn], in_=rs2[:, 0:tn]), [rs2], [rs2])
            P.stt(t[:, 0:tn], t[:, 0:tn], pc[:, 2:3], rs2[:, 0:tn], ALU.mult, ALU.mult, [t, pc, rs2], [t])
            s = sg[c]
            P.actf(s[:, 0:tn], B(6, c), AF.Silu, [b_], [s])
            P.tt(yb[:, 4 + c, 0:tn], t[:, 0:tn], s[:, 0:tn], ALU.mult, [t, s], [yb])
        for k in range(8):
            for i in range(4):
                pg_ = pg[(k * 4 + i) % 3]
                pb_ = pb[(k * 4 + i) % 3]
                for c in range(8):
                    P.mm(pg_[:, 0:tn], wgb[:, c, i * 1024 + k * 128:i * 1024 + (k + 1) * 128], hb[:, c, 0:tn],
                         c == 0, c == 7, [wgb, hb], [pg_])
                for kc in range(2):
                    P.mm(pb_[:, 0:tn], wbrb[:, i * 2 + kc, k * 128:(k + 1) * 128], yb[:, i * 2 + kc, 0:tn],
                         kc == 0, kc == 1, [wbrb, yb], [pb_])
                s = sg[i % 2]
                P.actf(s[:, 0:tn], pg_[:, 0:tn], AF.Sigmoid, [pg_], [s])
                if i == 0:
                    P.tt(acc[:, 0:tn], s[:, 0:tn], pb_[:, 0:tn], ALU.mult, [s, pb_], [acc])
                else:
                    t = t3[i % 3]
                    P.tt(t[:, 0:tn], s[:, 0:tn], pb_[:, 0:tn], ALU.mult, [s, pb_], [t])
                    if i < 3:
                        P.tt(acc[:, 0:tn], acc[:, 0:tn], t[:, 0:tn], ALU.add, [acc, t], [acc], e="pool")
                    else:
                        P.tt(mb[:, k, 0:tn], acc[:, 0:tn], t[:, 0:tn], ALU.add, [acc, t], [mb], e="pool")
        for k in range(8):
            p = pg[k % 3]
            for c in range(8):
                P.mm(p[:, 0:tn], woutb[:, c, k * 128:(k + 1) * 128], mb[:, c, 0:tn], c == 0, c == 7, [woutb, mb], [p])
            P.copy(ys[:, k, 0:tn], p[:, 0:tn], [p], [ys], e="act")
        rms_rstd(P, lambda c: ys[:, c, 0:tn], 8, tn, ones_b, sq, ss_ps, rstd, 1.0 / D, [ys])
        for c in range(8):
            t = tmp[c % 2]
            P.stt(t[:, 0:tn], ys[:, c, 0:tn], cols[:, c, co + 2:co + 3], rstd[:, 0:tn], ALU.mult, ALU.mult,
                  [ys, cols, rstd], [t])
            P.tt(x[:, c, 0:tn], t[:, 0:tn], x[:, c, 0:tn], ALU.add, [t, x], [x], e="pool")
        P.store(oTv[:, :, s0:s0 + tn], x, x[:, :, 0:tn])
    return outT


_PROGS = {}


def prog(name, fn, *a):
    key = (name,) + a
    if key not in _PROGS:
        _PROGS[key] = fn(*a)
    return _PROGS[key]


def colz(v):
    return np.ascontiguousarray(np.asarray(v, np.float32).reshape(-1, 128).T)


def core_tokens(lat, ctx, i):
    b, seg = i // 4, i % 4
    return np.concatenate([lat[b, seg * LTOK:(seg + 1) * LTOK], ctx[b, seg * CTOK:(seg + 1) * CTOK]], axis=0)


def gather_fm(outs, F):
    dt = outs[0].dtype
    lat = np.empty((2, SEQ, F), dt)
    ctx = np.empty((2, CTX, F), dt)
    for i, o in enumerate(outs):
        b, seg = i // 4, i % 4
        lat[b, seg * LTOK:(seg + 1) * LTOK] = o[:, :LTOK].T
        ctx[b, seg * CTOK:(seg + 1) * CTOK] = o[:, LTOK:].T
    return lat, ctx


def gather_tm(outs, F):
    dt = outs[0].dtype
    lat = np.empty((2, SEQ, F), dt)
    ctx = np.empty((2, CTX, F), dt)
    for i, o in enumerate(outs):
        b, seg = i // 4, i % 4
        lat[b, seg * LTOK:(seg + 1) * LTOK] = o[:LTOK]
        ctx[b, seg * CTOK:(seg + 1) * CTOK] = o[LTOK:]
    return lat, ctx


def rope_perm(r):
    q = r // 4
    j = np.arange(r)
    return np.where((j % (2 * q)) < q, j + q, j - q)


def rope_tables(r, pos):
    q = r // 4
    half = r // 2
    inv = (10000.0 ** (-np.arange(0, half, 2, dtype=np.float32) / half)).astype(np.float32)
    rows = (pos // 64).astype(np.float32)
    cols_ = (pos % 64).astype(np.float32)
    ang_r = rows[:, None] * inv
    ang_c = cols_[:, None] * inv
    ang = np.concatenate([ang_r, ang_r, ang_c, ang_c], axis=-1).astype(np.float32)
    cos = np.cos(ang).T
    sin = np.sin(ang).T
    j = np.arange(r)
    sgn = np.where((j % (2 * q)) < q, -1.0, 1.0)[:, None]
    return cos.astype(np.float32), (sin * sgn).astype(np.float32)


def scan_order(ctx, lat, d):
    if d == 0:
        return np.concatenate([ctx, lat], axis=0)
    return np.concatenate([ctx[::-1], lat[::-1]], axis=0)


def unscan(y, d):
    c, l = y[:CTX], y[CTX:]
    if d == 1:
        c, l = c[::-1], l[::-1]
    return c, l


DBG = {}


def build_A0():
    P = Prog("A0")
    with P.stage("f0_"):
        x1 = emit_ffn(P)
    with P.stage("p_"):
        emit_proj(P, x_in=x1)
    return P.finish()


def build_mix(with_ctx):
    P = Prog("mix")
    with P.stage("am_"):
        emit_attn(P, 96, 96 ** -0.5)
    with P.stage("ag_"):
        emit_attn(P, 64, 0.125)
    if with_ctx:
        with P.stage("cm_"):
            emit_attn(P, 96, 96 ** -0.5, CTX, CTX)
        with P.stage("cg_"):
            emit_attn(P, 64, 0.125, CTX, CTX)
    with P.stage("s5_"):
        emit_s5(P)
    with P.stage("hg_"):
        emit_hgrn(P)
    return P.finish()


def build_A1():
    P = Prog("A1")
    with P.stage("m_"):
        x2 = emit_merge(P)
    with P.stage("f2_"):
        x3 = emit_ffn(P, x_in=x2)
    with P.stage("f0_"):
        x1 = emit_ffn(P, x_in=x3)
    with P.stage("p_"):
        emit_proj(P, x_in=x1)
    return P.finish()


def build_A2():
    P = Prog("A2")
    with P.stage("m_"):
        x2 = emit_merge(P)
    with P.stage("f2_"):
        emit_ffn(P, x_in=x2)
    return P.finish()


def kernel(x, c, ctx, c_ctx, w_ada, b_ada, norm_pre, norm_post, ffn_w1, ffn_w3, ffn_w2, w_in,
           s5_lambda_re, s5_lambda_im, s5_log_dt, s5_b_re, s5_b_im, s5_c_re, s5_c_im, s5_d, s5_w_glu,
           mla_q_norm, mla_w_uq, mla_kv_norm, mla_w_ukv, hgrn_lb_raw, hgrn_o_norm,
           gqa_q_norm, gqa_k_norm, w_branch, w_out):
    f32 = np.float32
    A = lambda a: np.ascontiguousarray(np.asarray(a))
    x, ctx = A(x), A(ctx)
    L = w_ada.shape[0]
    cT = np.stack([colz(c[0]), colz(c[1]), colz(c_ctx)], axis=-1)
    Wall = np.concatenate([A(w_ada[l]) for l in range(L)], axis=1)
    ball = np.concatenate([A(b_ada[l]) for l in range(L)], axis=0)
    ims = []
    for i in range(NCORE):
        sl = slice(i * MODC, (i + 1) * MODC)
        ims.append(dict(cT=A(cT), W=A(Wall[:, sl]), bias=A(np.broadcast_to(ball[sl], (3, MODC)))))
    res = run(prog("mod", build_mod), ims)
    mod = np.concatenate([r["out"] for r in res], axis=1).reshape(3, L, 9, D)

    def modc_for(i, l, j):
        b = i // 4
        vs_ = [mod[b, l, 3 * j + k] for k in range(3)] + [mod[2, l, 3 * j + k] for k in range(3)]
        return A(np.stack([colz(v_) for v_ in vs_], axis=-1))

    def gp_for(g0, g1):
        return A(np.stack([colz(g0), colz(g1)], axis=-1))

    def ffn_in(pre, i, l, j, jj):
        return {pre + "w1": A(ffn_w1[l, jj]), pre + "w3": A(ffn_w3[l, jj]), pre + "w2": A(ffn_w2[l, jj]),
                pre + "gp": gp_for(norm_pre[l, j], norm_post[l, j]), pre + "modc": modc_for(i, l, j)}

    blk = np.zeros((128, 128), f32)
    blk[:64, :64] = 1
    blk[64:, 64:] = 1
    sel = np.zeros((65, 64), f32)
    sel[64] = 1
    p32, p64 = rope_perm(32), rope_perm(64)
    rope_cache = {}

    def proj_in(pre, i, l):
        W = A(w_in[l])
        key = ("w", l)
        if key not in rope_cache:
            wext = A(np.concatenate([W[:, :2400], W[:, 576:608][:, p32],
                                     W[:, 1888:2144].reshape(D, 4, 64)[:, :, p64].reshape(D, 256),
                                     W[:, 2144:2272].reshape(D, 2, 64)[:, :, p64].reshape(D, 128)], axis=1))
            wuq = np.zeros((192, 512), f32)
            uq = A(mla_w_uq[l]).reshape(192, 4, 96)
            for h in range(4):
                wuq[:, h * 128:h * 128 + 96] = uq[:, h]
                wuq[:, h * 128 + 96:h * 128 + 128] = uq[:, h, 64:][:, p32]
            ukv = A(mla_w_ukv[l]).reshape(128, 4, 128)
            wukv = A(np.concatenate([ukv[:, :, :64].reshape(128, 256), ukv[:, :, 64:].reshape(128, 256)], axis=1))
            pcols = np.zeros((128, NPC), f32)
            pcols[:, 0] = mla_q_norm[l][:128]
            pcols[:64, 1] = mla_q_norm[l][128:]
            pcols[:, 2] = mla_kv_norm[l]
            pcols[:, 3] = np.tile(gqa_q_norm[l], 2)
            pcols[:, 4] = np.tile(np.asarray(gqa_q_norm[l])[p64], 2)
            pcols[:, 5] = np.tile(gqa_k_norm[l], 2)
            pcols[:, 6] = np.tile(np.asarray(gqa_k_norm[l])[p64], 2)
            for d_ in range(2):
                for l2 in range(2):
                    for c_ in range(2):
                        pcols[:, 7 + (d_ * 2 + l2) * 2 + c_] = hgrn_lb_raw[d_, l2, c_ * 128:(c_ + 1) * 128]
            pcols[:, 15] = 1.0 if l == 0 else 0.0
            pcols[:, 16] = 1.0 if l == 1 else 0.0
            rope_cache[key] = (wext, wuq, wukv, pcols)
        wext, wuq, wukv, pcols = rope_cache[key]
        seg = i % 4
        if ("r", seg) not in rope_cache:
            pos = np.arange(seg * LTOK, (seg + 1) * LTOK)
            rb = np.zeros((32, 2, NTOK), f32)
            rd = np.zeros((128, 2, NTOK), f32)
            rb[:, 0, LTOK:] = 1.0
            rd[:, 0, LTOK:] = 1.0
            cb_, sb_ = rope_tables(32, pos)
            cd_, sd_ = rope_tables(64, pos)
            rb[:, 0, :LTOK], rb[:, 1, :LTOK] = cb_, sb_
            rd[:, 0, :LTOK], rd[:, 1, :LTOK] = np.tile(cd_, (2, 1)), np.tile(sd_, (2, 1))
            rope_cache[("r", seg)] = (rb, rd)
        rb, rd = rope_cache[("r", seg)]
        return {pre + "wext": wext, pre + "gp": gp_for(norm_pre[l, 1], norm_post[l, 1]), pre + "modc": modc_for(i, l, 1),
                pre + "pcols": pcols, pre + "wuq": wuq, pre + "wukv": wukv, pre + "ropeb": rb, pre + "roped": rd,
                pre + "blk": blk}

    XT = [A(core_tokens(x, ctx, i).T) for i in range(NCORE)]
    ims = []
    for i in range(NCORE):
        m = ffn_in("f0_", i, 0, 0, 0)
        m["f0_xT"] = XT[i]
        m.update(proj_in("p_", i, 0))
        ims.append(m)
    pres = run(prog("A0", build_A0), ims)
    out_lat = None
    for l in range(L):
        last = l == L - 1
        X1 = [r["f0_outT"] for r in pres]
        u_l, u_c = gather_fm([r["p_o_u"] for r in pres], 256)
        mq_l, mq_c = gather_fm([r["p_o_mq"].reshape(384, NTOK) for r in pres], 384)
        mk_l, mk_c = gather_fm([r["p_o_mk"].reshape(384, NTOK) for r in pres], 384)
        mv_l, mv_c = gather_tm([r["p_o_mv"] for r in pres], 256)
        hq_l, hq_c = gather_fm([r["p_o_hq"] for r in pres], 256)
        hk_l, hk_c = gather_fm([r["p_o_hk"].reshape(512, NTOK) for r in pres], 512)
        hl_l, hl_c = gather_fm([r["p_o_hl"].reshape(512, NTOK) for r in pres], 512)
        hv_l, hv_c = gather_tm([r["p_o_hv"] for r in pres], 256)
        hg_l, hg_c = gather_fm([r["p_o_hg"] for r in pres], 256)
        dq_l, dq_c = gather_fm([r["p_o_dq"] for r in pres], 256)
        dk_l, dk_c = gather_fm([r["p_o_dk"] for r in pres], 128)
        dv_l, dv_c = gather_tm([r["p_o_dv"] for r in pres], 128)
        mask = np.triu(np.ones((64, 64), f32))
        ims = []
        for i in range(NCORE):
            b, h = i // 4, i % 4
            kvh = h // 2
            m = {}
            m["am_qT"] = A(mq_l[b][:, h * 96:(h + 1) * 96].T)
            m["am_kT"] = A(np.concatenate([mk_c[b], mk_l[b]], 0)[:, h * 96:(h + 1) * 96].T)
            m["am_v"] = A(np.concatenate([mv_c[b], mv_l[b]], 0)[:, h * 64:(h + 1) * 64])
            m["am_sel"] = sel
            m["ag_qT"] = A(dq_l[b][:, h * 64:(h + 1) * 64].T)
            m["ag_kT"] = A(np.concatenate([dk_c[b], dk_l[b]], 0)[:, kvh * 64:(kvh + 1) * 64].T)
            m["ag_v"] = A(np.concatenate([dv_c[b], dv_l[b]], 0)[:, kvh * 64:(kvh + 1) * 64])
            m["ag_sel"] = sel
            if not last:
                m["cm_qT"] = A(mq_c[b][:, h * 96:(h + 1) * 96].T)
                m["cm_kT"] = A(mk_c[b][:, h * 96:(h + 1) * 96].T)
                m["cm_v"] = A(mv_c[b][:, h * 64:(h + 1) * 64])
                m["cm_sel"] = sel
                m["cg_qT"] = A(dq_c[b][:, h * 64:(h + 1) * 64].T)
                m["cg_kT"] = A(dk_c[b][:, kvh * 64:(kvh + 1) * 64].T)
                m["cg_v"] = A(dv_c[b][:, kvh * 64:(kvh + 1) * 64])
                m["cg_sel"] = sel
            d_, half = (i % 4) // 2, i % 2
            fs = slice(half * 128, (half + 1) * 128)
            prm = np.zeros((128, 4, 3), f32)
            bri = np.zeros((128, 4, 2, 16), f32)
            cbl = np.zeros((128, 4, 2, 32), f32)
            for ct in range(4):
                for gl in range(2):
                    g = 8 * half + 2 * ct + gl
                    rs_ = slice(gl * 64, (gl + 1) * 64)
                    prm[rs_, ct, 0] = s5_lambda_re[l, d_, g]
                    prm[rs_, ct, 1] = s5_lambda_im[l, d_, g]
                    prm[rs_, ct, 2] = s5_log_dt[l, d_, g]
                    bri[rs_, ct, 0] = s5_b_re[l, d_, g]
                    bri[rs_, ct, 1] = s5_b_im[l, d_, g]
                    cbl[rs_, ct, 0, gl * 16:(gl + 1) * 16] = np.asarray(s5_c_re[l, d_, g]).T
                    cbl[rs_, ct, 1, gl * 16:(gl + 1) * 16] = np.asarray(s5_c_im[l, d_, g]).T
            m["s5_uT"] = A(scan_order(u_c[b][:, fs], u_l[b][:, fs], d_).T)
            m["s5_prm"], m["s5_bri"], m["s5_cblk"], m["s5_ident"] = prm, bri, cbl, np.eye(128, dtype=f32)
            fk = slice(d_ * 256 + half * 128, d_ * 256 + (half + 1) * 128)
            m["hg_qT"] = A(scan_order(hq_c[b][:, fs], hq_l[b][:, fs], d_).T)
            m["hg_kT"] = A(scan_order(hk_c[b][:, fk], hk_l[b][:, fk], d_).T)
            m["hg_lT"] = A(scan_order(hl_c[b][:, fk], hl_l[b][:, fk], d_).T)
            m["hg_v"] = A(scan_order(hv_c[b][:, fs], hv_l[b][:, fs], d_))
            m["hg_identb"], m["hg_mask"] = np.eye(64, dtype=f32), mask
            ims.append(m)
        rs = run(prog("mix", build_mix, not last), ims)
        mla_l = np.empty((2, SEQ, 256), f32)
        gqa_l = np.empty((2, SEQ, 256), f32)
        mla_c = np.zeros((2, CTX, 256), f32)
        gqa_c = np.zeros((2, CTX, 256), f32)
        s5_l = np.zeros((2, 2, SEQ, 256), f32)
        s5_c = np.zeros((2, 2, CTX, 256), f32)
        ho_l = np.zeros((2, 2, SEQ, 256), f32)
        ho_c = np.zeros((2, 2, CTX, 256), f32)
        for i, r in enumerate(rs):
            b, h = i // 4, i % 4
            mla_l[b, :, h * 64:(h + 1) * 64] = r["am_oT"].T
            gqa_l[b, :, h * 64:(h + 1) * 64] = r["ag_oT"].T
            if not last:
                mla_c[b, :, h * 64:(h + 1) * 64] = r["cm_oT"].T
                gqa_c[b, :, h * 64:(h + 1) * 64] = r["cg_oT"].T
            d_, half = (i % 4) // 2, i % 2
            fs = slice(half * 128, (half + 1) * 128)
            yc, yl = unscan(r["s5_yT"].T, d_)
            s5_l[d_, b][:, fs] = yl
            s5_c[d_, b][:, fs] = yc
            yc, yl = unscan(r["hg_oT"].T, d_)
            ho_l[d_, b][:, fs] = yl
            ho_c[d_, b][:, fs] = yc
        pcm = np.zeros((128, 4), f32)
        pcm[:, 0] = s5_d[l][:128]
        pcm[:, 1] = s5_d[l][128:]
        pcm[:, 2] = np.tile(hgrn_o_norm[l], 2)
        W = A(w_in[l])
        ims = []
        for i in range(NCORE):
            parts = [(u_l, u_c), (s5_l[0], s5_c[0]), (s5_l[1], s5_c[1]), (mla_l, mla_c), (ho_l[0], ho_c[0]),
                     (ho_l[1], ho_c[1]), (hg_l, hg_c), (gqa_l, gqa_c)]
            bin_ = np.stack([core_tokens(pl, pc_, i).T.astype(f32) for (pl, pc_) in parts], axis=0)
            m = dict(m_xT=X1[i], m_wg=A(W[:, 2400:]), m_wbr=A(np.asarray(w_branch[l]).reshape(1024, D)),
                     m_wout=A(w_out[l]), m_wglu=A(s5_w_glu[l]), m_gp=gp_for(norm_pre[l, 1], norm_post[l, 1]),
                     m_modc=modc_for(i, l, 1), m_pcols=pcm, m_blk=blk, m_bin=A(bin_))
            m.update(ffn_in("f2_", i, l, 2, 1))
            if not last:
                m.update(ffn_in("f0_", i, l + 1, 0, 0))
                m.update(proj_in("p_", i, l + 1))
            ims.append(m)
        if not last:
            pres = run(prog("A1", build_A1), ims)
        else:
            fin = run(prog("A2", build_A2), ims)
            out_lat, _ = gather_fm([r["f2_outT"] for r in fin], D)
    return out_lat.astype(np.float32)
```
